# Optimizing a Trainium2 kernel written in Bass

```python
import math
import jax, jax.numpy as jnp
from jax import lax
import numpy as np

D_MODEL = 1024
BATCH = 1
SEQ = 16384
DEPTH = 2

D_MIX = D_MODEL
GM_HEADS = 4
GM_WIDTH = D_MIX // 4
GM_HEAD_DIM = GM_WIDTH // GM_HEADS
CHUNK = 128
DA_HEADS = 4
DA_WIDTH = D_MIX // 2
DA_V_DIM = DA_WIDTH // DA_HEADS
DA_QK_DIM = DA_V_DIM // 2
Q_BLOCK = 128
CV_GROUPS = 4
CV_WIDTH = D_MIX // 4
CV_KERNEL = 31
D_FF = 2816
FFN_KERNEL = 3
N_BUCKETS = 32
MAX_DISTANCE = 128
EPS = 1e-6

GM_IN = 2 * GM_WIDTH
DA_Q_IN = DA_HEADS * 2 * DA_QK_DIM
DA_K_IN = DA_HEADS * 2 * DA_QK_DIM
DA_V_IN = DA_HEADS * DA_V_DIM
CV_IN = 2 * CV_WIDTH
IN_WIDTH = GM_IN + DA_Q_IN + DA_K_IN + DA_V_IN + CV_IN
SPLITS = tuple(np.cumsum([GM_IN, DA_Q_IN, DA_K_IN, DA_V_IN])[:].tolist())

kernel_name = "hybrid_gmlp_diffattn_conformer_block"


def rms_norm(x, g):
    xf = x.astype(jnp.float32)
    y = xf * lax.rsqrt(jnp.mean(xf * xf, axis=-1, keepdims=True) + EPS)
    return (y * g).astype(x.dtype)


def layer_norm(x, g, b):
    xf = x.astype(jnp.float32)
    mu = jnp.mean(xf, axis=-1, keepdims=True)
    var = jnp.mean(jnp.square(xf - mu), axis=-1, keepdims=True)
    return ((xf - mu) * lax.rsqrt(var + EPS) * g + b).astype(x.dtype)


def group_layer_norm(x, g, b, groups):
    B, S, C = x.shape
    xf = x.astype(jnp.float32).reshape(B, S, groups, C // groups)
    mu = jnp.mean(xf, axis=-1, keepdims=True)
    var = jnp.mean(jnp.square(xf - mu), axis=-1, keepdims=True)
    y = ((xf - mu) * lax.rsqrt(var + EPS)).reshape(B, S, C)
    return (y * g + b).astype(x.dtype)


def causal_depthwise_conv(x, w, b):
    K, C = w.shape
    y = lax.conv_general_dilated(
        x, w[:, None, :].astype(x.dtype), window_strides=(1,),
        padding=[(K - 1, 0)], dimension_numbers=("NWC", "WIO", "NWC"),
        feature_group_count=C)
    return y + b


def t5_causal_bucket(rel):
    n = jnp.maximum(rel, 0)
    max_exact = N_BUCKETS // 2
    nf = jnp.maximum(n, 1).astype(jnp.float32)
    large = max_exact + (jnp.log(nf / max_exact) / math.log(MAX_DISTANCE / max_exact)
                         * (N_BUCKETS - max_exact)).astype(jnp.int32)
    large = jnp.minimum(large, N_BUCKETS - 1)
    return jnp.where(n < max_exact, n, large)


def chunked_spatial_gating(u, v, w_s, b_s, ln_g, ln_b):
    B, S, _ = v.shape
    v = layer_norm(v, ln_g, ln_b)
    vc = v.reshape(B, S // CHUNK, CHUNK, GM_HEADS, GM_HEAD_DIM)
    causal = jnp.tril(jnp.ones((CHUNK, CHUNK), dtype=bool))
    w = jnp.where(causal[None], w_s, jnp.zeros_like(w_s))
    mixed = jnp.einsum("hts,bcshd->bcthd", w, vc) + jnp.transpose(b_s)[None, None, :, :, None]
    return u * mixed.reshape(B, S, GM_WIDTH)


def differential_attention(q1, q2, k1, k2, v, lam, rel_bias):
    B, S, H, _ = q1.shape
    n_blocks = S // Q_BLOCK
    scale = DA_QK_DIM ** -0.5
    k_pos = jnp.arange(S, dtype=jnp.int32)

    def block(i):
        start = i * Q_BLOCK
        qb1 = lax.dynamic_slice_in_dim(q1, start, Q_BLOCK, axis=1)
        qb2 = lax.dynamic_slice_in_dim(q2, start, Q_BLOCK, axis=1)
        q_pos = start + jnp.arange(Q_BLOCK, dtype=jnp.int32)
        rel = q_pos[:, None] - k_pos[None, :]
        bias = jnp.transpose(rel_bias[t5_causal_bucket(rel)], (2, 0, 1)).astype(jnp.float32)
        visible = rel >= 0
        s1 = jnp.einsum("bqhd,bkhd->bhqk", qb1, k1).astype(jnp.float32) * scale + bias
        s2 = jnp.einsum("bqhd,bkhd->bhqk", qb2, k2).astype(jnp.float32) * scale + bias
        p1 = jax.nn.softmax(jnp.where(visible, s1, -jnp.inf), axis=-1)
        p2 = jax.nn.softmax(jnp.where(visible, s2, -jnp.inf), axis=-1)
        a = (p1 - lam * p2).astype(v.dtype)
        return jnp.einsum("bhqk,bkhd->bqhd", a, v)

    out = lax.map(block, jnp.arange(n_blocks, dtype=jnp.int32))
    return jnp.transpose(out, (1, 0, 2, 3, 4)).reshape(B, S, H, DA_V_DIM)


def conformer_conv_module(a, g, dw_w, dw_b, ln_g, ln_b):
    h = a * jax.nn.sigmoid(g)
    h = causal_depthwise_conv(h, dw_w, dw_b)
    h = group_layer_norm(h, ln_g, ln_b, CV_GROUPS)
    return jax.nn.silu(h)


def setup_inputs(seed: int = 0) -> dict:
    key = jax.random.key(seed)
    ks = jax.random.split(key, 32)
    f32 = jnp.float32
    nrm = lambda k, shape, s: jax.random.normal(k, shape, f32) * s
    gain = lambda k, shape: 1.0 + 0.02 * jax.random.normal(k, shape, f32)
    return {
        "x": jax.random.normal(ks[0], (BATCH, SEQ, D_MODEL), f32),
        "w_in": nrm(ks[1], (DEPTH, D_MODEL, IN_WIDTH), D_MODEL ** -0.5),
        "w_out": nrm(ks[2], (DEPTH, D_MIX, D_MODEL), D_MIX ** -0.5),
        "gm_ln_g": gain(ks[3], (DEPTH, GM_WIDTH)),
        "gm_ln_b": nrm(ks[4], (DEPTH, GM_WIDTH), 0.02),
        "gm_w_s": nrm(ks[5], (DEPTH, GM_HEADS, CHUNK, CHUNK), CHUNK ** -0.5),
        "gm_b_s": gain(ks[6], (DEPTH, GM_HEADS, CHUNK)),
        "da_lq1": nrm(ks[7], (DEPTH, DA_QK_DIM), 0.1),
        "da_lk1": nrm(ks[8], (DEPTH, DA_QK_DIM), 0.1),
        "da_lq2": nrm(ks[9], (DEPTH, DA_QK_DIM), 0.1),
        "da_lk2": nrm(ks[10], (DEPTH, DA_QK_DIM), 0.1),
        "da_subln_g": gain(ks[11], (DEPTH, DA_V_DIM)),
        "rel_bias": nrm(ks[12], (N_BUCKETS, DA_HEADS), 0.5),
        "cv_dw_w": nrm(ks[13], (DEPTH, CV_KERNEL, CV_WIDTH), CV_KERNEL ** -0.5),
        "cv_dw_b": nrm(ks[14], (DEPTH, CV_WIDTH), 0.02),
        "cv_ln_g": gain(ks[15], (DEPTH, CV_WIDTH)),
        "cv_ln_b": nrm(ks[16], (DEPTH, CV_WIDTH), 0.02),
        "ffn_w_up": nrm(ks[17], (DEPTH, D_MODEL, 2 * D_FF), D_MODEL ** -0.5),
        "ffn_conv_w": nrm(ks[18], (DEPTH, FFN_KERNEL, 2 * D_FF), FFN_KERNEL ** -0.5),
        "ffn_conv_b": nrm(ks[19], (DEPTH, 2 * D_FF), 0.02),
        "ffn_w_down": nrm(ks[20], (DEPTH, D_FF, D_MODEL), D_FF ** -0.5),
        "pre_mix_g": gain(ks[21], (DEPTH, D_MODEL)),
        "post_mix_g": gain(ks[22], (DEPTH, D_MODEL)),
        "pre_ffn_g": gain(ks[23], (DEPTH, D_MODEL)),
        "post_ffn_g": gain(ks[24], (DEPTH, D_MODEL)),
    }


def reference(x, w_in, w_out, gm_ln_g, gm_ln_b, gm_w_s, gm_b_s,
              da_lq1, da_lk1, da_lq2, da_lk2, da_subln_g, rel_bias,
              cv_dw_w, cv_dw_b, cv_ln_g, cv_ln_b,
              ffn_w_up, ffn_conv_w, ffn_conv_b, ffn_w_down,
              pre_mix_g, post_mix_g, pre_ffn_g, post_ffn_g):
    B, S, _ = x.shape
    for l in range(DEPTH):
        h = rms_norm(x, pre_mix_g[l])
        p = h @ w_in[l]
        gm_p, q, k, v, cv_p = jnp.split(p, SPLITS, axis=-1)

        gm_u, gm_v = jnp.split(jax.nn.gelu(gm_p), 2, axis=-1)
        out_a = chunked_spatial_gating(gm_u, gm_v, gm_w_s[l], gm_b_s[l], gm_ln_g[l], gm_ln_b[l])

        q = q.reshape(B, S, DA_HEADS, 2, DA_QK_DIM)
        k = k.reshape(B, S, DA_HEADS, 2, DA_QK_DIM)
        v = v.reshape(B, S, DA_HEADS, DA_V_DIM)
        lambda_init = 0.8 - 0.6 * math.exp(-0.3 * l)
        lam = (jnp.exp(jnp.sum(da_lq1[l].astype(jnp.float32) * da_lk1[l].astype(jnp.float32)))
               - jnp.exp(jnp.sum(da_lq2[l].astype(jnp.float32) * da_lk2[l].astype(jnp.float32)))
               + lambda_init)
        attn = differential_attention(q[..., 0, :], q[..., 1, :], k[..., 0, :], k[..., 1, :], v, lam, rel_bias)
        attn = rms_norm(attn, da_subln_g[l]) * (1.0 - lambda_init)
        out_b = attn.reshape(B, S, DA_WIDTH)

        cv_a, cv_g = jnp.split(cv_p, 2, axis=-1)
        out_c = conformer_conv_module(cv_a, cv_g, cv_dw_w[l], cv_dw_b[l], cv_ln_g[l], cv_ln_b[l])

        mix = jnp.concatenate([out_a, out_b, out_c], axis=-1) @ w_out[l]
        x = x + rms_norm(mix, post_mix_g[l])

        h = rms_norm(x, pre_ffn_g[l])
        up = causal_depthwise_conv(h @ ffn_w_up[l], ffn_conv_w[l], ffn_conv_b[l])
        gate, val = jnp.split(up, 2, axis=-1)
        y = (jax.nn.gelu(gate) * val) @ ffn_w_down[l]
        x = x + rms_norm(y, post_ffn_g[l])
    return x
```

```python
import math
from contextlib import ExitStack
import numpy as np
import concourse.bass as bass
import concourse.mybir as mybir
from concourse.bass_utils import run_bass_kernel_spmd

F32, BF16 = mybir.dt.float32, mybir.dt.bfloat16
AF = mybir.ActivationFunctionType
ALU = mybir.AluOpType
NCORES = 8
D = 1024
INW = 2560
DFF = 2816
NFC = 44
EPS = 1e-6
NEG = -30000.0
DEPTH = 2
ENG = ("pe", "act", "dve", "pool", "sp")


class Tracker:
    def __init__(self, nc, stack):
        self.nc, self.stack = nc, stack
        self.ops = {e: [] for e in ENG}
        self.sem, self.cnt = {}, {}
        self.waited = {e: {} for e in ENG}
        self.lastw, self.readers = {}, {}
        for e in ENG:
            self._mk("E" + e)

    def _mk(self, name):
        if name not in self.sem:
            self.sem[name] = self.stack.enter_context(self.nc.semaphore(name))
            self.cnt[name] = 0
        return name

    def _deps(self, reads, writes):
        deps = {}
        def add(tok):
            if tok is not None:
                deps[tok[0]] = max(deps.get(tok[0], 0), tok[1])
        for b in reads:
            add(self.lastw.get(b))
        for b in writes:
            add(self.lastw.get(b))
            for s, v in self.readers.get(b, {}).items():
                add((s, v))
        return deps

    def _waits(self, eng, deps):
        w = []
        for s, v in deps.items():
            if eng == "pe" and s == "Epe":
                continue
            if self.waited[eng].get(s, 0) < v:
                self.waited[eng][s] = v
                w.append((self.sem[s], v))
        return w

    def _commit(self, tok, reads, writes):
        for b in reads:
            self.readers.setdefault(b, {})[tok[0]] = tok[1]
        for b in writes:
            self.lastw[b] = tok
            self.readers[b] = {}

    def op(self, eng, fn, reads=(), writes=()):
        w = self._waits(eng, self._deps(reads, writes))
        s = "E" + eng
        self.cnt[s] += 1
        tok = (s, self.cnt[s])
        sem = self.sem[s]
        def emit(e, fn=fn, w=w, sem=sem):
            for sh, v in w:
                e.wait_ge(sh, v)
            fn(e).then_inc(sem, 1)
        self.ops[eng].append(emit)
        self._commit(tok, reads, writes)

    def dma(self, eng, fn, reads=(), writes=(), slot=None, inc=16):
        w = self._waits(eng, self._deps(reads, writes))
        s = self._mk("D" + slot)
        self.cnt[s] += inc
        tok = (s, self.cnt[s])
        sem = self.sem[s]
        def emit(e, fn=fn, w=w, sem=sem, inc=inc):
            for sh, v in w:
                e.wait_ge(sh, v)
            fn(e).then_inc(sem, inc)
        self.ops[eng].append(emit)
        self._commit(tok, reads, writes)

    def barrier(self):
        allv = {s: v for s, v in self.cnt.items() if v > 0}
        for eng in ENG:
            w = self._waits(eng, dict(allv))
            def emit(e, w=w):
                for sh, v in w:
                    e.wait_ge(sh, v)
            self.ops[eng].append(emit)

    def run(self, block):
        for name, meth in (("pe", block.tensor), ("act", block.scalar), ("dve", block.vector),
                           ("pool", block.gpsimd), ("sp", block.sync)):
            ops = self.ops[name]
            def body(e, ops=ops):
                for o in ops:
                    o(e)
            meth(body)


def t5_bucket(n):
    n = np.maximum(n, 0)
    nf = np.maximum(n, 1).astype(np.float32)
    large = 16 + (np.log(nf / np.float32(16)) / np.float32(math.log(128 / 16)) * np.float32(16)).astype(np.int32)
    large = np.minimum(large, 31)
    return np.where(n < 16, n, large)


def build(NT=16, stop_after=None):
    S_LOC = NT * 128
    NBLK = NT * 8
    nc = bass.Bass("TRN2", target_bir_lowering=False)
    dt_in = lambda name, shape, dt=F32: nc.dram_tensor(name, list(shape), dt, kind="ExternalInput").ap()
    x_in = dt_in("x", [NT, 128, D])
    w_in = dt_in("w_in", [DEPTH, D, INW])
    w_out = dt_in("w_out", [DEPTH, D, D])
    w_up = dt_in("ffn_w_up", [DEPTH, D, 2 * DFF])
    w_down = dt_in("ffn_w_down", [DEPTH, DFF, D])
    gvec = dt_in("gvec", [DEPTH, 128, 4, D])
    gmv = dt_in("gmv", [DEPTH, 128, 2, 256])
    gm_wT = dt_in("gm_wT", [DEPTH, 128, 4, 128])
    gm_bs = dt_in("gm_bs", [DEPTH, 128, 256])
    tri_in = dt_in("tri", [128, 128])
    lam_in = dt_in("lam_in", [DEPTH, 128, 4, 64])
    subln_in = dt_in("subln", [DEPTH, 128, 128])
    btab_in = dt_in("btab", [4, 128, 10, 128])
    cfar_in = dt_in("cfar", [128, 4])
    cvw_in = dt_in("cvw", [DEPTH, 128, 2, 31])
    cvp_in = dt_in("cvp", [DEPTH, 128, 2, 3])
    gavg_in = dt_in("gavg", [128, 128])
    fcw_in = dt_in("fcw", [DEPTH, 128, NFC, 4])
    sel_in = dt_in("sel", [128, 8])
    ident_in = dt_in("ident", [128, 128])
    y_out = nc.dram_tensor("y", [NT, 128, D], F32, kind="ExternalOutput").ap()
    dbg = {}
    if stop_after is not None:
        dbg["q"] = nc.dram_tensor("dbg_q", [128, 4, S_LOC], BF16, kind="ExternalOutput").ap()
        dbg["ct"] = nc.dram_tensor("dbg_ct", [128, 8, S_LOC], BF16, kind="ExternalOutput").ap()
        dbg["gl"] = nc.dram_tensor("dbg_gl", [128, 2, S_LOC], F32, kind="ExternalOutput").ap()
        dbg["kg"] = nc.dram_tensor("dbg_kg", [NCORES * 512, S_LOC], BF16, kind="ExternalOutput").ap()
        dbg["vg"] = nc.dram_tensor("dbg_vg", [NCORES * S_LOC, 512], BF16, kind="ExternalOutput").ap()
        dbg["h2"] = nc.dram_tensor("dbg_h2", [128, 8, S_LOC], BF16, kind="ExternalOutput").ap()

    xbuf = [nc.dram_tensor(f"xbuf{i}", [NT, 128, D], F32).ap() for i in range(2)]
    Kc = nc.dram_tensor("Kc", [512, S_LOC], BF16).ap()
    Vc = nc.dram_tensor("Vc", [S_LOC, 512], BF16).ap()
    Hc = nc.dram_tensor("Hc", [128, 2 * NT * 32], F32).ap()
    Tc = nc.dram_tensor("Tc", [128, 8 * NT * 2], BF16).ap()
    Kg = nc.dram_tensor("Kg", [NCORES * 512, S_LOC], BF16, addr_space="Shared").ap()
    Vg = nc.dram_tensor("Vg", [NCORES * S_LOC, 512], BF16, addr_space="Shared").ap()
    Hg = nc.dram_tensor("Hg", [NCORES * 128, 2 * NT * 32], F32, addr_space="Shared").ap()
    Tg = nc.dram_tensor("Tg", [NCORES * 128, 8 * NT * 2], BF16, addr_space="Shared").ap()

    with ExitStack() as st:
        sb = lambda name, shape, dt=F32: st.enter_context(nc.sbuf_tensor("s_" + name, list(shape), dt))
        ps = [st.enter_context(nc.psum_tensor(f"ps{i}", [128, 512], F32)) for i in range(7)]
        psT = st.enter_context(nc.psum_tensor("psT", [128, 1024], BF16))
        ARENA = sb("ARENA", [128, 63488], BF16)
        ident = sb("ident", [128, 128], BF16)
        identf = sb("identf", [128, 128], F32)
        gv = sb("gv", [128, 2, D], F32)
        wT = sb("wT", [128, 4, 128], BF16)
        wTf = sb("wTf", [128, 4, 128], F32)
        tri = sb("tri", [128, 128], F32)
        gmvt = sb("gmvt", [128, 2, 256], F32)
        gbs = sb("gbs", [128, 256], F32)
        lamt = sb("lamt", [128, 4, 64], F32)
        lamw = sb("lamw", [128, 8], F32)
        gsub = sb("gsub", [128, 128], F32)
        cfar = sb("cfar", [128, 4], F32)
        btab = sb("btab", [128, 10, 128], F32)
        cvw = sb("cvw", [128, 2, 31], F32)
        cvp = sb("cvp", [128, 2, 3], F32)
        gavg = sb("gavg", [128, 128], F32)
        fcw = sb("fcw", [128, NFC, 4], F32)
        sel = sb("sel", [128, 8], F32)
        stt = sb("stt", [128, 64], F32)
        bnst = sb("bnst", [128, 2, 8], F32)
        xt = [sb(f"xt{i}", [128, D], F32) for i in range(2)]
        junk = sb("junk", [128, D], BF16)
        hb = [sb(f"hb{i}", [128, D], BF16) for i in range(2)]
        w1 = [sb(f"w1_{i}", [128, D], F32) for i in range(2)]
        w2all = sb("w2all", [128, 3, 512], F32)
        w2 = [w2all[:, i, :] for i in range(3)]
        wb = [sb(f"wb_{i}", [128, 512], BF16) for i in range(3)]
        candT = sb("candT", [128, 8, 8 * NT * 2], BF16)
        halT = sb("halT", [128, 8, NT, 2], BF16)
        halTf = sb("halTf", [128, 8 * NT * 2], F32)
        tailT = sb("tailT", [128, 8, NT, 2], BF16)
        upsb = [sb(f"upsb{i}", [128, 8, 130], F32) for i in range(2)]
        acc1 = sb("acc1", [128, 8, 128], F32)
        acc = [w2all[:, 0:2, :].rearrange("p a (t n) -> p (a t) n", n=128), acc1[:]]
        gact = btab[:, 0:8, :]
        cvq = w2[2]

        block = st.enter_context(nc.Block())
        T = Tracker(nc, st)

        WIN = ARENA[:, 0:8 * INW].rearrange("p (k n) -> p k n", k=8)
        HT = [ARENA[:, 20480 + i * 4096: 20480 + (i + 1) * 4096].rearrange("p (k n) -> p k n", k=8) for i in range(2)]
        GL = ARENA[:, 28672:38912].bitcast(F32).rearrange("p (i t n) -> p i t n", i=2, n=160)[:, :, 0:NT, :]
        CT = ARENA[:, 38912:55296].rearrange("p (k n) -> p k n", k=8)[:, :, 0:S_LOC]
        QT = ARENA[:, 55296:63488].rearrange("p (k n) -> p k n", k=4)[:, :, 0:S_LOC]
        cand = ARENA[:, 0:16384].bitcast(F32).rearrange("p (r n) -> p r n", r=8)[:, :, 0:2 * NT * 32]
        cvy = ARENA[:, 16384:24576].bitcast(F32).rearrange("p (i n) -> p i n", i=2)[:, :, 0:S_LOC]
        KT = ARENA[:, 0:NBLK * 128].rearrange("p (j n) -> p j n", n=128)
        VH = ARENA[:, 16384:16384 + NBLK * 130].rearrange("p (j n) -> p j n", n=130)
        QP = ARENA[:, 33280:37376].rearrange("p (a n) -> p a n", a=2)[:, :, 0:S_LOC]
        WOUT = ARENA[:, 16384:24576].rearrange("p (k n) -> p k n", k=8)
        H2T = ARENA[:, 0:16384].rearrange("p (k n) -> p k n", k=8)[:, :, 0:S_LOC]
        GT = ARENA[:, 16384:38912].rearrange("p (j n) -> p j n", j=22)
        WDN = ARENA[:, 38912:61440].rearrange("p (j n) -> p j n", j=22)

        ARW = [f"arenaW{k}" for k in range(8)]
        KTN = [f"KT{c}" for c in range(8)]
        VHN = [f"VH{c}" for c in range(8)]
        CANDN = ["cand00", "cand01"] + [f"cand{r}" for r in range(1, 8)]
        stat_i = [0]
        def newstat():
            stat_i[0] = (stat_i[0] + 1) % 64
            i = stat_i[0]
            return stt[:, i:i + 1], f"st{i}"

        def rstd_from(src_ap, srcname, n, eng_sq="act"):
            ss, ssn = newstat()
            T.op("act", lambda e: e.activation(out=junk[:, 0:n], in_=src_ap, func=AF.Square, accum_out=ss),
                 reads=[srcname], writes=["junk", ssn])
            return rstd_of(ss, ssn, 1.0 / n)

        def rstd_of(v, vn, scale):
            lnv, lnn = newstat()
            T.op("act", lambda e: e.activation(out=lnv, in_=v, func=AF.Ln, bias=epsT[:, 0:1], scale=scale),
                 reads=[vn], writes=[lnn])
            r, rn = newstat()
            T.op("act", lambda e: e.activation(out=r, in_=lnv, func=AF.Exp, scale=-0.5), reads=[lnn], writes=[rn])
            return r, rn

        epsT = sb("epsT", [128, 1], F32)
        T.op("dve", lambda e: e.memset(epsT[:], EPS), writes=["epsT"])

        def ld(dst, src, name, eng="sp"):
            T.dma(eng, lambda e: e.dma_start(out=dst, in_=src), writes=[name], slot="c_" + name)

        ld(identf[:], ident_in, "identf")
        T.op("dve", lambda e: e.tensor_copy(out=ident[:], in_=identf[:]), reads=["identf"], writes=["ident"])
        ld(tri[:], tri_in, "tri")
        ld(cfar[:], cfar_in, "cfar")
        ld(gavg[:], gavg_in, "gavg")
        ld(sel[:], sel_in, "sel")

        def transpose_to(dst_ap, dstname, src_tile, srcname, nchunk, psname="psT"):
            def f(e):
                r = None
                for k in range(nchunk):
                    r = e.transpose(out=psT[:, k * 128:(k + 1) * 128], in_=src_tile[:, k * 128:(k + 1) * 128], identity=ident[:])
                return r
            T.op("pe", f, reads=[srcname, "ident"], writes=[psname])
            T.op("act", lambda e: e.activation(out=dst_ap, in_=psT[:, 0:nchunk * 128].rearrange("p (k n) -> p k n", k=nchunk), func=AF.Copy),
                 reads=[psname], writes=[dstname])

        def layer_consts(l):
            ld(wTf[:], gm_wT[l], "wTf")
            for h in range(4):
                T.op("dve", lambda e, h=h: e.tensor_tensor(out=wT[:, h, :], in0=wTf[:, h, :], in1=tri[:], op=ALU.mult),
                     reads=["wTf", "tri"], writes=["wT"])
            ld(gmvt[:], gmv[l], "gmvt")
            ld(gbs[:], gm_bs[l], "gbs")
            ld(lamt[:], lam_in[l], "lamt")
            ld(gsub[:], subln_in[l], "gsub")
            ld(cvw[:], cvw_in[l], "cvw")
            ld(cvp[:], cvp_in[l], "cvp")
            ld(fcw[:], fcw_in[l], "fcw")
            lam_init = 0.8 - 0.6 * math.exp(-0.3 * l)
            for i in range(2):
                T.op("dve", lambda e, i=i: e.tensor_tensor(out=junk[:, i * 64:(i + 1) * 64], in0=lamt[:, 2 * i, :], in1=lamt[:, 2 * i + 1, :], op=ALU.mult),
                     reads=["lamt"], writes=["junk"])
                T.op("dve", lambda e, i=i: e.reduce_sum(out=lamw[:, i:i + 1], in_=junk[:, i * 64:(i + 1) * 64], axis=mybir.AxisListType.X),
                     reads=["junk"], writes=["lamw"])
            T.op("act", lambda e: e.activation(out=lamw[:, 4:6], in_=lamw[:, 0:2], func=AF.Exp), reads=["lamw"], writes=["lamw"])
            T.op("dve", lambda e: e.tensor_tensor(out=lamw[:, 6:7], in0=lamw[:, 5:6], in1=lamw[:, 4:5], op=ALU.subtract),
                 reads=["lamw"], writes=["lamw"])
            T.op("dve", lambda e: e.tensor_scalar(out=lamw[:, 2:3], in0=lamw[:, 6:7], scalar1=-lam_init, scalar2=None, op0=ALU.add),
                 reads=["lamw"], writes=["lamw"])
            T.op("dve", lambda e: e.tensor_scalar(out=gsub[:], in0=gsub[:], scalar1=1.0 - lam_init, scalar2=None, op0=ALU.mult),
                 reads=["gsub"], writes=["gsub"])

        def phase_A(l, xsrc):
            ld(gv[:, 0, :], gvec[l][:, 0, :], "gv0")
            for kc in range(8):
                T.dma("pool", lambda e, kc=kc: e.dma_start(
                    out=WIN[:, kc, :].rearrange("p (a b) -> p a b", b=512),
                    in_=w_in[l, kc * 128:(kc + 1) * 128, :].rearrange("p (a b) -> p a b", b=512)),
                    writes=[f"arenaW{kc}"], slot=f"win{kc}")
            for g in range(NT // 4):
                hT = HT[g % 2]
                hTn = f"hT{g % 2}"
                for t in range(4):
                    m = 4 * g + t
                    b = m % 2
                    T.dma("sp", lambda e, m=m, b=b: e.dma_start(out=xt[b][:], in_=xsrc[m]), writes=[f"xt{b}"], slot=f"xt{b}")
                    r, rn = rstd_from(xt[b][:], f"xt{b}", D)
                    T.op("dve", lambda e, b=b, r=r: e.scalar_tensor_tensor(out=hb[b][:], in0=xt[b][:], scalar=r, in1=gv[:, 0, :], op0=ALU.mult, op1=ALU.mult),
                         reads=[f"xt{b}", rn, "gv0"], writes=[f"hb{b}"])
                    transpose_to(hT[:, :, t * 128:(t + 1) * 128], hTn, hb[b], f"hb{b}", 8)
                order = [("q", h, 512 + 128 * h) for h in range(4)] + [("k", h, 1024 + 128 * h) for h in range(4)] + \
                        [("cg", i, 2304 + 128 * i) for i in range(2)] + [("ca", i, 2048 + 128 * i) for i in range(2)]
                for oi, (kind, idx, col) in enumerate(order):
                    pb = oi % 2
                    def f(e, col=col, pb=pb, hT=hT):
                        r = None
                        for kc in range(8):
                            r = e.matmul(out=ps[pb][:], lhsT=WIN[:, kc, col:col + 128], rhs=hT[:, kc, :], start=(kc == 0), stop=(kc == 7))
                        return r
                    T.op("pe", f, reads=[hTn] + ARW, writes=[f"ps{pb}"])
                    tok = slice(g * 512, (g + 1) * 512)
                    if kind == "q":
                        T.op("act", lambda e, pb=pb, idx=idx, tok=tok: e.activation(out=QT[:, idx, tok], in_=ps[pb][:], func=AF.Copy),
                             reads=[f"ps{pb}"], writes=["QT"])
                    elif kind == "k":
                        wi = idx % 3
                        T.op("dve", lambda e, pb=pb, wi=wi: e.tensor_copy(out=wb[wi][:], in_=ps[pb][:]), reads=[f"ps{pb}"], writes=[f"wb{wi}"])
                        T.dma("sp", lambda e, wi=wi, idx=idx, tok=tok: e.dma_start(out=Kc[idx * 128:(idx + 1) * 128, tok], in_=wb[wi][:]),
                              reads=[f"wb{wi}"], writes=[f"Kc{g}_{idx}"], slot=f"wb{wi}")
                    elif kind == "cg":
                        T.op("act", lambda e, pb=pb, idx=idx: e.activation(out=w2[idx][:], in_=ps[pb][:], func=AF.Sigmoid),
                             reads=[f"ps{pb}"], writes=[f"w2{idx}"])
                    else:
                        T.op("dve", lambda e, pb=pb, idx=idx, g=g: e.tensor_tensor(
                            out=GL[:, idx, 4 * g:4 * g + 4, 32:160], in0=ps[pb][:].rearrange("p (t n) -> p t n", t=4),
                            in1=w2[idx][:].rearrange("p (t n) -> p t n", t=4), op=ALU.mult),
                            reads=[f"ps{pb}", f"w2{idx}"], writes=["GL"])
                for t in range(4):
                    m = 4 * g + t
                    tcols = slice(t * 128, (t + 1) * 128)
                    def fg(e, tcols=tcols, hT=hT):
                        r = None
                        for kc in range(8):
                            r = e.matmul(out=ps[2][:], lhsT=hT[:, kc, tcols], rhs=WIN[:, kc, 0:512], start=(kc == 0), stop=(kc == 7))
                        return r
                    T.op("pe", fg, reads=[hTn] + ARW, writes=["ps2"])
                    def fv(e, tcols=tcols, hT=hT):
                        r = None
                        for kc in range(8):
                            r = e.matmul(out=ps[3][:], lhsT=hT[:, kc, tcols], rhs=WIN[:, kc, 1536:2048], start=(kc == 0), stop=(kc == 7))
                        return r
                    T.op("pe", fv, reads=[hTn] + ARW, writes=["ps3"])
                    T.op("dve", lambda e: e.tensor_copy(out=wb[2][:], in_=ps[3][:]), reads=["ps3"], writes=["wb2"])
                    T.dma("sp", lambda e, m=m: e.dma_start(out=Vc[m * 128:(m + 1) * 128, :], in_=wb[2][:]),
                          reads=["wb2"], writes=[f"Vc{m}"], slot="wb2")
                    ug = w1[0]
                    T.op("act", lambda e: e.activation(out=ug[:, 0:512], in_=ps[2][:], func=AF.Gelu_apprx_tanh), reads=["ps2"], writes=["w1_0"])
                    T.op("dve", lambda e: e.bn_stats(out=bnst[:, 0, 0:6], in_=ug[:, 256:512]), reads=["w1_0"], writes=["bnst0"])
                    T.op("dve", lambda e: e.bn_aggr(out=bnst[:, 1, 0:2], in_=bnst[:, 0, 0:6]), reads=["bnst0"], writes=["bnst1"])
                    r, rn = rstd_of(bnst[:, 1, 1:2], "bnst1", 1.0)
                    vn = w1[0][:, 512:768]
                    T.op("dve", lambda e, r=r: e.tensor_scalar(out=vn, in0=ug[:, 256:512], scalar1=bnst[:, 1, 0:1], scalar2=r, op0=ALU.subtract, op1=ALU.mult),
                         reads=["w1_0", "bnst1", rn], writes=["w1_0b"])
                    T.op("dve", lambda e: e.tensor_tensor(out=vn, in0=vn, in1=gmvt[:, 0, :], op=ALU.mult), reads=["w1_0b", "gmvt"], writes=["w1_0b"])
                    T.op("dve", lambda e: e.tensor_tensor(out=hb[0][:, 0:256], in0=vn, in1=gmvt[:, 1, :], op=ALU.add), reads=["w1_0b", "gmvt"], writes=["hb0"])
                    def fs(e):
                        r = None
                        for h in range(4):
                            r = e.matmul(out=ps[4][:, h * 64:(h + 1) * 64], lhsT=wT[:, h, :], rhs=hb[0][:, h * 64:(h + 1) * 64], start=True, stop=True)
                        return r
                    T.op("pe", fs, reads=["hb0", "wT"], writes=["ps4"])
                    T.op("dve", lambda e: e.tensor_tensor(out=w1[0][:, 768:1024], in0=ps[4][:, 0:256], in1=gbs[:], op=ALU.add),
                         reads=["ps4", "gbs"], writes=["w1_0c"])
                    T.op("dve", lambda e: e.tensor_tensor(out=hb[1][:, 0:256], in0=w1[0][:, 768:1024], in1=ug[:, 0:256], op=ALU.mult),
                         reads=["w1_0c", "w1_0"], writes=["hb1"])
                    transpose_to(CT[:, 0:2, m * 128:(m + 1) * 128], "CT", hb[1], "hb1", 2)
            for i in range(2):
                T.dma("sp", lambda e, i=i: e.dma_start(out=Hc.rearrange("p (i t n) -> p i t n", i=2, t=NT)[:, i], in_=GL[:, i, :, 128:160]),
                      reads=["GL"], writes=[f"Hc{i}"], slot=f"hc{i}")

        def gather(src, dst, reads, name):
            T.dma("pool", lambda e: e.collective_compute("AllGather", ALU.bypass, replica_groups=[list(range(NCORES))], ins=[src], outs=[dst]),
                  reads=reads, writes=[name], slot="cc_" + name, inc=1)

        def halo_select(dst_f32, dstname, cand_t, candname, width):
            T.op("dve", lambda e: e.tensor_scalar(out=dst_f32, in0=cand_t[:, 0, :], scalar1=sel[:, 0:1], scalar2=None, op0=ALU.mult),
                 reads=candname + ["sel"], writes=[dstname])
            for r_ in range(1, 8):
                T.op("dve", lambda e, r_=r_: e.scalar_tensor_tensor(out=dst_f32, in0=cand_t[:, r_, :], scalar=sel[:, r_:r_ + 1], in1=dst_f32, op0=ALU.mult, op1=ALU.add),
                     reads=candname + ["sel", dstname], writes=[dstname])

        def phase_B(l, xsrc, xdst, upto=3):
            kg_names = [f"Kg"]
            T.op("pool", lambda e: e.memset(cand[:, 0, :], 0.0), writes=["cand00", "cand01"])
            cview = cand[:].rearrange("p r (i t n) -> p r i t n", i=2, t=NT)
            hgv = Hg.rearrange("(r p) (i t n) -> r p i t n", p=128, i=2, t=NT)
            if NT > 1:
                for i in range(2):
                    T.dma("sp", lambda e, i=i: e.dma_start(out=cview[:, 0, i, 1:NT, :], in_=hgv[7, :, i, 0:NT - 1, :]), reads=["Hg"], writes=[f"cand0{i}"], slot=f"cand0{i}")
            for r_ in range(1, 8):
                T.dma("sp", lambda e, r_=r_: e.dma_start(out=cand[:, r_, :], in_=Hg[(r_ - 1) * 128:r_ * 128, :]), reads=["Hg"], writes=[f"cand{r_}"], slot=f"cand{r_}")
            hsel = cvy[:, 0, 0:2 * NT * 32]
            halo_select(hsel, "cvy", cand, CANDN, 2 * NT * 32)
            T.op("dve", lambda e: e.tensor_copy(out=GL[:, :, :, 0:32], in_=hsel.rearrange("p (i t n) -> p i t n", i=2, t=NT)),
                 reads=["cvy"], writes=["GL"])
            for i in range(2):
                ceng = "dve"
                T.op(ceng, lambda e, i=i: e.tensor_scalar(out=cvy[:, i, :].rearrange("p (t n) -> p t n", t=NT), in0=GL[:, i, :, 2:130],
                                                           scalar1=cvw[:, i, 0:1], scalar2=cvp[:, i, 0:1], op0=ALU.mult, op1=ALU.add),
                     reads=["GL", "cvw", "cvp"], writes=[f"cvy{i}"])
                for k in range(1, 31):
                    T.op(ceng, lambda e, i=i, k=k: e.scalar_tensor_tensor(
                        out=cvy[:, i, :].rearrange("p (t n) -> p t n", t=NT), in0=GL[:, i, :, 2 + k:130 + k], scalar=cvw[:, i, k:k + 1],
                        in1=cvy[:, i, :].rearrange("p (t n) -> p t n", t=NT), op0=ALU.mult, op1=ALU.add),
                        reads=["GL", "cvw", f"cvy{i}"], writes=[f"cvy{i}"])
            for i in range(2):
                for q4 in range(S_LOC // 512):
                    cs = slice(q4 * 512, (q4 + 1) * 512)
                    T.op("pe", lambda e, i=i, cs=cs: e.matmul(out=ps[0][:], lhsT=gavg[:], rhs=cvy[:, i, cs], start=True, stop=True),
                         reads=[f"cvy{i}", "gavg"], writes=["ps0"])
                    T.op("act", lambda e, i=i, cs=cs: e.activation(out=cvq[:], in_=cvy[:, i, cs], func=AF.Square), reads=[f"cvy{i}"], writes=["cvq"])
                    T.op("pe", lambda e: e.matmul(out=ps[1][:], lhsT=gavg[:], rhs=cvq[:], start=True, stop=True), reads=["cvq", "gavg"], writes=["ps1"])
                    T.op("act", lambda e: e.activation(out=w2[0][:], in_=ps[0][:], func=AF.Square), reads=["ps0"], writes=["w20"])
                    T.op("dve", lambda e: e.tensor_tensor(out=w2[0][:], in0=ps[1][:], in1=w2[0][:], op=ALU.subtract), reads=["ps1", "w20"], writes=["w20"])
                    T.op("act", lambda e: e.activation(out=w2[0][:], in_=w2[0][:], func=AF.Ln, bias=epsT[:, 0:1]), reads=["w20"], writes=["w20"])
                    T.op("act", lambda e: e.activation(out=w2[0][:], in_=w2[0][:], func=AF.Exp, scale=-0.5), reads=["w20"], writes=["w20"])
                    T.op("dve", lambda e, i=i, cs=cs: e.tensor_tensor(out=w2[1][:], in0=cvy[:, i, cs], in1=ps[0][:], op=ALU.subtract), reads=[f"cvy{i}", "ps0"], writes=["w21"])
                    T.op("dve", lambda e: e.tensor_tensor(out=w2[1][:], in0=w2[1][:], in1=w2[0][:], op=ALU.mult), reads=["w21", "w20"], writes=["w21"])
                    T.op("act", lambda e, i=i, cs=cs: e.activation(out=CT[:, 6 + i, cs], in_=w2[1][:], func=AF.Silu, bias=cvp[:, i, 2:3], scale=cvp[:, i, 1:2]),
                         reads=["w21", "cvp"], writes=["CT"])
            T.barrier()
            if upto == 1:
                return
            T.op("pool", lambda e: e.memset(QP[:, :, :], 0.0), writes=["QP"])
            for h in range(4):
                for mp in range(2):
                    T.op("pool", lambda e, mp=mp, h=h: e.tensor_copy(out=QP[mp * 64:(mp + 1) * 64, mp, :], in_=QT[mp * 64:(mp + 1) * 64, h, :]),
                         reads=["QT", "QP"], writes=["QP"])
                for c_ in range(8):
                    T.dma("sp", lambda e, c_=c_, h=h: e.dma_start(
                        out=KT[:, :, :].rearrange("p (m c) n -> p m c n", c=8)[:, :, c_, :],
                        in_=Kg[c_ * 512 + h * 128:c_ * 512 + (h + 1) * 128, :].rearrange("p (m n) -> p m n", n=128)),
                        reads=["Kg"], writes=[f"KT{c_}"], slot=f"kt{c_}")
                    T.dma("sp", lambda e, c_=c_, h=h: e.dma_start(
                        out=VH[:, :, 0:128].rearrange("p (m c) n -> p m c n", c=8)[:, :, c_, :],
                        in_=Vg[c_ * S_LOC:(c_ + 1) * S_LOC, h * 128:(h + 1) * 128].rearrange("(m p) n -> p m n", p=128)),
                        reads=["Vg"], writes=[f"VH{c_}"], slot=f"vh{c_}")
                T.op("pool", lambda e: e.memset(VH[:, :, 128:130], 1.0), writes=["VH1"])
                T.dma("sp", lambda e, h=h: e.dma_start(out=btab[:], in_=btab_in[h]), writes=["btab"], slot="btab")
                for m in range(NT):
                    npairs = 4 * m + 4
                    qs = slice(m * 128, (m + 1) * 128)
                    for pp in range(npairs):
                        sbk = pp % 3
                        S = ps[sbk]
                        def fsc(e, pp=pp, S=S, qs=qs, h=h):
                            r = None
                            for kb in range(2):
                                for mp in range(2):
                                    r = e.matmul(out=S[:, kb * 256 + mp * 128: kb * 256 + (mp + 1) * 128],
                                                 lhsT=KT[:, 2 * pp + kb, :], rhs=QP[:, mp, qs], start=True, stop=True)
                            return r
                        T.op("pe", fsc, reads=KTN + ["QP"], writes=[f"ps{sbk}"])
                        P = wb[sbk]
                        near = (2 * pp >= 8 * m - 2)
                        if not near:
                            T.op("act", lambda e, S=S, P=P, h=h: e.activation(out=P[:], in_=S[:], func=AF.Exp, bias=cfar[:, h:h + 1], scale=0.125),
                                 reads=[f"ps{sbk}", "cfar"], writes=[f"wb{sbk}"])
                        else:
                            n0 = 2 * pp - (8 * m - 2)
                            for mp in range(2):
                                T.op("dve", lambda e, S=S, mp=mp, n0=n0, sbk=sbk: e.scalar_tensor_tensor(
                                    out=w2[sbk][:].rearrange("p (k a n) -> p k a n", k=2, a=2)[:, :, mp, :],
                                    in0=S[:].rearrange("p (k a n) -> p k a n", k=2, a=2)[:, :, mp, :], scalar=0.125,
                                    in1=btab[:, n0:n0 + 2, :], op0=ALU.mult, op1=ALU.add),
                                    reads=[f"ps{sbk}", "btab"], writes=[f"w2{sbk}"])
                            T.op("act", lambda e, P=P, sbk=sbk: e.activation(out=P[:], in_=w2[sbk][:], func=AF.Exp), reads=[f"w2{sbk}"], writes=[f"wb{sbk}"])
                        def fpv(e, pp=pp, P=P, npairs=npairs):
                            r = None
                            for kb in range(2):
                                j = 2 * pp + kb
                                for mp in range(2):
                                    r = e.matmul(out=ps[3 + mp][:, 0:130], lhsT=P[:, kb * 256 + mp * 128: kb * 256 + (mp + 1) * 128], rhs=VH[:, j, :],
                                                 start=(j == 0), stop=(j == 2 * npairs - 1))
                            return r
                        T.op("pe", fpv, reads=[f"wb{sbk}", "VH1"] + VHN, writes=["ps3", "ps4"])
                    r1, r1n = newstat()
                    r2, r2n = newstat()
                    T.op("dve", lambda e, r1=r1: e.reciprocal(out=r1, in_=ps[3][:, 128:129]), reads=["ps3"], writes=[r1n])
                    T.op("dve", lambda e, r2=r2: e.reciprocal(out=r2, in_=ps[4][:, 128:129]), reads=["ps4"], writes=[r2n])
                    T.op("dve", lambda e, r2=r2: e.tensor_tensor(out=r2, in0=r2, in1=lamw[:, 2:3], op=ALU.mult), reads=[r2n, "lamw"], writes=[r2n])
                    at = w1[1]
                    T.op("dve", lambda e, r2=r2: e.tensor_scalar(out=at[:, 0:128], in0=ps[4][:, 0:128], scalar1=r2, scalar2=None, op0=ALU.mult),
                         reads=["ps4", r2n], writes=["w1_1"])
                    T.op("dve", lambda e, r1=r1: e.scalar_tensor_tensor(out=at[:, 128:256], in0=ps[3][:, 0:128], scalar=r1, in1=at[:, 0:128], op0=ALU.mult, op1=ALU.add),
                         reads=["ps3", r1n, "w1_1"], writes=["w1_1b"])
                    rr, rrn = rstd_from(at[:, 128:256], "w1_1b", 128)
                    T.op("dve", lambda e, rr=rr: e.scalar_tensor_tensor(out=hb[1][:, 0:128], in0=at[:, 128:256], scalar=rr, in1=gsub[:], op0=ALU.mult, op1=ALU.mult),
                         reads=["w1_1b", rrn, "gsub"], writes=["hb1"])
                    transpose_to(CT[:, 2 + h:3 + h, qs], "CT", hb[1], "hb1", 1)
            T.barrier()
            if upto == 2:
                return
            ld(gv[:, 0, :], gvec[l][:, 1, :], "gv0")
            ld(gv[:, 1, :], gvec[l][:, 2, :], "gv1")
            for kc in range(8):
                T.dma("pool", lambda e, kc=kc: e.dma_start(out=WOUT[:, kc, :].rearrange("p (a b) -> p a b", b=512),
                                                           in_=w_out[l, kc * 128:(kc + 1) * 128, :].rearrange("p (a b) -> p a b", b=512)),
                      writes=[f"arenaW{kc}"], slot=f"win{kc}")
            for m in range(NT):
                b = m % 2
                qs = slice(m * 128, (m + 1) * 128)
                T.dma("sp", lambda e, m=m, b=b: e.dma_start(out=xt[b][:], in_=xsrc[m]), writes=[f"xt{b}"], slot=f"xt{b}")
                for nh in range(2):
                    def fo(e, nh=nh, qs=qs):
                        r = None
                        for kc in range(8):
                            r = e.matmul(out=ps[5 + nh][:], lhsT=CT[:, kc, qs], rhs=WOUT[:, kc, nh * 512:(nh + 1) * 512], start=(kc == 0), stop=(kc == 7))
                        return r
                    T.op("pe", fo, reads=["CT"] + ARW, writes=[f"ps{5 + nh}"])
                    T.op("dve", lambda e, nh=nh, b=b: e.tensor_copy(out=w1[b][:, nh * 512:(nh + 1) * 512], in_=ps[5 + nh][:]),
                         reads=[f"ps{5 + nh}"], writes=[f"w1_{b}"])
                r, rn = rstd_from(w1[b][:], f"w1_{b}", D)
                T.op("dve", lambda e, b=b, r=r: e.scalar_tensor_tensor(out=w1[b][:], in0=w1[b][:], scalar=r, in1=gv[:, 0, :], op0=ALU.mult, op1=ALU.mult),
                     reads=[f"w1_{b}", rn, "gv0"], writes=[f"w1_{b}"])
                T.op("dve", lambda e, b=b: e.tensor_tensor(out=xt[b][:], in0=xt[b][:], in1=w1[b][:], op=ALU.add), reads=[f"xt{b}", f"w1_{b}"], writes=[f"xt{b}"])
                T.dma("sp", lambda e, m=m, b=b: e.dma_start(out=xdst[m], in_=xt[b][:]), reads=[f"xt{b}"], writes=[f"xmid{m}"], slot=f"xo{b}")
                r, rn = rstd_from(xt[b][:], f"xt{b}", D)
                T.op("dve", lambda e, b=b, r=r: e.scalar_tensor_tensor(out=hb[b][:], in0=xt[b][:], scalar=r, in1=gv[:, 1, :], op0=ALU.mult, op1=ALU.mult),
                     reads=[f"xt{b}", rn, "gv1"], writes=[f"hb{b}"])
                transpose_to(H2T[:, :, qs], "H2T", hb[b], f"hb{b}", 8)
            T.op("dve", lambda e: e.tensor_copy(out=tailT[:], in_=H2T[:].rearrange("p k (t n) -> p k t n", n=128)[:, :, :, 126:128]),
                 reads=["H2T"], writes=["tailT"])
            T.dma("sp", lambda e: e.dma_start(out=Tc, in_=tailT[:].rearrange("p k t n -> p (k t n)")), reads=["tailT"], writes=["Tc"], slot="tc")

        def phase_C(l, xsrc, xdst):
            ld(gv[:, 0, :], gvec[l][:, 3, :], "gv0")
            T.op("pool", lambda e: e.memset(candT[:, 0, :], 0.0), writes=["cand00", "cand01"])
            cv_ = candT[:].rearrange("p r (k t n) -> p r k t n", k=8, t=NT)
            tgv = Tg.rearrange("(r p) (k t n) -> r p k t n", p=128, k=8, t=NT)
            if NT > 1:
                for k in range(8):
                    T.dma("sp", lambda e, k=k: e.dma_start(out=cv_[:, 0, k, 1:NT, :], in_=tgv[7, :, k, 0:NT - 1, :]), reads=["Tg"], writes=[f"cand0{k % 2}"], slot=f"cand0{k % 2}")
            for r_ in range(1, 8):
                T.dma("sp", lambda e, r_=r_: e.dma_start(out=candT[:, r_, :], in_=Tg[(r_ - 1) * 128:r_ * 128, :]), reads=["Tg"], writes=[f"cand{r_}"], slot=f"cand{r_}")
            halo_select(halTf[:], "halTf", candT, CANDN, 8 * NT * 2)
            T.op("dve", lambda e: e.tensor_copy(out=halT[:].rearrange("p k t n -> p (k t n)"), in_=halTf[:]), reads=["halTf"], writes=["halT"])
            for j in range(22):
                T.dma("pool", lambda e, j=j: e.dma_start(out=WDN[:, j, :].rearrange("p (a b) -> p a b", b=512),
                                                         in_=w_down[l, j * 128:(j + 1) * 128, :].rearrange("p (a b) -> p a b", b=512)),
                      writes=[f"wdn{j % 4}"], slot=f"wdn{j % 4}")
            NPASS = max(1, NT // 8)
            TP = NT // NPASS
            for pa in range(NPASS):
                t0 = pa * TP
                for j in range(22):
                    for part in range(2):
                        fc = part * 22 + j
                        col = part * DFF + j * 128
                        wbuf = wup_t[fc % 3]
                        wn = f"wup{fc % 3}"
                        T.dma("pool", lambda e, col=col, wbuf=wbuf: e.dma_start(out=wbuf[:], in_=w_up[l, :, col:col + 128].rearrange("(k p) n -> p k n", p=128)),
                              writes=[wn], slot=wn)
                        ub = upsb[part]
                        nbank = (TP + 3) // 4
                        for hb_ in range(nbank):
                            nt_ = min(4, TP - hb_ * 4)
                            cs = slice((t0 + hb_ * 4) * 128, (t0 + hb_ * 4 + nt_) * 128)
                            pbk = (fc * 2 + hb_) % 4
                            def fu(e, cs=cs, pbk=pbk, wbuf=wbuf, nt_=nt_):
                                r = None
                                for kc in range(8):
                                    r = e.matmul(out=ps[pbk][:, 0:nt_ * 128], lhsT=wbuf[:, kc, :], rhs=H2T[:, kc, cs], start=(kc == 0), stop=(kc == 7))
                                return r
                            T.op("pe", fu, reads=["H2T", wn], writes=[f"ps{pbk}"])
                            T.op("act", lambda e, pbk=pbk, ub=ub, hb_=hb_, nt_=nt_: e.activation(
                                out=ub[:, hb_ * 4:hb_ * 4 + nt_, 2:130], in_=ps[pbk][:, 0:nt_ * 128].rearrange("p (t n) -> p t n", n=128), func=AF.Copy),
                                reads=[f"ps{pbk}"], writes=[f"upsb{part}"])
                        def ft(e, wbuf=wbuf, t0=t0):
                            r = None
                            for kc in range(8):
                                r = e.matmul(out=ps[4][:, 0:TP * 2].rearrange("p (t n) -> p t n", n=2), lhsT=wbuf[:, kc, :], rhs=halT[:, kc, t0:t0 + TP, :], start=(kc == 0), stop=(kc == 7))
                            return r
                        T.op("pe", ft, reads=["halT", wn], writes=["ps4"])
                        T.op("act", lambda e, ub=ub: e.activation(out=ub[:, 0:TP, 0:2], in_=ps[4][:, 0:TP * 2].rearrange("p (t n) -> p t n", n=2), func=AF.Copy),
                             reads=["ps4"], writes=[f"upsb{part}"])
                        ac = acc[part]
                        T.op("dve", lambda e, ub=ub, ac=ac, fc=fc: e.tensor_scalar(out=ac[:, 0:TP, :], in0=ub[:, 0:TP, 2:130], scalar1=fcw[:, fc, 2:3], scalar2=fcw[:, fc, 3:4], op0=ALU.mult, op1=ALU.add),
                             reads=[f"upsb{part}", "fcw"], writes=[f"acc{part}"])
                        for k in range(2):
                            T.op("dve", lambda e, ub=ub, ac=ac, fc=fc, k=k: e.scalar_tensor_tensor(out=ac[:, 0:TP, :], in0=ub[:, 0:TP, k:128 + k], scalar=fcw[:, fc, k:k + 1], in1=ac[:, 0:TP, :], op0=ALU.mult, op1=ALU.add),
                                 reads=[f"upsb{part}", "fcw", f"acc{part}"], writes=[f"acc{part}"])
                        if part == 0:
                            T.op("act", lambda e, ac=ac: e.activation(out=gact[:, 0:TP, :], in_=ac[:, 0:TP, :], func=AF.Gelu_apprx_tanh), reads=["acc0"], writes=["gact"])
                        else:
                            T.op("pool", lambda e, ac=ac, j=j: e.tensor_tensor(out=GT[:, j, 0:TP * 128].rearrange("p (t n) -> p t n", n=128), in0=gact[:, 0:TP, :], in1=ac[:, 0:TP, :], op=ALU.mult),
                                 reads=["gact", "acc1"], writes=["GT"])
                for tt in range(TP):
                    m = t0 + tt
                    b = m % 2
                    T.dma("sp", lambda e, m=m, b=b: e.dma_start(out=xt[b][:], in_=xsrc[m]), reads=[f"xmid{m}"], writes=[f"xt{b}"], slot=f"xt{b}")
                    for nh in range(2):
                        def fd(e, nh=nh, tt=tt):
                            r = None
                            for j in range(22):
                                r = e.matmul(out=ps[5 + nh][:], lhsT=GT[:, j, tt * 128:(tt + 1) * 128], rhs=WDN[:, j, nh * 512:(nh + 1) * 512], start=(j == 0), stop=(j == 21))
                            return r
                        T.op("pe", fd, reads=["GT", "wdn0", "wdn1", "wdn2", "wdn3"], writes=[f"ps{5 + nh}"])
                        T.op("dve", lambda e, nh=nh, b=b: e.tensor_copy(out=w1[b][:, nh * 512:(nh + 1) * 512], in_=ps[5 + nh][:]), reads=[f"ps{5 + nh}"], writes=[f"w1_{b}"])
                    r, rn = rstd_from(w1[b][:], f"w1_{b}", D)
                    T.op("dve", lambda e, b=b, r=r: e.scalar_tensor_tensor(out=w1[b][:], in0=w1[b][:], scalar=r, in1=gv[:, 0, :], op0=ALU.mult, op1=ALU.mult),
                         reads=[f"w1_{b}", rn, "gv0"], writes=[f"w1_{b}"])
                    T.op("dve", lambda e, b=b: e.tensor_tensor(out=xt[b][:], in0=xt[b][:], in1=w1[b][:], op=ALU.add), reads=[f"xt{b}", f"w1_{b}"], writes=[f"xt{b}"])
                    T.dma("sp", lambda e, m=m, b=b: e.dma_start(out=xdst[m], in_=xt[b][:]), reads=[f"xt{b}"], writes=[f"xo{m}"], slot=f"xo{b}")

        wup_t = [sb(f"wup{i}", [128, 8, 128], BF16) for i in range(3)]

        def dump(which):
            T.barrier()
            if which in ("A0", "A1"):
                T.dma("sp", lambda e: e.dma_start(out=dbg["q"], in_=QT[:]), reads=["QT"], writes=["dq"], slot="dbg0")
                T.dma("sp", lambda e: e.dma_start(out=dbg["ct"], in_=CT[:]), reads=["CT"], writes=["dc"], slot="dbg1")
                for i in range(2):
                    T.dma("sp", lambda e, i=i: e.dma_start(out=dbg["gl"].rearrange("p i (t n) -> p i t n", n=128)[:, i], in_=GL[:, i, :, 32:160]), reads=["GL"], writes=[f"dg{i}"], slot=f"dbg2{i}")
                T.dma("sp", lambda e: e.dma_start(out=dbg["kg"], in_=Kg), reads=["Kg"], writes=["dk"], slot="dbg3")
                T.dma("sp", lambda e: e.dma_start(out=dbg["vg"], in_=Vg), reads=["Vg"], writes=["dv"], slot="dbg4")
            if which in ("B0", "B1"):
                T.dma("sp", lambda e: e.dma_start(out=dbg["ct"], in_=CT[:]), reads=["CT"], writes=["dc"], slot="dbg1")
                T.dma("sp", lambda e: e.dma_start(out=dbg["h2"], in_=H2T[:]), reads=["H2T"], writes=["dh"], slot="dbg2")
                for m in range(NT):
                    T.dma("sp", lambda e, m=m: e.dma_start(out=y_out[m], in_=xbuf[0][m]), reads=[f"xmid{m}"], writes=[f"y{m}"], slot=f"dy{m % 4}")
            T.barrier()

        done = False
        for l in range(DEPTH):
            xsrc = x_in if l == 0 else xbuf[1]
            layer_consts(l)
            phase_A(l, xsrc)
            T.barrier()
            gather(Kc, Kg, [f"Kc{g}_{h}" for g in range(NT // 4) for h in range(4)], "Kg")
            gather(Vc, Vg, [f"Vc{m}" for m in range(NT)], "Vg")
            gather(Hc, Hg, ["Hc0", "Hc1"], "Hg")
            T.barrier()
            if stop_after == f"A{l}":
                dump(stop_after); done = True; break
            if stop_after in (f"P{l}", f"Q{l}"):
                phase_B(l, xsrc, xbuf[0], upto=1 if stop_after[0] == "P" else 2)
                T.barrier()
                T.dma("sp", lambda e: e.dma_start(out=dbg["ct"], in_=CT[:]), reads=["CT"], writes=["dc"], slot="dbg1")
                T.barrier()
                done = True
                break
            phase_B(l, xsrc, xbuf[0])
            T.barrier()
            gather(Tc, Tg, ["Tc"], "Tg")
            T.barrier()
            if stop_after == f"B{l}":
                dump(stop_after); done = True; break
            phase_C(l, xbuf[0], y_out if l == DEPTH - 1 else xbuf[1])
            T.barrier()
            if stop_after == f"C{l}":
                for m in range(NT):
                    T.dma("sp", lambda e, m=m: e.dma_start(out=y_out[m], in_=xbuf[1][m]), reads=[f"xo{m}"], writes=[f"y{m}"], slot=f"dy{m % 4}")
                T.barrier()
                done = True
                break
        T.barrier()
        T.run(block)
    return nc


def host_inputs(inputs, NT=16):
    f = lambda a: np.ascontiguousarray(np.asarray(a, dtype=np.float32))
    x = f(inputs["x"])[0]
    S = x.shape[0]
    nblk = S // 128
    assert nblk == NT * 8
    xb = x.reshape(NT, 8, 128, D)
    bc = lambda a, shape: np.ascontiguousarray(np.broadcast_to(a, shape))
    gvec = np.stack([f(inputs[k]) for k in ("pre_mix_g", "post_mix_g", "pre_ffn_g", "post_ffn_g")], 1)
    gvec = bc(gvec[:, None], (DEPTH, 128, 4, D))
    gmv = np.stack([f(inputs["gm_ln_g"]), f(inputs["gm_ln_b"])], 1)
    gmv = bc(gmv[:, None], (DEPTH, 128, 2, 256))
    gm_wT = np.ascontiguousarray(f(inputs["gm_w_s"]).transpose(0, 3, 1, 2))
    gm_bs = np.ascontiguousarray(np.repeat(f(inputs["gm_b_s"]).transpose(0, 2, 1), 64, axis=2))
    tri = (np.arange(128)[:, None] <= np.arange(128)[None, :]).astype(np.float32)
    lam_in = np.stack([f(inputs[k]) for k in ("da_lq1", "da_lk1", "da_lq2", "da_lk2")], 1)
    lam_in = bc(lam_in[:, None], (DEPTH, 128, 4, 64))
    subln = bc(f(inputs["da_subln_g"])[:, None], (DEPTH, 128, 128))
    rb = f(inputs["rel_bias"])
    cfar = bc(rb[31][None], (128, 4))
    cvw = np.ascontiguousarray(f(inputs["cv_dw_w"]).reshape(DEPTH, 31, 2, 128).transpose(0, 3, 2, 1))
    cvp = np.stack([f(inputs[k]).reshape(DEPTH, 2, 128) for k in ("cv_dw_b", "cv_ln_g", "cv_ln_b")], -1)
    cvp = np.ascontiguousarray(cvp.transpose(0, 2, 1, 3))
    gavg = np.zeros((128, 128), np.float32)
    gavg[:64, :64] = 1.0 / 64
    gavg[64:, 64:] = 1.0 / 64
    fw = f(inputs["ffn_conv_w"]).reshape(DEPTH, 3, NFC, 128)
    fb = f(inputs["ffn_conv_b"]).reshape(DEPTH, 1, NFC, 128)
    fcw = np.ascontiguousarray(np.concatenate([fw, fb], 1).transpose(0, 3, 2, 1))
    ident = np.eye(128, dtype=np.float32)
    common = dict(w_in=f(inputs["w_in"]), w_out=f(inputs["w_out"]), ffn_w_up=f(inputs["ffn_w_up"]), ffn_w_down=f(inputs["ffn_w_down"]),
                  gvec=gvec, gmv=gmv, gm_wT=gm_wT, gm_bs=gm_bs, tri=tri, lam_in=lam_in, subln=subln, cfar=cfar,
                  cvw=cvw, cvp=cvp, gavg=gavg, fcw=fcw, ident=ident)
    k = np.arange(128)[:, None, None]
    n = np.arange(10)[None, :, None]
    q = np.arange(128)[None, None, :]
    maps = []
    for c in range(NCORES):
        rel = (c + 2 - n) * 128 + q - k
        idx = t5_bucket(rel)
        tab = rb[idx]
        tab = np.where((rel >= 0)[..., None], tab, np.float32(NEG)).astype(np.float32)
        btab = np.ascontiguousarray(tab.transpose(3, 0, 1, 2))
        sel = np.zeros((128, 8), np.float32)
        sel[:, c] = 1.0
        d = dict(common)
        d.update(x=np.ascontiguousarray(xb[:, c]), btab=btab, sel=sel)
        maps.append(d)
    return maps


_CACHE = {}


def kernel(**inputs):
    NT = 16
    if "nc" not in _CACHE:
        _CACHE["nc"] = build(NT)
    nc = _CACHE["nc"]
    maps = host_inputs(inputs, NT)
    res = run_bass_kernel_spmd(nc, maps, core_ids=list(range(NCORES)))
    out = np.zeros((NT, 8, 128, D), np.float32)
    for c in range(NCORES):
        out[:, c] = np.asarray(res.results[c]["y"])
    return out.reshape(1, NT * 8 * 128, D)
```

```python
import math
from contextlib import ExitStack
import numpy as np
import concourse.bass as bass
import concourse.mybir as mybir
from concourse.bass_utils import run_bass_kernel_spmd

F32, BF16 = mybir.dt.float32, mybir.dt.bfloat16
AF = mybir.ActivationFunctionType
ALU = mybir.AluOpType
NCORES = 8
D = 1024
INW = 2560
DFF = 2816
NFC = 44
EPS = 1e-6
NEG = -30000.0
DEPTH = 2
ENG = ("pe", "act", "dve", "pool", "sp")


class Tracker:
    def __init__(self, nc, stack):
        self.nc, self.stack = nc, stack
        self.ops = {e: [] for e in ENG}
        self.sem, self.cnt = {}, {}
        self.waited = {e: {} for e in ENG}
        self.lastw, self.readers = {}, {}
        for e in ENG:
            self._mk("E" + e)

    def _mk(self, name):
        if name not in self.sem:
            self.sem[name] = self.stack.enter_context(self.nc.semaphore(name))
            self.cnt[name] = 0
        return name

    def _deps(self, reads, writes):
        deps = {}
        def add(tok):
            if tok is not None:
                deps[tok[0]] = max(deps.get(tok[0], 0), tok[1])
        for b in reads:
            add(self.lastw.get(b))
        for b in writes:
            add(self.lastw.get(b))
            for s, v in self.readers.get(b, {}).items():
                add((s, v))
        return deps

    def _waits(self, eng, deps):
        w = []
        for s, v in deps.items():
            if eng == "pe" and s == "Epe":
                continue
            if self.waited[eng].get(s, 0) < v:
                self.waited[eng][s] = v
                w.append((self.sem[s], v))
        return w

    def _commit(self, tok, reads, writes):
        for b in reads:
            self.readers.setdefault(b, {})[tok[0]] = tok[1]
        for b in writes:
            self.lastw[b] = tok
            self.readers[b] = {}

    def op(self, eng, fn, reads=(), writes=()):
        w = self._waits(eng, self._deps(reads, writes))
        s = "E" + eng
        self.cnt[s] += 1
        tok = (s, self.cnt[s])
        sem = self.sem[s]
        def emit(e, fn=fn, w=w, sem=sem):
            for sh, v in w:
                e.wait_ge(sh, v)
            fn(e).then_inc(sem, 1)
        self.ops[eng].append(emit)
        self._commit(tok, reads, writes)

    def dma(self, eng, fn, reads=(), writes=(), slot=None, inc=16):
        w = self._waits(eng, self._deps(reads, writes))
        s = self._mk("D" + slot)
        self.cnt[s] += inc
        tok = (s, self.cnt[s])
        sem = self.sem[s]
        def emit(e, fn=fn, w=w, sem=sem, inc=inc):
            for sh, v in w:
                e.wait_ge(sh, v)
            fn(e).then_inc(sem, inc)
        self.ops[eng].append(emit)
        self._commit(tok, reads, writes)

    def barrier(self):
        allv = {s: v for s, v in self.cnt.items() if v > 0}
        for eng in ENG:
            w = self._waits(eng, dict(allv))
            def emit(e, w=w):
                for sh, v in w:
                    e.wait_ge(sh, v)
            self.ops[eng].append(emit)

    def run(self, block):
        for name, meth in (("pe", block.tensor), ("act", block.scalar), ("dve", block.vector),
                           ("pool", block.gpsimd), ("sp", block.sync)):
            ops = self.ops[name]
            def body(e, ops=ops):
                for o in ops:
                    o(e)
            meth(body)


def t5_bucket(n):
    n = np.maximum(n, 0)
    nf = np.maximum(n, 1).astype(np.float32)
    large = 16 + (np.log(nf / np.float32(16)) / np.float32(math.log(128 / 16)) * np.float32(16)).astype(np.int32)
    large = np.minimum(large, 31)
    return np.where(n < 16, n, large)


def build(NT=16, stop_after=None):
    S_LOC = NT * 128
    NBLK = NT * 8
    nc = bass.Bass("TRN2", target_bir_lowering=False)
    dt_in = lambda name, shape, dt=F32: nc.dram_tensor(name, list(shape), dt, kind="ExternalInput").ap()
    x_in = dt_in("x", [NT, 128, D])
    w_in = dt_in("w_in", [DEPTH, D, INW])
    w_out = dt_in("w_out", [DEPTH, D, D])
    w_up = dt_in("ffn_w_up", [DEPTH, D, 2 * DFF])
    w_down = dt_in("ffn_w_down", [DEPTH, DFF, D])
    gvec = dt_in("gvec", [DEPTH, 128, 4, D])
    gmv = dt_in("gmv", [DEPTH, 128, 2, 256])
    gm_wT = dt_in("gm_wT", [DEPTH, 128, 4, 128])
    gm_bs = dt_in("gm_bs", [DEPTH, 128, 256])
    tri_in = dt_in("tri", [128, 128])
    lam_in = dt_in("lam_in", [DEPTH, 128, 4, 64])
    subln_in = dt_in("subln", [DEPTH, 128, 128])
    btab_in = dt_in("btab", [4, 128, 12, 128])
    cfar_in = dt_in("cfar", [128, 4])
    cvw_in = dt_in("cvw", [DEPTH, 128, 2, 31])
    cvp_in = dt_in("cvp", [DEPTH, 128, 2, 3])
    gavg_in = dt_in("gavg", [128, 128])
    fcw_in = dt_in("fcw", [DEPTH, 128, NFC, 4])
    sel_in = dt_in("sel", [128, 8])
    ident_in = dt_in("ident", [128, 128])
    y_out = nc.dram_tensor("y", [NT, 128, D], F32, kind="ExternalOutput").ap()
    dbg = {}
    if stop_after is not None:
        dbg["q"] = nc.dram_tensor("dbg_q", [128, 4, S_LOC], BF16, kind="ExternalOutput").ap()
        dbg["ct"] = nc.dram_tensor("dbg_ct", [128, 8, S_LOC], BF16, kind="ExternalOutput").ap()
        dbg["gl"] = nc.dram_tensor("dbg_gl", [128, 2, S_LOC], F32, kind="ExternalOutput").ap()
        dbg["kg"] = nc.dram_tensor("dbg_kg", [NCORES * 512, S_LOC], BF16, kind="ExternalOutput").ap()
        dbg["vg"] = nc.dram_tensor("dbg_vg", [NCORES * S_LOC, 512], BF16, kind="ExternalOutput").ap()
        dbg["h2"] = nc.dram_tensor("dbg_h2", [128, 8, S_LOC], BF16, kind="ExternalOutput").ap()

    xbuf = [nc.dram_tensor(f"xbuf{i}", [NT, 128, D], F32).ap() for i in range(2)]
    Kc = nc.dram_tensor("Kc", [512, S_LOC], BF16).ap()
    Vc = nc.dram_tensor("Vc", [S_LOC, 512], BF16).ap()
    Hc = nc.dram_tensor("Hc", [128, 2 * NT * 32], F32).ap()
    Tc = nc.dram_tensor("Tc", [128, 8 * NT * 2], BF16).ap()
    Kg = nc.dram_tensor("Kg", [NCORES * 512, S_LOC], BF16, addr_space="Shared").ap()
    Vg = nc.dram_tensor("Vg", [NCORES * S_LOC, 512], BF16, addr_space="Shared").ap()
    Hg = nc.dram_tensor("Hg", [NCORES * 128, 2 * NT * 32], F32, addr_space="Shared").ap()
    Tg = nc.dram_tensor("Tg", [NCORES * 128, 8 * NT * 2], BF16, addr_space="Shared").ap()

    with ExitStack() as st:
        sb = lambda name, shape, dt=F32: st.enter_context(nc.sbuf_tensor("s_" + name, list(shape), dt))
        psS = [st.enter_context(nc.psum_tensor(f"psS{i}", [128, 1024], F32)) for i in range(2)]
        ps = [psS[0][:, 0:512], psS[0][:, 512:1024], psS[1][:, 0:512], psS[1][:, 512:1024]] + \
             [st.enter_context(nc.psum_tensor(f"ps{i}", [128, 512], F32)) for i in range(4, 7)]
        psT = st.enter_context(nc.psum_tensor("psT", [128, 1024], BF16))
        ARENA = sb("ARENA", [128, 63488], BF16)
        ident = sb("ident", [128, 128], BF16)
        identf = sb("identf", [128, 128], F32)
        gv = sb("gv", [128, 2, D], F32)
        wT = sb("wT", [128, 4, 128], BF16)
        wTf = sb("wTf", [128, 4, 128], F32)
        tri = sb("tri", [128, 128], F32)
        gmvt = sb("gmvt", [128, 2, 256], F32)
        gbs = sb("gbs", [128, 256], F32)
        lamt = sb("lamt", [128, 4, 64], F32)
        lamw = sb("lamw", [128, 8], F32)
        gsub = sb("gsub", [128, 128], F32)
        cfar = sb("cfar", [128, 4], F32)
        btab = sb("btab", [128, 12, 128], F32)
        cvw = sb("cvw", [128, 2, 31], F32)
        cvp = sb("cvp", [128, 2, 3], F32)
        gavg = sb("gavg", [128, 128], F32)
        fcw = sb("fcw", [128, NFC, 4], F32)
        sel = sb("sel", [128, 8], F32)
        stt = sb("stt", [128, 64], F32)
        bnst = sb("bnst", [128, 2, 8], F32)
        xt = [sb(f"xt{i}", [128, D], F32) for i in range(2)]
        junk = sb("junk", [128, D], BF16)
        hb = [sb(f"hb{i}", [128, D], BF16) for i in range(2)]
        w1 = [sb(f"w1_{i}", [128, D], F32) for i in range(2)]
        w2all = sb("w2all", [128, 3, 512], F32)
        w2 = [w2all[:, i, :] for i in range(3)]
        wball = sb("wball", [128, 2, 1024], BF16)
        wb = [wball[:, 0, 0:512], wball[:, 0, 512:1024], wball[:, 1, 0:512], wball[:, 1, 512:1024]]
        candT = sb("candT", [128, 8, 8 * NT * 2], BF16)
        halT = sb("halT", [128, 8, NT, 2], BF16)
        halTf = sb("halTf", [128, 8 * NT * 2], F32)
        tailT = sb("tailT", [128, 8, NT, 2], BF16)
        upsb = [sb(f"upsb{i}", [128, 8, 130], F32) for i in range(2)]
        acc1 = sb("acc1", [128, 8, 128], F32)
        acc = [w2all[:, 0:2, :].rearrange("p a (t n) -> p (a t) n", n=128), acc1[:]]
        gact = btab[:, 0:8, :]
        cvq = w2[2]

        block = st.enter_context(nc.Block())
        T = Tracker(nc, st)

        WIN = ARENA[:, 0:8 * INW].rearrange("p (k n) -> p k n", k=8)
        HT = [ARENA[:, 20480 + i * 4096: 20480 + (i + 1) * 4096].rearrange("p (k n) -> p k n", k=8) for i in range(2)]
        GL = ARENA[:, 28672:38912].bitcast(F32).rearrange("p (i t n) -> p i t n", i=2, n=160)[:, :, 0:NT, :]
        CT = ARENA[:, 38912:55296].rearrange("p (k n) -> p k n", k=8)[:, :, 0:S_LOC]
        QT = ARENA[:, 55296:63488].rearrange("p (k n) -> p k n", k=4)[:, :, 0:S_LOC]
        cand = ARENA[:, 0:16384].bitcast(F32).rearrange("p (r n) -> p r n", r=8)[:, :, 0:2 * NT * 32]
        cvy = ARENA[:, 16384:24576].bitcast(F32).rearrange("p (i n) -> p i n", i=2)[:, :, 0:S_LOC]
        KT = ARENA[:, 0:NBLK * 128].rearrange("p (j n) -> p j n", n=128)
        VH = ARENA[:, 16384:16384 + NBLK * 130].rearrange("p (j n) -> p j n", n=130)
        QP = ARENA[:, 33280:37376].rearrange("p (a n) -> p a n", a=2)[:, :, 0:S_LOC]
        WOUT = ARENA[:, 16384:24576].rearrange("p (k n) -> p k n", k=8)
        H2T = ARENA[:, 0:16384].rearrange("p (k n) -> p k n", k=8)[:, :, 0:S_LOC]
        GT = ARENA[:, 16384:38912].rearrange("p (j n) -> p j n", j=22)
        WDN = ARENA[:, 38912:61440].rearrange("p (j n) -> p j n", j=22)

        ARW = [f"arenaW{k}" for k in range(8)]
        KTN = [f"KT{c}" for c in range(8)]
        VHN = [f"VH{c}" for c in range(8)]
        CANDN = ["cand00", "cand01"] + [f"cand{r}" for r in range(1, 8)]
        stat_i = [0]
        def newstat():
            stat_i[0] = (stat_i[0] + 1) % 64
            i = stat_i[0]
            return stt[:, i:i + 1], f"st{i}"

        def rstd_from(src_ap, srcname, n, eng_sq="act"):
            ss, ssn = newstat()
            T.op("act", lambda e: e.activation(out=junk[:, 0:n], in_=src_ap, func=AF.Square, accum_out=ss),
                 reads=[srcname], writes=["junk", ssn])
            return rstd_of(ss, ssn, 1.0 / n)

        def rstd_of(v, vn, scale):
            lnv, lnn = newstat()
            T.op("act", lambda e: e.activation(out=lnv, in_=v, func=AF.Ln, bias=epsT[:, 0:1], scale=scale),
                 reads=[vn], writes=[lnn])
            r, rn = newstat()
            T.op("act", lambda e: e.activation(out=r, in_=lnv, func=AF.Exp, scale=-0.5), reads=[lnn], writes=[rn])
            return r, rn

        epsT = sb("epsT", [128, 1], F32)
        T.op("dve", lambda e: e.memset(epsT[:], EPS), writes=["epsT"])

        def ld(dst, src, name, eng="sp"):
            T.dma(eng, lambda e: e.dma_start(out=dst, in_=src), writes=[name], slot="c_" + name)

        ld(identf[:], ident_in, "identf")
        T.op("dve", lambda e: e.tensor_copy(out=ident[:], in_=identf[:]), reads=["identf"], writes=["ident"])
        ld(tri[:], tri_in, "tri")
        ld(cfar[:], cfar_in, "cfar")
        ld(gavg[:], gavg_in, "gavg")
        ld(sel[:], sel_in, "sel")

        def transpose_to(dst_ap, dstname, src_tile, srcname, nchunk, psname="psT"):
            def f(e):
                r = None
                for k in range(nchunk):
                    r = e.transpose(out=psT[:, k * 128:(k + 1) * 128], in_=src_tile[:, k * 128:(k + 1) * 128], identity=ident[:])
                return r
            T.op("pe", f, reads=[srcname, "ident"], writes=[psname])
            T.op("act", lambda e: e.activation(out=dst_ap, in_=psT[:, 0:nchunk * 128].rearrange("p (k n) -> p k n", k=nchunk), func=AF.Copy),
                 reads=[psname], writes=[dstname])

        def layer_consts(l):
            ld(wTf[:], gm_wT[l], "wTf")
            for h in range(4):
                T.op("dve", lambda e, h=h: e.tensor_tensor(out=wT[:, h, :], in0=wTf[:, h, :], in1=tri[:], op=ALU.mult),
                     reads=["wTf", "tri"], writes=["wT"])
            ld(gmvt[:], gmv[l], "gmvt")
            ld(gbs[:], gm_bs[l], "gbs")
            ld(lamt[:], lam_in[l], "lamt")
            ld(gsub[:], subln_in[l], "gsub")
            ld(cvw[:], cvw_in[l], "cvw")
            ld(cvp[:], cvp_in[l], "cvp")
            ld(fcw[:], fcw_in[l], "fcw")
            lam_init = 0.8 - 0.6 * math.exp(-0.3 * l)
            for i in range(2):
                T.op("dve", lambda e, i=i: e.tensor_tensor(out=junk[:, i * 64:(i + 1) * 64], in0=lamt[:, 2 * i, :], in1=lamt[:, 2 * i + 1, :], op=ALU.mult),
                     reads=["lamt"], writes=["junk"])
                T.op("dve", lambda e, i=i: e.reduce_sum(out=lamw[:, i:i + 1], in_=junk[:, i * 64:(i + 1) * 64], axis=mybir.AxisListType.X),
                     reads=["junk"], writes=["lamw"])
            T.op("act", lambda e: e.activation(out=lamw[:, 4:6], in_=lamw[:, 0:2], func=AF.Exp), reads=["lamw"], writes=["lamw"])
            T.op("dve", lambda e: e.tensor_tensor(out=lamw[:, 6:7], in0=lamw[:, 5:6], in1=lamw[:, 4:5], op=ALU.subtract),
                 reads=["lamw"], writes=["lamw"])
            T.op("dve", lambda e: e.tensor_scalar(out=lamw[:, 2:3], in0=lamw[:, 6:7], scalar1=-lam_init, scalar2=None, op0=ALU.add),
                 reads=["lamw"], writes=["lamw"])
            T.op("dve", lambda e: e.tensor_scalar(out=gsub[:], in0=gsub[:], scalar1=1.0 - lam_init, scalar2=None, op0=ALU.mult),
                 reads=["gsub"], writes=["gsub"])

        def phase_A(l, xsrc):
            ld(gv[:, 0, :], gvec[l][:, 0, :], "gv0")
            for kc in range(8):
                T.dma("pool", lambda e, kc=kc: e.dma_start(
                    out=WIN[:, kc, :].rearrange("p (a b) -> p a b", b=512),
                    in_=w_in[l, kc * 128:(kc + 1) * 128, :].rearrange("p (a b) -> p a b", b=512)),
                    writes=[f"arenaW{kc}"], slot=f"win{kc}")
            for g in range(NT // 4):
                hT = HT[g % 2]
                hTn = f"hT{g % 2}"
                for t in range(4):
                    m = 4 * g + t
                    b = m % 2
                    T.dma("sp", lambda e, m=m, b=b: e.dma_start(out=xt[b][:], in_=xsrc[m]), writes=[f"xt{b}"], slot=f"xt{b}")
                    r, rn = rstd_from(xt[b][:], f"xt{b}", D)
                    T.op("dve", lambda e, b=b, r=r: e.scalar_tensor_tensor(out=hb[b][:], in0=xt[b][:], scalar=r, in1=gv[:, 0, :], op0=ALU.mult, op1=ALU.mult),
                         reads=[f"xt{b}", rn, "gv0"], writes=[f"hb{b}"])
                    transpose_to(hT[:, :, t * 128:(t + 1) * 128], hTn, hb[b], f"hb{b}", 8)
                order = [("q", h, 512 + 128 * h) for h in range(4)] + [("k", h, 1024 + 128 * h) for h in range(4)] + \
                        [("cg", i, 2304 + 128 * i) for i in range(2)] + [("ca", i, 2048 + 128 * i) for i in range(2)]
                for oi, (kind, idx, col) in enumerate(order):
                    pb = oi % 2
                    def f(e, col=col, pb=pb, hT=hT):
                        r = None
                        for kc in range(8):
                            r = e.matmul(out=ps[pb][:], lhsT=WIN[:, kc, col:col + 128], rhs=hT[:, kc, :], start=(kc == 0), stop=(kc == 7))
                        return r
                    T.op("pe", f, reads=[hTn] + ARW, writes=[f"ps{pb}"])
                    tok = slice(g * 512, (g + 1) * 512)
                    if kind == "q":
                        T.op("act", lambda e, pb=pb, idx=idx, tok=tok: e.activation(out=QT[:, idx, tok], in_=ps[pb][:], func=AF.Copy),
                             reads=[f"ps{pb}"], writes=["QT"])
                    elif kind == "k":
                        wi = idx % 3
                        T.op("dve", lambda e, pb=pb, wi=wi: e.tensor_copy(out=wb[wi][:], in_=ps[pb][:]), reads=[f"ps{pb}"], writes=[f"wb{wi}"])
                        T.dma("sp", lambda e, wi=wi, idx=idx, tok=tok: e.dma_start(out=Kc[idx * 128:(idx + 1) * 128, tok], in_=wb[wi][:]),
                              reads=[f"wb{wi}"], writes=[f"Kc{g}_{idx}"], slot=f"wb{wi}")
                    elif kind == "cg":
                        T.op("act", lambda e, pb=pb, idx=idx: e.activation(out=w2[idx][:], in_=ps[pb][:], func=AF.Sigmoid),
                             reads=[f"ps{pb}"], writes=[f"w2{idx}"])
                    else:
                        T.op("dve", lambda e, pb=pb, idx=idx, g=g: e.tensor_tensor(
                            out=GL[:, idx, 4 * g:4 * g + 4, 32:160], in0=ps[pb][:].rearrange("p (t n) -> p t n", t=4),
                            in1=w2[idx][:].rearrange("p (t n) -> p t n", t=4), op=ALU.mult),
                            reads=[f"ps{pb}", f"w2{idx}"], writes=["GL"])
                for t in range(4):
                    m = 4 * g + t
                    tcols = slice(t * 128, (t + 1) * 128)
                    def fg(e, tcols=tcols, hT=hT):
                        r = None
                        for kc in range(8):
                            r = e.matmul(out=ps[2][:], lhsT=hT[:, kc, tcols], rhs=WIN[:, kc, 0:512], start=(kc == 0), stop=(kc == 7))
                        return r
                    T.op("pe", fg, reads=[hTn] + ARW, writes=["ps2"])
                    def fv(e, tcols=tcols, hT=hT):
                        r = None
                        for kc in range(8):
                            r = e.matmul(out=ps[3][:], lhsT=hT[:, kc, tcols], rhs=WIN[:, kc, 1536:2048], start=(kc == 0), stop=(kc == 7))
                        return r
                    T.op("pe", fv, reads=[hTn] + ARW, writes=["ps3"])
                    T.op("dve", lambda e: e.tensor_copy(out=wb[2][:], in_=ps[3][:]), reads=["ps3"], writes=["wb2"])
                    T.dma("sp", lambda e, m=m: e.dma_start(out=Vc[m * 128:(m + 1) * 128, :], in_=wb[2][:]),
                          reads=["wb2"], writes=[f"Vc{m}"], slot="wb2")
                    ug = w1[0]
                    T.op("act", lambda e: e.activation(out=ug[:, 0:512], in_=ps[2][:], func=AF.Gelu_apprx_tanh), reads=["ps2"], writes=["w1_0"])
                    T.op("dve", lambda e: e.bn_stats(out=bnst[:, 0, 0:6], in_=ug[:, 256:512]), reads=["w1_0"], writes=["bnst0"])
                    T.op("dve", lambda e: e.bn_aggr(out=bnst[:, 1, 0:2], in_=bnst[:, 0, 0:6]), reads=["bnst0"], writes=["bnst1"])
                    r, rn = rstd_of(bnst[:, 1, 1:2], "bnst1", 1.0)
                    vn = w1[0][:, 512:768]
                    T.op("dve", lambda e, r=r: e.tensor_scalar(out=vn, in0=ug[:, 256:512], scalar1=bnst[:, 1, 0:1], scalar2=r, op0=ALU.subtract, op1=ALU.mult),
                         reads=["w1_0", "bnst1", rn], writes=["w1_0b"])
                    T.op("dve", lambda e: e.tensor_tensor(out=vn, in0=vn, in1=gmvt[:, 0, :], op=ALU.mult), reads=["w1_0b", "gmvt"], writes=["w1_0b"])
                    T.op("dve", lambda e: e.tensor_tensor(out=hb[0][:, 0:256], in0=vn, in1=gmvt[:, 1, :], op=ALU.add), reads=["w1_0b", "gmvt"], writes=["hb0"])
                    def fs(e):
                        r = None
                        for h in range(4):
                            r = e.matmul(out=ps[4][:, h * 64:(h + 1) * 64], lhsT=wT[:, h, :], rhs=hb[0][:, h * 64:(h + 1) * 64], start=True, stop=True)
                        return r
                    T.op("pe", fs, reads=["hb0", "wT"], writes=["ps4"])
                    T.op("dve", lambda e: e.tensor_tensor(out=w1[0][:, 768:1024], in0=ps[4][:, 0:256], in1=gbs[:], op=ALU.add),
                         reads=["ps4", "gbs"], writes=["w1_0c"])
                    T.op("dve", lambda e: e.tensor_tensor(out=hb[1][:, 0:256], in0=w1[0][:, 768:1024], in1=ug[:, 0:256], op=ALU.mult),
                         reads=["w1_0c", "w1_0"], writes=["hb1"])
                    transpose_to(CT[:, 0:2, m * 128:(m + 1) * 128], "CT", hb[1], "hb1", 2)
            for i in range(2):
                T.dma("sp", lambda e, i=i: e.dma_start(out=Hc.rearrange("p (i t n) -> p i t n", i=2, t=NT)[:, i], in_=GL[:, i, :, 128:160]),
                      reads=["GL"], writes=[f"Hc{i}"], slot=f"hc{i}")

        def gather(src, dst, reads, name):
            T.dma("pool", lambda e: e.collective_compute("AllGather", ALU.bypass, replica_groups=[list(range(NCORES))], ins=[src], outs=[dst]),
                  reads=reads, writes=[name], slot="cc_" + name, inc=1)

        def halo_select(dst_f32, dstname, cand_t, candname, width):
            T.op("dve", lambda e: e.tensor_scalar(out=dst_f32, in0=cand_t[:, 0, :], scalar1=sel[:, 0:1], scalar2=None, op0=ALU.mult),
                 reads=candname + ["sel"], writes=[dstname])
            for r_ in range(1, 8):
                T.op("dve", lambda e, r_=r_: e.scalar_tensor_tensor(out=dst_f32, in0=cand_t[:, r_, :], scalar=sel[:, r_:r_ + 1], in1=dst_f32, op0=ALU.mult, op1=ALU.add),
                     reads=candname + ["sel", dstname], writes=[dstname])

        def phase_B(l, xsrc, xdst, upto=3):
            kg_names = [f"Kg"]
            T.op("pool", lambda e: e.memset(cand[:, 0, :], 0.0), writes=["cand00", "cand01"])
            cview = cand[:].rearrange("p r (i t n) -> p r i t n", i=2, t=NT)
            hgv = Hg.rearrange("(r p) (i t n) -> r p i t n", p=128, i=2, t=NT)
            if NT > 1:
                for i in range(2):
                    T.dma("sp", lambda e, i=i: e.dma_start(out=cview[:, 0, i, 1:NT, :], in_=hgv[7, :, i, 0:NT - 1, :]), reads=["Hg"], writes=[f"cand0{i}"], slot=f"cand0{i}")
            for r_ in range(1, 8):
                T.dma("sp", lambda e, r_=r_: e.dma_start(out=cand[:, r_, :], in_=Hg[(r_ - 1) * 128:r_ * 128, :]), reads=["Hg"], writes=[f"cand{r_}"], slot=f"cand{r_}")
            hsel = cvy[:, 0, 0:2 * NT * 32]
            halo_select(hsel, "cvy", cand, CANDN, 2 * NT * 32)
            T.op("dve", lambda e: e.tensor_copy(out=GL[:, :, :, 0:32], in_=hsel.rearrange("p (i t n) -> p i t n", i=2, t=NT)),
                 reads=["cvy"], writes=["GL"])
            for i in range(2):
                ceng = "dve"
                T.op(ceng, lambda e, i=i: e.tensor_scalar(out=cvy[:, i, :].rearrange("p (t n) -> p t n", t=NT), in0=GL[:, i, :, 2:130],
                                                           scalar1=cvw[:, i, 0:1], scalar2=cvp[:, i, 0:1], op0=ALU.mult, op1=ALU.add),
                     reads=["GL", "cvw", "cvp"], writes=[f"cvy{i}"])
                for k in range(1, 31):
                    T.op(ceng, lambda e, i=i, k=k: e.scalar_tensor_tensor(
                        out=cvy[:, i, :].rearrange("p (t n) -> p t n", t=NT), in0=GL[:, i, :, 2 + k:130 + k], scalar=cvw[:, i, k:k + 1],
                        in1=cvy[:, i, :].rearrange("p (t n) -> p t n", t=NT), op0=ALU.mult, op1=ALU.add),
                        reads=["GL", "cvw", f"cvy{i}"], writes=[f"cvy{i}"])
            for i in range(2):
                for q4 in range(S_LOC // 512):
                    cs = slice(q4 * 512, (q4 + 1) * 512)
                    T.op("pe", lambda e, i=i, cs=cs: e.matmul(out=ps[0][:], lhsT=gavg[:], rhs=cvy[:, i, cs], start=True, stop=True),
                         reads=[f"cvy{i}", "gavg"], writes=["ps0"])
                    T.op("act", lambda e, i=i, cs=cs: e.activation(out=cvq[:], in_=cvy[:, i, cs], func=AF.Square), reads=[f"cvy{i}"], writes=["cvq"])
                    T.op("pe", lambda e: e.matmul(out=ps[1][:], lhsT=gavg[:], rhs=cvq[:], start=True, stop=True), reads=["cvq", "gavg"], writes=["ps1"])
                    T.op("act", lambda e: e.activation(out=w2[0][:], in_=ps[0][:], func=AF.Square), reads=["ps0"], writes=["w20"])
                    T.op("dve", lambda e: e.tensor_tensor(out=w2[0][:], in0=ps[1][:], in1=w2[0][:], op=ALU.subtract), reads=["ps1", "w20"], writes=["w20"])
                    T.op("act", lambda e: e.activation(out=w2[0][:], in_=w2[0][:], func=AF.Ln, bias=epsT[:, 0:1]), reads=["w20"], writes=["w20"])
                    T.op("act", lambda e: e.activation(out=w2[0][:], in_=w2[0][:], func=AF.Exp, scale=-0.5), reads=["w20"], writes=["w20"])
                    T.op("dve", lambda e, i=i, cs=cs: e.tensor_tensor(out=w2[1][:], in0=cvy[:, i, cs], in1=ps[0][:], op=ALU.subtract), reads=[f"cvy{i}", "ps0"], writes=["w21"])
                    T.op("dve", lambda e: e.tensor_tensor(out=w2[1][:], in0=w2[1][:], in1=w2[0][:], op=ALU.mult), reads=["w21", "w20"], writes=["w21"])
                    T.op("act", lambda e, i=i, cs=cs: e.activation(out=CT[:, 6 + i, cs], in_=w2[1][:], func=AF.Silu, bias=cvp[:, i, 2:3], scale=cvp[:, i, 1:2]),
                         reads=["w21", "cvp"], writes=["CT"])
            T.barrier()
            if upto == 1:
                return
            T.op("pool", lambda e: e.memset(QP[:, :, :], 0.0), writes=["QP"])
            for h in range(4):
                for mp in range(2):
                    T.op("pool", lambda e, mp=mp, h=h: e.tensor_copy(out=QP[mp * 64:(mp + 1) * 64, mp, :], in_=QT[mp * 64:(mp + 1) * 64, h, :]),
                         reads=["QT", "QP"], writes=["QP"])
                for c_ in range(8):
                    T.dma("sp", lambda e, c_=c_, h=h: e.dma_start(
                        out=KT[:, :, :].rearrange("p (m c) n -> p m c n", c=8)[:, :, c_, :],
                        in_=Kg[c_ * 512 + h * 128:c_ * 512 + (h + 1) * 128, :].rearrange("p (m n) -> p m n", n=128)),
                        reads=["Kg"], writes=[f"KT{c_}"], slot=f"kt{c_}")
                    T.dma("sp", lambda e, c_=c_, h=h: e.dma_start(
                        out=VH[:, :, 0:128].rearrange("p (m c) n -> p m c n", c=8)[:, :, c_, :],
                        in_=Vg[c_ * S_LOC:(c_ + 1) * S_LOC, h * 128:(h + 1) * 128].rearrange("(m p) n -> p m n", p=128)),
                        reads=["Vg"], writes=[f"VH{c_}"], slot=f"vh{c_}")
                T.op("pool", lambda e: e.memset(VH[:, :, 128:130], 1.0), writes=["VH1"])
                T.dma("sp", lambda e, h=h: e.dma_start(out=btab[:], in_=btab_in[h]), writes=["btab"], slot="btab")
                units = [(m, qd) for m in range(NT) for qd in range(2 * m + 2)]

                def score(ui, m, qd, h=h):
                    sb_ = ui % 2
                    S = psS[sb_]
                    qs = slice(m * 128, (m + 1) * 128)
                    def fsc(e):
                        r = None
                        for kb in range(4):
                            r = e.matmul(out=S[:, kb * 256:(kb + 1) * 256].rearrange("p (a n) -> p a n", a=2),
                                         lhsT=KT[:, 4 * qd + kb, :], rhs=QP[:, 0:2, qs], start=True, stop=True)
                        return r
                    T.op("pe", fsc, reads=KTN + ["QP"], writes=[f"ps{2 * sb_}", f"ps{2 * sb_ + 1}"])

                def softmax(ui, m, qd, h=h):
                    sb_ = ui % 2
                    S = psS[sb_]
                    P = wball[:, sb_, :]
                    sn = [f"ps{2 * sb_}", f"ps{2 * sb_ + 1}"]
                    pn = [f"wb{2 * sb_}", f"wb{2 * sb_ + 1}"]
                    near = (4 * qd + 3 >= 8 * m - 2)
                    if not near:
                        T.op("act", lambda e: e.activation(out=P, in_=S[:], func=AF.Exp, bias=cfar[:, h:h + 1], scale=0.125),
                             reads=sn + ["cfar"], writes=pn)
                    else:
                        n0 = 4 * qd - (8 * m - 4)
                        stg = w2all[:, 0:2, :]
                        for mp in range(2):
                            T.op("dve", lambda e, mp=mp: e.scalar_tensor_tensor(
                                out=stg.rearrange("p x (k2 a n) -> p (x k2) a n", a=2, n=128)[:, :, mp, :],
                                in0=S[:].rearrange("p (k a n) -> p k a n", k=4, a=2)[:, :, mp, :], scalar=0.125,
                                in1=btab[:, n0:n0 + 4, :], op0=ALU.mult, op1=ALU.add),
                                reads=sn + ["btab"], writes=["w20", "w21"])
                        T.op("act", lambda e: e.activation(out=P, in_=stg.rearrange("p x n -> p (x n)"), func=AF.Exp), reads=["w20", "w21"], writes=pn)

                def pv(ui, m, qd):
                    sb_ = ui % 2
                    P = wball[:, sb_, :]
                    ob = 4
                    nblk = 8 * m + 8
                    def fpv(e):
                        r = None
                        for kb in range(4):
                            j = 4 * qd + kb
                            for mp in range(2):
                                r = e.matmul(out=ps[ob + mp][:, 0:130], lhsT=P[:, kb * 256 + mp * 128: kb * 256 + (mp + 1) * 128], rhs=VH[:, j, :],
                                             start=(j == 0), stop=(j == nblk - 1))
                        return r
                    T.op("pe", fpv, reads=[f"wb{2 * sb_}", f"wb{2 * sb_ + 1}", "VH1"] + VHN, writes=[f"ps{ob}", f"ps{ob + 1}"])

                def fin1(m):
                    ob = 4
                    o1, o2 = ps[ob], ps[ob + 1]
                    o1n, o2n = f"ps{ob}", f"ps{ob + 1}"
                    r1, r1n = newstat()
                    r2, r2n = newstat()
                    T.op("dve", lambda e: e.reciprocal(out=r1, in_=o1[:, 128:129]), reads=[o1n], writes=[r1n])
                    T.op("dve", lambda e: e.reciprocal(out=r2, in_=o2[:, 128:129]), reads=[o2n], writes=[r2n])
                    T.op("dve", lambda e: e.tensor_tensor(out=r2, in0=r2, in1=lamw[:, 2:3], op=ALU.mult), reads=[r2n, "lamw"], writes=[r2n])
                    at = w1[1]
                    T.op("dve", lambda e: e.tensor_scalar(out=at[:, 0:128], in0=o2[:, 0:128], scalar1=r2, scalar2=None, op0=ALU.mult),
                         reads=[o2n, r2n], writes=["w1_1"])
                    T.op("dve", lambda e: e.scalar_tensor_tensor(out=at[:, 128:256], in0=o1[:, 0:128], scalar=r1, in1=at[:, 0:128], op0=ALU.mult, op1=ALU.add),
                         reads=[o1n, r1n, "w1_1"], writes=["w1_1b"])
                    rr, rrn = rstd_from(at[:, 128:256], "w1_1b", 128)
                    hbm = hb[m % 2]
                    T.op("dve", lambda e: e.scalar_tensor_tensor(out=hbm[:, 0:128], in0=at[:, 128:256], scalar=rr, in1=gsub[:], op0=ALU.mult, op1=ALU.mult),
                         reads=["w1_1b", rrn, "gsub"], writes=[f"hb{m % 2}"])

                def fin2(m, h=h):
                    transpose_to(CT[:, 2 + h:3 + h, m * 128:(m + 1) * 128], "CT", hb[m % 2], f"hb{m % 2}", 1)

                pending = []
                score(0, *units[0])
                for ui, (m, qd) in enumerate(units):
                    if ui + 1 < len(units):
                        score(ui + 1, *units[ui + 1])
                    softmax(ui, m, qd)
                    pv(ui, m, qd)
                    for item in list(pending):
                        item[0] -= 1
                        if item[0] <= 0:
                            fin2(item[1])
                            pending.remove(item)
                    if qd == 2 * m + 1:
                        fin1(m)
                        pending.append([3, m])
                for item in pending:
                    fin2(item[1])
            T.barrier()
            if upto == 2:
                return
            ld(gv[:, 0, :], gvec[l][:, 1, :], "gv0")
            ld(gv[:, 1, :], gvec[l][:, 2, :], "gv1")
            for kc in range(8):
                T.dma("pool", lambda e, kc=kc: e.dma_start(out=WOUT[:, kc, :].rearrange("p (a b) -> p a b", b=512),
                                                           in_=w_out[l, kc * 128:(kc + 1) * 128, :].rearrange("p (a b) -> p a b", b=512)),
                      writes=[f"arenaW{kc}"], slot=f"win{kc}")
            for m in range(NT):
                b = m % 2
                qs = slice(m * 128, (m + 1) * 128)
                T.dma("sp", lambda e, m=m, b=b: e.dma_start(out=xt[b][:], in_=xsrc[m]), writes=[f"xt{b}"], slot=f"xt{b}")
                for nh in range(2):
                    def fo(e, nh=nh, qs=qs):
                        r = None
                        for kc in range(8):
                            r = e.matmul(out=ps[5 + nh][:], lhsT=CT[:, kc, qs], rhs=WOUT[:, kc, nh * 512:(nh + 1) * 512], start=(kc == 0), stop=(kc == 7))
                        return r
                    T.op("pe", fo, reads=["CT"] + ARW, writes=[f"ps{5 + nh}"])
                    T.op("dve", lambda e, nh=nh, b=b: e.tensor_copy(out=w1[b][:, nh * 512:(nh + 1) * 512], in_=ps[5 + nh][:]),
                         reads=[f"ps{5 + nh}"], writes=[f"w1_{b}"])
                r, rn = rstd_from(w1[b][:], f"w1_{b}", D)
                T.op("dve", lambda e, b=b, r=r: e.scalar_tensor_tensor(out=w1[b][:], in0=w1[b][:], scalar=r, in1=gv[:, 0, :], op0=ALU.mult, op1=ALU.mult),
                     reads=[f"w1_{b}", rn, "gv0"], writes=[f"w1_{b}"])
                T.op("dve", lambda e, b=b: e.tensor_tensor(out=xt[b][:], in0=xt[b][:], in1=w1[b][:], op=ALU.add), reads=[f"xt{b}", f"w1_{b}"], writes=[f"xt{b}"])
                T.dma("sp", lambda e, m=m, b=b: e.dma_start(out=xdst[m], in_=xt[b][:]), reads=[f"xt{b}"], writes=[f"xmid{m}"], slot=f"xo{b}")
                r, rn = rstd_from(xt[b][:], f"xt{b}", D)
                T.op("dve", lambda e, b=b, r=r: e.scalar_tensor_tensor(out=hb[b][:], in0=xt[b][:], scalar=r, in1=gv[:, 1, :], op0=ALU.mult, op1=ALU.mult),
                     reads=[f"xt{b}", rn, "gv1"], writes=[f"hb{b}"])
                transpose_to(H2T[:, :, qs], "H2T", hb[b], f"hb{b}", 8)
            T.op("dve", lambda e: e.tensor_copy(out=tailT[:], in_=H2T[:].rearrange("p k (t n) -> p k t n", n=128)[:, :, :, 126:128]),
                 reads=["H2T"], writes=["tailT"])
            T.dma("sp", lambda e: e.dma_start(out=Tc, in_=tailT[:].rearrange("p k t n -> p (k t n)")), reads=["tailT"], writes=["Tc"], slot="tc")

        def phase_C(l, xsrc, xdst):
            ld(gv[:, 0, :], gvec[l][:, 3, :], "gv0")
            T.op("pool", lambda e: e.memset(candT[:, 0, :], 0.0), writes=["cand00", "cand01"])
            cv_ = candT[:].rearrange("p r (k t n) -> p r k t n", k=8, t=NT)
            tgv = Tg.rearrange("(r p) (k t n) -> r p k t n", p=128, k=8, t=NT)
            if NT > 1:
                for k in range(8):
                    T.dma("sp", lambda e, k=k: e.dma_start(out=cv_[:, 0, k, 1:NT, :], in_=tgv[7, :, k, 0:NT - 1, :]), reads=["Tg"], writes=[f"cand0{k % 2}"], slot=f"cand0{k % 2}")
            for r_ in range(1, 8):
                T.dma("sp", lambda e, r_=r_: e.dma_start(out=candT[:, r_, :], in_=Tg[(r_ - 1) * 128:r_ * 128, :]), reads=["Tg"], writes=[f"cand{r_}"], slot=f"cand{r_}")
            halo_select(halTf[:], "halTf", candT, CANDN, 8 * NT * 2)
            T.op("dve", lambda e: e.tensor_copy(out=halT[:].rearrange("p k t n -> p (k t n)"), in_=halTf[:]), reads=["halTf"], writes=["halT"])
            for j in range(22):
                T.dma("pool", lambda e, j=j: e.dma_start(out=WDN[:, j, :].rearrange("p (a b) -> p a b", b=512),
                                                         in_=w_down[l, j * 128:(j + 1) * 128, :].rearrange("p (a b) -> p a b", b=512)),
                      writes=[f"wdn{j % 4}"], slot=f"wdn{j % 4}")
            NPASS = max(1, NT // 8)
            TP = NT // NPASS
            for pa in range(NPASS):
                t0 = pa * TP
                for j in range(22):
                    for part in range(2):
                        fc = part * 22 + j
                        col = part * DFF + j * 128
                        wbuf = wup_t[fc % 3]
                        wn = f"wup{fc % 3}"
                        T.dma("pool", lambda e, col=col, wbuf=wbuf: e.dma_start(out=wbuf[:], in_=w_up[l, :, col:col + 128].rearrange("(k p) n -> p k n", p=128)),
                              writes=[wn], slot=wn)
                        ub = upsb[part]
                        nbank = (TP + 3) // 4
                        for hb_ in range(nbank):
                            nt_ = min(4, TP - hb_ * 4)
                            cs = slice((t0 + hb_ * 4) * 128, (t0 + hb_ * 4 + nt_) * 128)
                            pbk = (fc * 2 + hb_) % 4
                            def fu(e, cs=cs, pbk=pbk, wbuf=wbuf, nt_=nt_):
                                r = None
                                for kc in range(8):
                                    r = e.matmul(out=ps[pbk][:, 0:nt_ * 128], lhsT=wbuf[:, kc, :], rhs=H2T[:, kc, cs], start=(kc == 0), stop=(kc == 7))
                                return r
                            T.op("pe", fu, reads=["H2T", wn], writes=[f"ps{pbk}"])
                            T.op("act", lambda e, pbk=pbk, ub=ub, hb_=hb_, nt_=nt_: e.activation(
                                out=ub[:, hb_ * 4:hb_ * 4 + nt_, 2:130], in_=ps[pbk][:, 0:nt_ * 128].rearrange("p (t n) -> p t n", n=128), func=AF.Copy),
                                reads=[f"ps{pbk}"], writes=[f"upsb{part}"])
                        def ft(e, wbuf=wbuf, t0=t0):
                            r = None
                            for kc in range(8):
                                r = e.matmul(out=ps[4][:, 0:TP * 2].rearrange("p (t n) -> p t n", n=2), lhsT=wbuf[:, kc, :], rhs=halT[:, kc, t0:t0 + TP, :], start=(kc == 0), stop=(kc == 7))
                            return r
                        T.op("pe", ft, reads=["halT", wn], writes=["ps4"])
                        T.op("act", lambda e, ub=ub: e.activation(out=ub[:, 0:TP, 0:2], in_=ps[4][:, 0:TP * 2].rearrange("p (t n) -> p t n", n=2), func=AF.Copy),
                             reads=["ps4"], writes=[f"upsb{part}"])
                        ac = acc[part]
                        T.op("dve", lambda e, ub=ub, ac=ac, fc=fc: e.tensor_scalar(out=ac[:, 0:TP, :], in0=ub[:, 0:TP, 2:130], scalar1=fcw[:, fc, 2:3], scalar2=fcw[:, fc, 3:4], op0=ALU.mult, op1=ALU.add),
                             reads=[f"upsb{part}", "fcw"], writes=[f"acc{part}"])
                        for k in range(2):
                            T.op("dve", lambda e, ub=ub, ac=ac, fc=fc, k=k: e.scalar_tensor_tensor(out=ac[:, 0:TP, :], in0=ub[:, 0:TP, k:128 + k], scalar=fcw[:, fc, k:k + 1], in1=ac[:, 0:TP, :], op0=ALU.mult, op1=ALU.add),
                                 reads=[f"upsb{part}", "fcw", f"acc{part}"], writes=[f"acc{part}"])
                        if part == 0:
                            T.op("act", lambda e, ac=ac: e.activation(out=gact[:, 0:TP, :], in_=ac[:, 0:TP, :], func=AF.Gelu_apprx_tanh), reads=["acc0"], writes=["gact"])
                        else:
                            T.op("pool", lambda e, ac=ac, j=j: e.tensor_tensor(out=GT[:, j, 0:TP * 128].rearrange("p (t n) -> p t n", n=128), in0=gact[:, 0:TP, :], in1=ac[:, 0:TP, :], op=ALU.mult),
                                 reads=["gact", "acc1"], writes=["GT"])
                for tt in range(TP):
                    m = t0 + tt
                    b = m % 2
                    T.dma("sp", lambda e, m=m, b=b: e.dma_start(out=xt[b][:], in_=xsrc[m]), reads=[f"xmid{m}"], writes=[f"xt{b}"], slot=f"xt{b}")
                    for nh in range(2):
                        def fd(e, nh=nh, tt=tt):
                            r = None
                            for j in range(22):
                                r = e.matmul(out=ps[5 + nh][:], lhsT=GT[:, j, tt * 128:(tt + 1) * 128], rhs=WDN[:, j, nh * 512:(nh + 1) * 512], start=(j == 0), stop=(j == 21))
                            return r
                        T.op("pe", fd, reads=["GT", "wdn0", "wdn1", "wdn2", "wdn3"], writes=[f"ps{5 + nh}"])
                        T.op("dve", lambda e, nh=nh, b=b: e.tensor_copy(out=w1[b][:, nh * 512:(nh + 1) * 512], in_=ps[5 + nh][:]), reads=[f"ps{5 + nh}"], writes=[f"w1_{b}"])
                    r, rn = rstd_from(w1[b][:], f"w1_{b}", D)
                    T.op("dve", lambda e, b=b, r=r: e.scalar_tensor_tensor(out=w1[b][:], in0=w1[b][:], scalar=r, in1=gv[:, 0, :], op0=ALU.mult, op1=ALU.mult),
                         reads=[f"w1_{b}", rn, "gv0"], writes=[f"w1_{b}"])
                    T.op("dve", lambda e, b=b: e.tensor_tensor(out=xt[b][:], in0=xt[b][:], in1=w1[b][:], op=ALU.add), reads=[f"xt{b}", f"w1_{b}"], writes=[f"xt{b}"])
                    T.dma("sp", lambda e, m=m, b=b: e.dma_start(out=xdst[m], in_=xt[b][:]), reads=[f"xt{b}"], writes=[f"xo{m}"], slot=f"xo{b}")

        wup_t = [sb(f"wup{i}", [128, 8, 128], BF16) for i in range(3)]

        def dump(which):
            T.barrier()
            if which in ("A0", "A1"):
                T.dma("sp", lambda e: e.dma_start(out=dbg["q"], in_=QT[:]), reads=["QT"], writes=["dq"], slot="dbg0")
                T.dma("sp", lambda e: e.dma_start(out=dbg["ct"], in_=CT[:]), reads=["CT"], writes=["dc"], slot="dbg1")
                for i in range(2):
                    T.dma("sp", lambda e, i=i: e.dma_start(out=dbg["gl"].rearrange("p i (t n) -> p i t n", n=128)[:, i], in_=GL[:, i, :, 32:160]), reads=["GL"], writes=[f"dg{i}"], slot=f"dbg2{i}")
                T.dma("sp", lambda e: e.dma_start(out=dbg["kg"], in_=Kg), reads=["Kg"], writes=["dk"], slot="dbg3")
                T.dma("sp", lambda e: e.dma_start(out=dbg["vg"], in_=Vg), reads=["Vg"], writes=["dv"], slot="dbg4")
            if which in ("B0", "B1"):
                T.dma("sp", lambda e: e.dma_start(out=dbg["ct"], in_=CT[:]), reads=["CT"], writes=["dc"], slot="dbg1")
                T.dma("sp", lambda e: e.dma_start(out=dbg["h2"], in_=H2T[:]), reads=["H2T"], writes=["dh"], slot="dbg2")
                for m in range(NT):
                    T.dma("sp", lambda e, m=m: e.dma_start(out=y_out[m], in_=xbuf[0][m]), reads=[f"xmid{m}"], writes=[f"y{m}"], slot=f"dy{m % 4}")
            T.barrier()

        done = False
        for l in range(DEPTH):
            xsrc = x_in if l == 0 else xbuf[1]
            layer_consts(l)
            phase_A(l, xsrc)
            T.barrier()
            gather(Kc, Kg, [f"Kc{g}_{h}" for g in range(NT // 4) for h in range(4)], "Kg")
            gather(Vc, Vg, [f"Vc{m}" for m in range(NT)], "Vg")
            gather(Hc, Hg, ["Hc0", "Hc1"], "Hg")
            T.barrier()
            if stop_after == f"A{l}":
                dump(stop_after); done = True; break
            if stop_after in (f"P{l}", f"Q{l}"):
                phase_B(l, xsrc, xbuf[0], upto=1 if stop_after[0] == "P" else 2)
                T.barrier()
                T.dma("sp", lambda e: e.dma_start(out=dbg["ct"], in_=CT[:]), reads=["CT"], writes=["dc"], slot="dbg1")
                T.barrier()
                done = True
                break
            phase_B(l, xsrc, xbuf[0])
            T.barrier()
            gather(Tc, Tg, ["Tc"], "Tg")
            T.barrier()
            if stop_after == f"B{l}":
                dump(stop_after); done = True; break
            phase_C(l, xbuf[0], y_out if l == DEPTH - 1 else xbuf[1])
            T.barrier()
            if stop_after == f"C{l}":
                for m in range(NT):
                    T.dma("sp", lambda e, m=m: e.dma_start(out=y_out[m], in_=xbuf[1][m]), reads=[f"xo{m}"], writes=[f"y{m}"], slot=f"dy{m % 4}")
                T.barrier()
                done = True
                break
        T.barrier()
        T.run(block)
    return nc


def host_inputs(inputs, NT=16):
    f = lambda a: np.ascontiguousarray(np.asarray(a, dtype=np.float32))
    x = f(inputs["x"])[0]
    S = x.shape[0]
    nblk = S // 128
    assert nblk == NT * 8
    xb = x.reshape(NT, 8, 128, D)
    bc = lambda a, shape: np.ascontiguousarray(np.broadcast_to(a, shape))
    gvec = np.stack([f(inputs[k]) for k in ("pre_mix_g", "post_mix_g", "pre_ffn_g", "post_ffn_g")], 1)
    gvec = bc(gvec[:, None], (DEPTH, 128, 4, D))
    gmv = np.stack([f(inputs["gm_ln_g"]), f(inputs["gm_ln_b"])], 1)
    gmv = bc(gmv[:, None], (DEPTH, 128, 2, 256))
    gm_wT = np.ascontiguousarray(f(inputs["gm_w_s"]).transpose(0, 3, 1, 2))
    gm_bs = np.ascontiguousarray(np.repeat(f(inputs["gm_b_s"]).transpose(0, 2, 1), 64, axis=2))
    tri = (np.arange(128)[:, None] <= np.arange(128)[None, :]).astype(np.float32)
    lam_in = np.stack([f(inputs[k]) for k in ("da_lq1", "da_lk1", "da_lq2", "da_lk2")], 1)
    lam_in = bc(lam_in[:, None], (DEPTH, 128, 4, 64))
    subln = bc(f(inputs["da_subln_g"])[:, None], (DEPTH, 128, 128))
    rb = f(inputs["rel_bias"])
    cfar = bc(rb[31][None], (128, 4))
    cvw = np.ascontiguousarray(f(inputs["cv_dw_w"]).reshape(DEPTH, 31, 2, 128).transpose(0, 3, 2, 1))
    cvp = np.stack([f(inputs[k]).reshape(DEPTH, 2, 128) for k in ("cv_dw_b", "cv_ln_g", "cv_ln_b")], -1)
    cvp = np.ascontiguousarray(cvp.transpose(0, 2, 1, 3))
    gavg = np.zeros((128, 128), np.float32)
    gavg[:64, :64] = 1.0 / 64
    gavg[64:, 64:] = 1.0 / 64
    fw = f(inputs["ffn_conv_w"]).reshape(DEPTH, 3, NFC, 128)
    fb = f(inputs["ffn_conv_b"]).reshape(DEPTH, 1, NFC, 128)
    fcw = np.ascontiguousarray(np.concatenate([fw, fb], 1).transpose(0, 3, 2, 1))
    ident = np.eye(128, dtype=np.float32)
    common = dict(w_in=f(inputs["w_in"]), w_out=f(inputs["w_out"]), ffn_w_up=f(inputs["ffn_w_up"]), ffn_w_down=f(inputs["ffn_w_down"]),
                  gvec=gvec, gmv=gmv, gm_wT=gm_wT, gm_bs=gm_bs, tri=tri, lam_in=lam_in, subln=subln, cfar=cfar,
                  cvw=cvw, cvp=cvp, gavg=gavg, fcw=fcw, ident=ident)
    k = np.arange(128)[:, None, None]
    n = np.arange(12)[None, :, None]
    q = np.arange(128)[None, None, :]
    maps = []
    for c in range(NCORES):
        rel = (c + 4 - n) * 128 + q - k
        idx = t5_bucket(rel)
        tab = rb[idx]
        tab = np.where((rel >= 0)[..., None], tab, np.float32(NEG)).astype(np.float32)
        btab = np.ascontiguousarray(tab.transpose(3, 0, 1, 2))
        sel = np.zeros((128, 8), np.float32)
        sel[:, c] = 1.0
        d = dict(common)
        d.update(x=np.ascontiguousarray(xb[:, c]), btab=btab, sel=sel)
        maps.append(d)
    return maps


_CACHE = {}


def kernel(**inputs):
    NT = 16
    if "nc" not in _CACHE:
        _CACHE["nc"] = build(NT)
    nc = _CACHE["nc"]
    maps = host_inputs(inputs, NT)
    res = run_bass_kernel_spmd(nc, maps, core_ids=list(range(NCORES)))
    out = np.zeros((NT, 8, 128, D), np.float32)
    for c in range(NCORES):
        out[:, c] = np.asarray(res.results[c]["y"])
    return out.reshape(1, NT * 8 * 128, D)
```

```python
import math
from contextlib import ExitStack
import numpy as np
import concourse.bass as bass
import concourse.mybir as mybir
from concourse.bass_utils import run_bass_kernel_spmd

F32, BF16 = mybir.dt.float32, mybir.dt.bfloat16
AF = mybir.ActivationFunctionType
ALU = mybir.AluOpType
NCORES = 8
D = 1024
INW = 2560
DFF = 2816
NFC = 44
EPS = 1e-6
NEG = -30000.0
DEPTH = 2
ENG = ("pe", "act", "dve", "pool", "sp")


class Tracker:
    def __init__(self, nc, stack):
        self.nc, self.stack = nc, stack
        self.ops = {e: [] for e in ENG}
        self.sem, self.cnt = {}, {}
        self.waited = {e: {} for e in ENG}
        self.lastw, self.readers = {}, {}
        for e in ENG:
            self._mk("E" + e)

    def _mk(self, name):
        if name not in self.sem:
            self.sem[name] = self.stack.enter_context(self.nc.semaphore(name))
            self.cnt[name] = 0
        return name

    def _deps(self, reads, writes):
        deps = {}
        def add(tok):
            if tok is not None:
                deps[tok[0]] = max(deps.get(tok[0], 0), tok[1])
        for b in reads:
            add(self.lastw.get(b))
        for b in writes:
            add(self.lastw.get(b))
            for s, v in self.readers.get(b, {}).items():
                add((s, v))
        return deps

    def _waits(self, eng, deps):
        w = []
        for s, v in deps.items():
            if eng == "pe" and s == "Epe":
                continue
            if self.waited[eng].get(s, 0) < v:
                self.waited[eng][s] = v
                w.append((self.sem[s], v))
        return w

    def _commit(self, tok, reads, writes):
        for b in reads:
            self.readers.setdefault(b, {})[tok[0]] = tok[1]
        for b in writes:
            self.lastw[b] = tok
            self.readers[b] = {}

    def op(self, eng, fn, reads=(), writes=()):
        w = self._waits(eng, self._deps(reads, writes))
        s = "E" + eng
        self.cnt[s] += 1
        tok = (s, self.cnt[s])
        sem = self.sem[s]
        def emit(e, fn=fn, w=w, sem=sem):
            for sh, v in w:
                e.wait_ge(sh, v)
            fn(e).then_inc(sem, 1)
        self.ops[eng].append(emit)
        self._commit(tok, reads, writes)

    def dma(self, eng, fn, reads=(), writes=(), slot=None, inc=16):
        w = self._waits(eng, self._deps(reads, writes))
        s = self._mk("D" + slot)
        self.cnt[s] += inc
        tok = (s, self.cnt[s])
        sem = self.sem[s]
        def emit(e, fn=fn, w=w, sem=sem, inc=inc):
            for sh, v in w:
                e.wait_ge(sh, v)
            fn(e).then_inc(sem, inc)
        self.ops[eng].append(emit)
        self._commit(tok, reads, writes)

    def barrier(self):
        allv = {s: v for s, v in self.cnt.items() if v > 0}
        for eng in ENG:
            w = self._waits(eng, dict(allv))
            def emit(e, w=w):
                for sh, v in w:
                    e.wait_ge(sh, v)
            self.ops[eng].append(emit)

    def run(self, block):
        for name, meth in (("pe", block.tensor), ("act", block.scalar), ("dve", block.vector),
                           ("pool", block.gpsimd), ("sp", block.sync)):
            ops = self.ops[name]
            def body(e, ops=ops):
                for o in ops:
                    o(e)
            meth(body)


def t5_bucket(n):
    n = np.maximum(n, 0)
    nf = np.maximum(n, 1).astype(np.float32)
    large = 16 + (np.log(nf / np.float32(16)) / np.float32(math.log(128 / 16)) * np.float32(16)).astype(np.int32)
    large = np.minimum(large, 31)
    return np.where(n < 16, n, large)


def build(NT=16, stop_after=None):
    S_LOC = NT * 128
    NBLK = NT * 8
    nc = bass.Bass("TRN2", target_bir_lowering=False)
    dt_in = lambda name, shape, dt=F32: nc.dram_tensor(name, list(shape), dt, kind="ExternalInput").ap()
    x_in = dt_in("x", [NT, 128, D])
    w_in = dt_in("w_in", [DEPTH, D, INW])
    w_out = dt_in("w_out", [DEPTH, D, D])
    wup_part = dt_in("wup_part", [11, 128, 1024])
    w_down = dt_in("ffn_w_down", [DEPTH, DFF, D])
    gvec = dt_in("gvec", [DEPTH, 128, 4, D])
    gmv = dt_in("gmv", [DEPTH, 128, 2, 256])
    gm_wT = dt_in("gm_wT", [DEPTH, 128, 4, 128])
    gm_bs = dt_in("gm_bs", [DEPTH, 128, 256])
    tri_in = dt_in("tri", [128, 128])
    lam_in = dt_in("lam_in", [DEPTH, 128, 4, 64])
    subln_in = dt_in("subln", [DEPTH, 128, 128])
    btab_in = dt_in("btab", [4, 128, 12, 128])
    cfar_in = dt_in("cfar", [128, 4])
    cvw_in = dt_in("cvw", [DEPTH, 128, 2, 31])
    cvp_in = dt_in("cvp", [DEPTH, 128, 2, 3])
    gavg_in = dt_in("gavg", [128, 128])
    fcw_in = dt_in("fcw", [DEPTH, 128, NFC, 4])
    sel_in = dt_in("sel", [128, 8])
    ident_in = dt_in("ident", [128, 128])
    y_out = nc.dram_tensor("y", [NT, 128, D], F32, kind="ExternalOutput").ap()
    dbg = {}
    if stop_after is not None:
        dbg["q"] = nc.dram_tensor("dbg_q", [128, 4, S_LOC], BF16, kind="ExternalOutput").ap()
        dbg["ct"] = nc.dram_tensor("dbg_ct", [128, 8, S_LOC], BF16, kind="ExternalOutput").ap()
        dbg["gl"] = nc.dram_tensor("dbg_gl", [128, 2, S_LOC], F32, kind="ExternalOutput").ap()
        dbg["kg"] = nc.dram_tensor("dbg_kg", [NCORES * 512, S_LOC], BF16, kind="ExternalOutput").ap()
        dbg["vg"] = nc.dram_tensor("dbg_vg", [NCORES * S_LOC, 512], BF16, kind="ExternalOutput").ap()
        dbg["h2"] = nc.dram_tensor("dbg_h2", [128, 8, S_LOC], BF16, kind="ExternalOutput").ap()

    xbuf = [nc.dram_tensor(f"xbuf{i}", [NT, 128, D], F32).ap() for i in range(2)]
    Kc = nc.dram_tensor("Kc", [512, S_LOC], BF16).ap()
    Vc = nc.dram_tensor("Vc", [S_LOC, 512], BF16).ap()
    Hc = nc.dram_tensor("Hc", [128, 2 * NT * 32], F32).ap()
    Tc = nc.dram_tensor("Tc", [128, 8 * NT * 2], BF16).ap()
    Wc = nc.dram_tensor("Wc", [11 * 128, 1024], BF16).ap()
    Wg = nc.dram_tensor("Wg", [NCORES * 11 * 128, 1024], BF16, addr_space="Shared").ap()
    Kg = nc.dram_tensor("Kg", [NCORES * 512, S_LOC], BF16, addr_space="Shared").ap()
    Vg = nc.dram_tensor("Vg", [NCORES * S_LOC, 512], BF16, addr_space="Shared").ap()
    Hg = nc.dram_tensor("Hg", [NCORES * 128, 2 * NT * 32], F32, addr_space="Shared").ap()
    Tg = nc.dram_tensor("Tg", [NCORES * 128, 8 * NT * 2], BF16, addr_space="Shared").ap()

    with ExitStack() as st:
        sb = lambda name, shape, dt=F32: st.enter_context(nc.sbuf_tensor("s_" + name, list(shape), dt))
        psS = [st.enter_context(nc.psum_tensor(f"psS{i}", [128, 1024], F32)) for i in range(2)]
        ps = [psS[0][:, 0:512], psS[0][:, 512:1024], psS[1][:, 0:512], psS[1][:, 512:1024]] + \
             [st.enter_context(nc.psum_tensor(f"ps{i}", [128, 512], F32)) for i in range(4, 7)]
        psT = st.enter_context(nc.psum_tensor("psT", [128, 1024], BF16))
        ARENA = sb("ARENA", [128, 63488], BF16)
        ident = sb("ident", [128, 128], BF16)
        identf = sb("identf", [128, 128], F32)
        gv = sb("gv", [128, 2, D], F32)
        wT = sb("wT", [128, 4, 128], BF16)
        wTf = sb("wTf", [128, 4, 128], F32)
        tri = sb("tri", [128, 128], F32)
        gmvt = sb("gmvt", [128, 2, 256], F32)
        gbs = sb("gbs", [128, 256], F32)
        lamt = sb("lamt", [128, 4, 64], F32)
        lamw = sb("lamw", [128, 8], F32)
        gsub = sb("gsub", [128, 128], F32)
        cfar = sb("cfar", [128, 4], F32)
        btab = sb("btab", [128, 12, 128], F32)
        cvw = sb("cvw", [128, 2, 31], F32)
        cvp = sb("cvp", [128, 2, 3], F32)
        gavg = sb("gavg", [128, 128], F32)
        fcw = sb("fcw", [128, NFC, 4], F32)
        sel = sb("sel", [128, 8], F32)
        stt = sb("stt", [128, 64], F32)
        bnst = sb("bnst", [128, 2, 8], F32)
        xt = [sb(f"xt{i}", [128, D], F32) for i in range(2)]
        junk = sb("junk", [128, D], BF16)
        hb = [sb(f"hb{i}", [128, D], BF16) for i in range(2)]
        w1 = [sb(f"w1_{i}", [128, D], F32) for i in range(2)]
        w2all = sb("w2all", [128, 3, 512], F32)
        w2 = [w2all[:, i, :] for i in range(3)]
        wball = sb("wball", [128, 2, 1024], BF16)
        wb = [wball[:, 0, 0:512], wball[:, 0, 512:1024], wball[:, 1, 0:512], wball[:, 1, 512:1024]]
        candT = sb("candT", [128, 8, 8 * NT * 2], BF16)
        halT = sb("halT", [128, 8, NT, 2], BF16)
        halTf = sb("halTf", [128, 8 * NT * 2], F32)
        tailT = sb("tailT", [128, 8, NT, 2], BF16)
        upsb = [sb(f"upsb{i}", [128, 8, 130], F32) for i in range(2)]
        acc1 = sb("acc1", [128, 8, 128], F32)
        acc = [w2all[:, 0:2, :].rearrange("p a (t n) -> p (a t) n", n=128), acc1[:]]
        gact = btab[:, 0:8, :]
        cvq = w2[2]

        block = st.enter_context(nc.Block())
        T = Tracker(nc, st)

        WIN = ARENA[:, 0:8 * INW].rearrange("p (k n) -> p k n", k=8)
        HT = [ARENA[:, 20480 + i * 4096: 20480 + (i + 1) * 4096].rearrange("p (k n) -> p k n", k=8) for i in range(2)]
        GL = ARENA[:, 28672:38912].bitcast(F32).rearrange("p (i t n) -> p i t n", i=2, n=160)[:, :, 0:NT, :]
        CT = ARENA[:, 38912:55296].rearrange("p (k n) -> p k n", k=8)[:, :, 0:S_LOC]
        QT = ARENA[:, 55296:63488].rearrange("p (k n) -> p k n", k=4)[:, :, 0:S_LOC]
        cand = ARENA[:, 0:16384].bitcast(F32).rearrange("p (r n) -> p r n", r=8)[:, :, 0:2 * NT * 32]
        cvy = ARENA[:, 16384:24576].bitcast(F32).rearrange("p (i n) -> p i n", i=2)[:, :, 0:S_LOC]
        KT = ARENA[:, 0:NBLK * 128].rearrange("p (j n) -> p j n", n=128)
        VH = ARENA[:, 16384:16384 + NBLK * 130].rearrange("p (j n) -> p j n", n=130)
        QP = ARENA[:, 33280:37376].rearrange("p (a n) -> p a n", a=2)[:, :, 0:S_LOC]
        WOUT = ARENA[:, 16384:24576].rearrange("p (k n) -> p k n", k=8)
        H2T = ARENA[:, 0:16384].rearrange("p (k n) -> p k n", k=8)[:, :, 0:S_LOC]
        GT = ARENA[:, 16384:38912].rearrange("p (j n) -> p j n", j=22)
        WDN = ARENA[:, 38912:61440].rearrange("p (j n) -> p j n", j=22)

        ARW = [f"arenaW{k}" for k in range(8)]
        KTN = [f"KT{c}" for c in range(8)]
        VHN = [f"VH{c}" for c in range(8)]
        CANDN = ["cand00", "cand01"] + [f"cand{r}" for r in range(1, 8)]
        stat_i = [0]
        def newstat():
            stat_i[0] = (stat_i[0] + 1) % 64
            i = stat_i[0]
            return stt[:, i:i + 1], f"st{i}"

        def rstd_from(src_ap, srcname, n, eng_sq="act"):
            ss, ssn = newstat()
            T.op("act", lambda e: e.activation(out=junk[:, 0:n], in_=src_ap, func=AF.Square, accum_out=ss),
                 reads=[srcname], writes=["junk", ssn])
            return rstd_of(ss, ssn, 1.0 / n)

        def rstd_of(v, vn, scale):
            lnv, lnn = newstat()
            T.op("act", lambda e: e.activation(out=lnv, in_=v, func=AF.Ln, bias=epsT[:, 0:1], scale=scale),
                 reads=[vn], writes=[lnn])
            r, rn = newstat()
            T.op("act", lambda e: e.activation(out=r, in_=lnv, func=AF.Exp, scale=-0.5), reads=[lnn], writes=[rn])
            return r, rn

        epsT = sb("epsT", [128, 1], F32)
        T.op("dve", lambda e: e.memset(epsT[:], EPS), writes=["epsT"])

        def ld(dst, src, name, eng="sp"):
            T.dma(eng, lambda e: e.dma_start(out=dst, in_=src), writes=[name], slot="c_" + name)

        ld(identf[:], ident_in, "identf")
        T.op("dve", lambda e: e.tensor_copy(out=ident[:], in_=identf[:]), reads=["identf"], writes=["ident"])
        ld(tri[:], tri_in, "tri")
        ld(cfar[:], cfar_in, "cfar")
        ld(gavg[:], gavg_in, "gavg")
        ld(sel[:], sel_in, "sel")

        def transpose_to(dst_ap, dstname, src_tile, srcname, nchunk, psname="psT"):
            def f(e):
                r = None
                for k in range(nchunk):
                    r = e.transpose(out=psT[:, k * 128:(k + 1) * 128], in_=src_tile[:, k * 128:(k + 1) * 128], identity=ident[:])
                return r
            T.op("pe", f, reads=[srcname, "ident"], writes=[psname])
            T.op("act", lambda e: e.activation(out=dst_ap, in_=psT[:, 0:nchunk * 128].rearrange("p (k n) -> p k n", k=nchunk), func=AF.Copy),
                 reads=[psname], writes=[dstname])

        def layer_consts(l):
            ld(wTf[:], gm_wT[l], "wTf")
            for h in range(4):
                T.op("dve", lambda e, h=h: e.tensor_tensor(out=wT[:, h, :], in0=wTf[:, h, :], in1=tri[:], op=ALU.mult),
                     reads=["wTf", "tri"], writes=["wT"])
            ld(gmvt[:], gmv[l], "gmvt")
            ld(gbs[:], gm_bs[l], "gbs")
            ld(lamt[:], lam_in[l], "lamt")
            ld(gsub[:], subln_in[l], "gsub")
            ld(cvw[:], cvw_in[l], "cvw")
            ld(cvp[:], cvp_in[l], "cvp")
            ld(fcw[:], fcw_in[l], "fcw")
            lam_init = 0.8 - 0.6 * math.exp(-0.3 * l)
            for i in range(2):
                T.op("dve", lambda e, i=i: e.tensor_tensor(out=junk[:, i * 64:(i + 1) * 64], in0=lamt[:, 2 * i, :], in1=lamt[:, 2 * i + 1, :], op=ALU.mult),
                     reads=["lamt"], writes=["junk"])
                T.op("dve", lambda e, i=i: e.reduce_sum(out=lamw[:, i:i + 1], in_=junk[:, i * 64:(i + 1) * 64], axis=mybir.AxisListType.X),
                     reads=["junk"], writes=["lamw"])
            T.op("act", lambda e: e.activation(out=lamw[:, 4:6], in_=lamw[:, 0:2], func=AF.Exp), reads=["lamw"], writes=["lamw"])
            T.op("dve", lambda e: e.tensor_tensor(out=lamw[:, 6:7], in0=lamw[:, 5:6], in1=lamw[:, 4:5], op=ALU.subtract),
                 reads=["lamw"], writes=["lamw"])
            T.op("dve", lambda e: e.tensor_scalar(out=lamw[:, 2:3], in0=lamw[:, 6:7], scalar1=-lam_init, scalar2=None, op0=ALU.add),
                 reads=["lamw"], writes=["lamw"])
            T.op("dve", lambda e: e.tensor_scalar(out=gsub[:], in0=gsub[:], scalar1=1.0 - lam_init, scalar2=None, op0=ALU.mult),
                 reads=["gsub"], writes=["gsub"])

        def phase_A(l, xsrc):
            ld(gv[:, 0, :], gvec[l][:, 0, :], "gv0")
            for kc in range(8):
                T.dma("pool", lambda e, kc=kc: e.dma_start(
                    out=WIN[:, kc, :].rearrange("p (a b) -> p a b", b=512),
                    in_=w_in[l, kc * 128:(kc + 1) * 128, :].rearrange("p (a b) -> p a b", b=512)),
                    writes=[f"arenaW{kc}"], slot=f"win{kc}")
            for g in range(NT // 4):
                hT = HT[g % 2]
                hTn = f"hT{g % 2}"
                for t in range(4):
                    m = 4 * g + t
                    b = m % 2
                    T.dma("sp", lambda e, m=m, b=b: e.dma_start(out=xt[b][:], in_=xsrc[m]), writes=[f"xt{b}"], slot=f"xt{b}")
                    r, rn = rstd_from(xt[b][:], f"xt{b}", D)
                    T.op("dve", lambda e, b=b, r=r: e.scalar_tensor_tensor(out=hb[b][:], in0=xt[b][:], scalar=r, in1=gv[:, 0, :], op0=ALU.mult, op1=ALU.mult),
                         reads=[f"xt{b}", rn, "gv0"], writes=[f"hb{b}"])
                    transpose_to(hT[:, :, t * 128:(t + 1) * 128], hTn, hb[b], f"hb{b}", 8)
                order = [("q", h, 512 + 128 * h) for h in range(4)] + [("k", h, 1024 + 128 * h) for h in range(4)] + \
                        [("cg", i, 2304 + 128 * i) for i in range(2)] + [("ca", i, 2048 + 128 * i) for i in range(2)]
                for oi, (kind, idx, col) in enumerate(order):
                    pb = oi % 2
                    def f(e, col=col, pb=pb, hT=hT):
                        r = None
                        for kc in range(8):
                            r = e.matmul(out=ps[pb][:], lhsT=WIN[:, kc, col:col + 128], rhs=hT[:, kc, :], start=(kc == 0), stop=(kc == 7))
                        return r
                    T.op("pe", f, reads=[hTn] + ARW, writes=[f"ps{pb}"])
                    tok = slice(g * 512, (g + 1) * 512)
                    if kind == "q":
                        T.op("act", lambda e, pb=pb, idx=idx, tok=tok: e.activation(out=QT[:, idx, tok], in_=ps[pb][:], func=AF.Copy),
                             reads=[f"ps{pb}"], writes=["QT"])
                    elif kind == "k":
                        wi = idx % 3
                        T.op("dve", lambda e, pb=pb, wi=wi: e.tensor_copy(out=wb[wi][:], in_=ps[pb][:]), reads=[f"ps{pb}"], writes=[f"wb{wi}"])
                        T.dma("sp", lambda e, wi=wi, idx=idx, tok=tok: e.dma_start(out=Kc[idx * 128:(idx + 1) * 128, tok], in_=wb[wi][:]),
                              reads=[f"wb{wi}"], writes=[f"Kc{g}_{idx}"], slot=f"wb{wi}")
                    elif kind == "cg":
                        T.op("act", lambda e, pb=pb, idx=idx: e.activation(out=w2[idx][:], in_=ps[pb][:], func=AF.Sigmoid),
                             reads=[f"ps{pb}"], writes=[f"w2{idx}"])
                    else:
                        T.op("dve", lambda e, pb=pb, idx=idx, g=g: e.tensor_tensor(
                            out=GL[:, idx, 4 * g:4 * g + 4, 32:160], in0=ps[pb][:].rearrange("p (t n) -> p t n", t=4),
                            in1=w2[idx][:].rearrange("p (t n) -> p t n", t=4), op=ALU.mult),
                            reads=[f"ps{pb}", f"w2{idx}"], writes=["GL"])
                for t in range(4):
                    m = 4 * g + t
                    tcols = slice(t * 128, (t + 1) * 128)
                    def fg(e, tcols=tcols, hT=hT):
                        r = None
                        for kc in range(8):
                            r = e.matmul(out=ps[2][:], lhsT=hT[:, kc, tcols], rhs=WIN[:, kc, 0:512], start=(kc == 0), stop=(kc == 7))
                        return r
                    T.op("pe", fg, reads=[hTn] + ARW, writes=["ps2"])
                    def fv(e, tcols=tcols, hT=hT):
                        r = None
                        for kc in range(8):
                            r = e.matmul(out=ps[3][:], lhsT=hT[:, kc, tcols], rhs=WIN[:, kc, 1536:2048], start=(kc == 0), stop=(kc == 7))
                        return r
                    T.op("pe", fv, reads=[hTn] + ARW, writes=["ps3"])
                    T.op("dve", lambda e: e.tensor_copy(out=wb[2][:], in_=ps[3][:]), reads=["ps3"], writes=["wb2"])
                    T.dma("sp", lambda e, m=m: e.dma_start(out=Vc[m * 128:(m + 1) * 128, :], in_=wb[2][:]),
                          reads=["wb2"], writes=[f"Vc{m}"], slot="wb2")
                    ug = w1[0]
                    T.op("act", lambda e: e.activation(out=ug[:, 0:512], in_=ps[2][:], func=AF.Gelu_apprx_tanh), reads=["ps2"], writes=["w1_0"])
                    T.op("dve", lambda e: e.bn_stats(out=bnst[:, 0, 0:6], in_=ug[:, 256:512]), reads=["w1_0"], writes=["bnst0"])
                    T.op("dve", lambda e: e.bn_aggr(out=bnst[:, 1, 0:2], in_=bnst[:, 0, 0:6]), reads=["bnst0"], writes=["bnst1"])
                    r, rn = rstd_of(bnst[:, 1, 1:2], "bnst1", 1.0)
                    vn = w1[0][:, 512:768]
                    T.op("dve", lambda e, r=r: e.tensor_scalar(out=vn, in0=ug[:, 256:512], scalar1=bnst[:, 1, 0:1], scalar2=r, op0=ALU.subtract, op1=ALU.mult),
                         reads=["w1_0", "bnst1", rn], writes=["w1_0b"])
                    T.op("dve", lambda e: e.tensor_tensor(out=vn, in0=vn, in1=gmvt[:, 0, :], op=ALU.mult), reads=["w1_0b", "gmvt"], writes=["w1_0b"])
                    T.op("dve", lambda e: e.tensor_tensor(out=hb[0][:, 0:256], in0=vn, in1=gmvt[:, 1, :], op=ALU.add), reads=["w1_0b", "gmvt"], writes=["hb0"])
                    def fs(e):
                        r = None
                        for h in range(4):
                            r = e.matmul(out=ps[4][:, h * 64:(h + 1) * 64], lhsT=wT[:, h, :], rhs=hb[0][:, h * 64:(h + 1) * 64], start=True, stop=True)
                        return r
                    T.op("pe", fs, reads=["hb0", "wT"], writes=["ps4"])
                    T.op("dve", lambda e: e.tensor_tensor(out=w1[0][:, 768:1024], in0=ps[4][:, 0:256], in1=gbs[:], op=ALU.add),
                         reads=["ps4", "gbs"], writes=["w1_0c"])
                    T.op("dve", lambda e: e.tensor_tensor(out=hb[1][:, 0:256], in0=w1[0][:, 768:1024], in1=ug[:, 0:256], op=ALU.mult),
                         reads=["w1_0c", "w1_0"], writes=["hb1"])
                    transpose_to(CT[:, 0:2, m * 128:(m + 1) * 128], "CT", hb[1], "hb1", 2)
            for i in range(2):
                T.dma("sp", lambda e, i=i: e.dma_start(out=Hc.rearrange("p (i t n) -> p i t n", i=2, t=NT)[:, i], in_=GL[:, i, :, 128:160]),
                      reads=["GL"], writes=[f"Hc{i}"], slot=f"hc{i}")

        def gather(src, dst, reads, name):
            T.dma("pool", lambda e: e.collective_compute("AllGather", ALU.bypass, replica_groups=[list(range(NCORES))], ins=[src], outs=[dst]),
                  reads=reads, writes=[name], slot="cc_" + name, inc=1)

        def halo_select(dst_f32, dstname, cand_t, candname, width):
            T.op("dve", lambda e: e.tensor_scalar(out=dst_f32, in0=cand_t[:, 0, :], scalar1=sel[:, 0:1], scalar2=None, op0=ALU.mult),
                 reads=candname + ["sel"], writes=[dstname])
            for r_ in range(1, 8):
                T.op("dve", lambda e, r_=r_: e.scalar_tensor_tensor(out=dst_f32, in0=cand_t[:, r_, :], scalar=sel[:, r_:r_ + 1], in1=dst_f32, op0=ALU.mult, op1=ALU.add),
                     reads=candname + ["sel", dstname], writes=[dstname])

        def phase_B(l, xsrc, xdst, upto=3):
            kg_names = [f"Kg"]
            T.op("pool", lambda e: e.memset(cand[:, 0, :], 0.0), writes=["cand00", "cand01"])
            cview = cand[:].rearrange("p r (i t n) -> p r i t n", i=2, t=NT)
            hgv = Hg.rearrange("(r p) (i t n) -> r p i t n", p=128, i=2, t=NT)
            if NT > 1:
                for i in range(2):
                    T.dma("sp", lambda e, i=i: e.dma_start(out=cview[:, 0, i, 1:NT, :], in_=hgv[7, :, i, 0:NT - 1, :]), reads=["Hg"], writes=[f"cand0{i}"], slot=f"cand0{i}")
            for r_ in range(1, 8):
                T.dma("sp", lambda e, r_=r_: e.dma_start(out=cand[:, r_, :], in_=Hg[(r_ - 1) * 128:r_ * 128, :]), reads=["Hg"], writes=[f"cand{r_}"], slot=f"cand{r_}")
            hsel = cvy[:, 0, 0:2 * NT * 32]
            halo_select(hsel, "cvy", cand, CANDN, 2 * NT * 32)
            T.op("dve", lambda e: e.tensor_copy(out=GL[:, :, :, 0:32], in_=hsel.rearrange("p (i t n) -> p i t n", i=2, t=NT)),
                 reads=["cvy"], writes=["GL"])
            for i in range(2):
                ceng = "dve"
                T.op(ceng, lambda e, i=i: e.tensor_scalar(out=cvy[:, i, :].rearrange("p (t n) -> p t n", t=NT), in0=GL[:, i, :, 2:130],
                                                           scalar1=cvw[:, i, 0:1], scalar2=cvp[:, i, 0:1], op0=ALU.mult, op1=ALU.add),
                     reads=["GL", "cvw", "cvp"], writes=[f"cvy{i}"])
                for k in range(1, 31):
                    T.op(ceng, lambda e, i=i, k=k: e.scalar_tensor_tensor(
                        out=cvy[:, i, :].rearrange("p (t n) -> p t n", t=NT), in0=GL[:, i, :, 2 + k:130 + k], scalar=cvw[:, i, k:k + 1],
                        in1=cvy[:, i, :].rearrange("p (t n) -> p t n", t=NT), op0=ALU.mult, op1=ALU.add),
                        reads=["GL", "cvw", f"cvy{i}"], writes=[f"cvy{i}"])
            for i in range(2):
                for q4 in range(S_LOC // 512):
                    cs = slice(q4 * 512, (q4 + 1) * 512)
                    T.op("pe", lambda e, i=i, cs=cs: e.matmul(out=ps[0][:], lhsT=gavg[:], rhs=cvy[:, i, cs], start=True, stop=True),
                         reads=[f"cvy{i}", "gavg"], writes=["ps0"])
                    T.op("act", lambda e, i=i, cs=cs: e.activation(out=cvq[:], in_=cvy[:, i, cs], func=AF.Square), reads=[f"cvy{i}"], writes=["cvq"])
                    T.op("pe", lambda e: e.matmul(out=ps[1][:], lhsT=gavg[:], rhs=cvq[:], start=True, stop=True), reads=["cvq", "gavg"], writes=["ps1"])
                    T.op("act", lambda e: e.activation(out=w2[0][:], in_=ps[0][:], func=AF.Square), reads=["ps0"], writes=["w20"])
                    T.op("dve", lambda e: e.tensor_tensor(out=w2[0][:], in0=ps[1][:], in1=w2[0][:], op=ALU.subtract), reads=["ps1", "w20"], writes=["w20"])
                    T.op("act", lambda e: e.activation(out=w2[0][:], in_=w2[0][:], func=AF.Ln, bias=epsT[:, 0:1]), reads=["w20"], writes=["w20"])
                    T.op("act", lambda e: e.activation(out=w2[0][:], in_=w2[0][:], func=AF.Exp, scale=-0.5), reads=["w20"], writes=["w20"])
                    T.op("dve", lambda e, i=i, cs=cs: e.tensor_tensor(out=w2[1][:], in0=cvy[:, i, cs], in1=ps[0][:], op=ALU.subtract), reads=[f"cvy{i}", "ps0"], writes=["w21"])
                    T.op("dve", lambda e: e.tensor_tensor(out=w2[1][:], in0=w2[1][:], in1=w2[0][:], op=ALU.mult), reads=["w21", "w20"], writes=["w21"])
                    T.op("act", lambda e, i=i, cs=cs: e.activation(out=CT[:, 6 + i, cs], in_=w2[1][:], func=AF.Silu, bias=cvp[:, i, 2:3], scale=cvp[:, i, 1:2]),
                         reads=["w21", "cvp"], writes=["CT"])
            T.barrier()
            if upto == 1:
                return
            T.op("pool", lambda e: e.memset(QP[:, :, :], 0.0), writes=["QP"])
            for h in range(4):
                for mp in range(2):
                    T.op("pool", lambda e, mp=mp, h=h: e.tensor_copy(out=QP[mp * 64:(mp + 1) * 64, mp, :], in_=QT[mp * 64:(mp + 1) * 64, h, :]),
                         reads=["QT", "QP"], writes=["QP"])
                for c_ in range(8):
                    T.dma("sp", lambda e, c_=c_, h=h: e.dma_start(
                        out=KT[:, :, :].rearrange("p (m c) n -> p m c n", c=8)[:, :, c_, :],
                        in_=Kg[c_ * 512 + h * 128:c_ * 512 + (h + 1) * 128, :].rearrange("p (m n) -> p m n", n=128)),
                        reads=["Kg"], writes=[f"KT{c_}"], slot=f"kt{c_}")
                    T.dma("sp", lambda e, c_=c_, h=h: e.dma_start(
                        out=VH[:, :, 0:128].rearrange("p (m c) n -> p m c n", c=8)[:, :, c_, :],
                        in_=Vg[c_ * S_LOC:(c_ + 1) * S_LOC, h * 128:(h + 1) * 128].rearrange("(m p) n -> p m n", p=128)),
                        reads=["Vg"], writes=[f"VH{c_}"], slot=f"vh{c_}")
                T.op("pool", lambda e: e.memset(VH[:, :, 128:130], 1.0), writes=["VH1"])
                T.dma("sp", lambda e, h=h: e.dma_start(out=btab[:], in_=btab_in[h]), writes=["btab"], slot="btab")
                units = [(m, qd) for m in range(NT) for qd in range(2 * m + 2)]

                def score(ui, m, qd, h=h):
                    sb_ = ui % 2
                    S = psS[sb_]
                    qs = slice(m * 128, (m + 1) * 128)
                    def fsc(e):
                        r = None
                        for kb in range(4):
                            r = e.matmul(out=S[:, kb * 256:(kb + 1) * 256].rearrange("p (a n) -> p a n", a=2),
                                         lhsT=KT[:, 4 * qd + kb, :], rhs=QP[:, 0:2, qs], start=True, stop=True)
                        return r
                    T.op("pe", fsc, reads=KTN + ["QP"], writes=[f"ps{2 * sb_}", f"ps{2 * sb_ + 1}"])

                def softmax(ui, m, qd, h=h):
                    sb_ = ui % 2
                    S = psS[sb_]
                    P = wball[:, sb_, :]
                    sn = [f"ps{2 * sb_}", f"ps{2 * sb_ + 1}"]
                    pn = [f"wb{2 * sb_}", f"wb{2 * sb_ + 1}"]
                    near = (4 * qd + 3 >= 8 * m - 2)
                    if not near:
                        T.op("act", lambda e: e.activation(out=P, in_=S[:], func=AF.Exp, bias=cfar[:, h:h + 1], scale=0.125),
                             reads=sn + ["cfar"], writes=pn)
                    else:
                        n0 = 4 * qd - (8 * m - 4)
                        stg = w2all[:, 0:2, :]
                        for mp in range(2):
                            T.op("dve", lambda e, mp=mp: e.scalar_tensor_tensor(
                                out=stg.rearrange("p x (k2 a n) -> p (x k2) a n", a=2, n=128)[:, :, mp, :],
                                in0=S[:].rearrange("p (k a n) -> p k a n", k=4, a=2)[:, :, mp, :], scalar=0.125,
                                in1=btab[:, n0:n0 + 4, :], op0=ALU.mult, op1=ALU.add),
                                reads=sn + ["btab"], writes=["w20", "w21"])
                        T.op("act", lambda e: e.activation(out=P, in_=stg.rearrange("p x n -> p (x n)"), func=AF.Exp), reads=["w20", "w21"], writes=pn)

                def pv(ui, m, qd):
                    sb_ = ui % 2
                    P = wball[:, sb_, :]
                    ob = 4
                    nblk = 8 * m + 8
                    def fpv(e):
                        r = None
                        for kb in range(4):
                            j = 4 * qd + kb
                            for mp in range(2):
                                r = e.matmul(out=ps[ob + mp][:, 0:130], lhsT=P[:, kb * 256 + mp * 128: kb * 256 + (mp + 1) * 128], rhs=VH[:, j, :],
                                             start=(j == 0), stop=(j == nblk - 1))
                        return r
                    T.op("pe", fpv, reads=[f"wb{2 * sb_}", f"wb{2 * sb_ + 1}", "VH1"] + VHN, writes=[f"ps{ob}", f"ps{ob + 1}"])

                def fin1(m):
                    ob = 4
                    o1, o2 = ps[ob], ps[ob + 1]
                    o1n, o2n = f"ps{ob}", f"ps{ob + 1}"
                    r1, r1n = newstat()
                    r2, r2n = newstat()
                    T.op("dve", lambda e: e.reciprocal(out=r1, in_=o1[:, 128:129]), reads=[o1n], writes=[r1n])
                    T.op("dve", lambda e: e.reciprocal(out=r2, in_=o2[:, 128:129]), reads=[o2n], writes=[r2n])
                    T.op("dve", lambda e: e.tensor_tensor(out=r2, in0=r2, in1=lamw[:, 2:3], op=ALU.mult), reads=[r2n, "lamw"], writes=[r2n])
                    at = w1[1]
                    T.op("dve", lambda e: e.tensor_scalar(out=at[:, 0:128], in0=o2[:, 0:128], scalar1=r2, scalar2=None, op0=ALU.mult),
                         reads=[o2n, r2n], writes=["w1_1"])
                    T.op("dve", lambda e: e.scalar_tensor_tensor(out=at[:, 128:256], in0=o1[:, 0:128], scalar=r1, in1=at[:, 0:128], op0=ALU.mult, op1=ALU.add),
                         reads=[o1n, r1n, "w1_1"], writes=["w1_1b"])
                    rr, rrn = rstd_from(at[:, 128:256], "w1_1b", 128)
                    hbm = hb[m % 2]
                    T.op("dve", lambda e: e.scalar_tensor_tensor(out=hbm[:, 0:128], in0=at[:, 128:256], scalar=rr, in1=gsub[:], op0=ALU.mult, op1=ALU.mult),
                         reads=["w1_1b", rrn, "gsub"], writes=[f"hb{m % 2}"])

                def fin2(m, h=h):
                    transpose_to(CT[:, 2 + h:3 + h, m * 128:(m + 1) * 128], "CT", hb[m % 2], f"hb{m % 2}", 1)

                pending = []
                score(0, *units[0])
                for ui, (m, qd) in enumerate(units):
                    if ui + 1 < len(units):
                        score(ui + 1, *units[ui + 1])
                    softmax(ui, m, qd)
                    pv(ui, m, qd)
                    for item in list(pending):
                        item[0] -= 1
                        if item[0] <= 0:
                            fin2(item[1])
                            pending.remove(item)
                    if qd == 2 * m + 1:
                        fin1(m)
                        pending.append([3, m])
                for item in pending:
                    fin2(item[1])
            T.barrier()
            if upto == 2:
                return
            ld(gv[:, 0, :], gvec[l][:, 1, :], "gv0")
            ld(gv[:, 1, :], gvec[l][:, 2, :], "gv1")
            for kc in range(8):
                T.dma("pool", lambda e, kc=kc: e.dma_start(out=WOUT[:, kc, :].rearrange("p (a b) -> p a b", b=512),
                                                           in_=w_out[l, kc * 128:(kc + 1) * 128, :].rearrange("p (a b) -> p a b", b=512)),
                      writes=[f"arenaW{kc}"], slot=f"win{kc}")
            for m in range(NT):
                b = m % 2
                qs = slice(m * 128, (m + 1) * 128)
                T.dma("sp", lambda e, m=m, b=b: e.dma_start(out=xt[b][:], in_=xsrc[m]), writes=[f"xt{b}"], slot=f"xt{b}")
                for nh in range(2):
                    def fo(e, nh=nh, qs=qs):
                        r = None
                        for kc in range(8):
                            r = e.matmul(out=ps[5 + nh][:], lhsT=CT[:, kc, qs], rhs=WOUT[:, kc, nh * 512:(nh + 1) * 512], start=(kc == 0), stop=(kc == 7))
                        return r
                    T.op("pe", fo, reads=["CT"] + ARW, writes=[f"ps{5 + nh}"])
                    T.op("dve", lambda e, nh=nh, b=b: e.tensor_copy(out=w1[b][:, nh * 512:(nh + 1) * 512], in_=ps[5 + nh][:]),
                         reads=[f"ps{5 + nh}"], writes=[f"w1_{b}"])
                r, rn = rstd_from(w1[b][:], f"w1_{b}", D)
                T.op("dve", lambda e, b=b, r=r: e.scalar_tensor_tensor(out=w1[b][:], in0=w1[b][:], scalar=r, in1=gv[:, 0, :], op0=ALU.mult, op1=ALU.mult),
                     reads=[f"w1_{b}", rn, "gv0"], writes=[f"w1_{b}"])
                T.op("dve", lambda e, b=b: e.tensor_tensor(out=xt[b][:], in0=xt[b][:], in1=w1[b][:], op=ALU.add), reads=[f"xt{b}", f"w1_{b}"], writes=[f"xt{b}"])
                T.dma("sp", lambda e, m=m, b=b: e.dma_start(out=xdst[m], in_=xt[b][:]), reads=[f"xt{b}"], writes=[f"xmid{m}"], slot=f"xo{b}")
                r, rn = rstd_from(xt[b][:], f"xt{b}", D)
                T.op("dve", lambda e, b=b, r=r: e.scalar_tensor_tensor(out=hb[b][:], in0=xt[b][:], scalar=r, in1=gv[:, 1, :], op0=ALU.mult, op1=ALU.mult),
                     reads=[f"xt{b}", rn, "gv1"], writes=[f"hb{b}"])
                transpose_to(H2T[:, :, qs], "H2T", hb[b], f"hb{b}", 8)
            T.op("dve", lambda e: e.tensor_copy(out=tailT[:], in_=H2T[:].rearrange("p k (t n) -> p k t n", n=128)[:, :, :, 126:128]),
                 reads=["H2T"], writes=["tailT"])
            T.dma("sp", lambda e: e.dma_start(out=Tc, in_=tailT[:].rearrange("p k t n -> p (k t n)")), reads=["tailT"], writes=["Tc"], slot="tc")

        def phase_C(l, xsrc, xdst):
            ld(gv[:, 0, :], gvec[l][:, 3, :], "gv0")
            T.op("pool", lambda e: e.memset(candT[:, 0, :], 0.0), writes=["cand00", "cand01"])
            cv_ = candT[:].rearrange("p r (k t n) -> p r k t n", k=8, t=NT)
            tgv = Tg.rearrange("(r p) (k t n) -> r p k t n", p=128, k=8, t=NT)
            if NT > 1:
                for k in range(8):
                    T.dma("sp", lambda e, k=k: e.dma_start(out=cv_[:, 0, k, 1:NT, :], in_=tgv[7, :, k, 0:NT - 1, :]), reads=["Tg"], writes=[f"cand0{k % 2}"], slot=f"cand0{k % 2}")
            for r_ in range(1, 8):
                T.dma("sp", lambda e, r_=r_: e.dma_start(out=candT[:, r_, :], in_=Tg[(r_ - 1) * 128:r_ * 128, :]), reads=["Tg"], writes=[f"cand{r_}"], slot=f"cand{r_}")
            halo_select(halTf[:], "halTf", candT, CANDN, 8 * NT * 2)
            T.op("dve", lambda e: e.tensor_copy(out=halT[:].rearrange("p k t n -> p (k t n)"), in_=halTf[:]), reads=["halTf"], writes=["halT"])
            for j in range(22):
                T.dma("pool", lambda e, j=j: e.dma_start(out=WDN[:, j, :].rearrange("p (a b) -> p a b", b=512),
                                                         in_=w_down[l, j * 128:(j + 1) * 128, :].rearrange("p (a b) -> p a b", b=512)),
                      writes=[f"wdn{j % 4}"], slot=f"wdn{j % 4}")
            NPASS = max(1, NT // 8)
            TP = NT // NPASS
            for pa in range(NPASS):
                t0 = pa * TP
                for j in range(22):
                    for part in range(2):
                        fc = part * 22 + j
                        col = part * DFF + j * 128
                        wbuf = wup_t[fc % 3]
                        wn = f"wup{fc % 3}"
                        gch = l * 44 + fc
                        T.dma("sp", lambda e, gch=gch, wbuf=wbuf: e.dma_start(out=wbuf[:].rearrange("p k n -> p (k n)"), in_=Wg[gch * 128:(gch + 1) * 128, :]),
                              reads=["Wg"], writes=[wn], slot=wn)
                        ub = upsb[part]
                        nbank = (TP + 3) // 4
                        for hb_ in range(nbank):
                            nt_ = min(4, TP - hb_ * 4)
                            cs = slice((t0 + hb_ * 4) * 128, (t0 + hb_ * 4 + nt_) * 128)
                            pbk = (fc * 2 + hb_) % 4
                            def fu(e, cs=cs, pbk=pbk, wbuf=wbuf, nt_=nt_):
                                r = None
                                for kc in range(8):
                                    r = e.matmul(out=ps[pbk][:, 0:nt_ * 128], lhsT=wbuf[:, kc, :], rhs=H2T[:, kc, cs], start=(kc == 0), stop=(kc == 7))
                                return r
                            T.op("pe", fu, reads=["H2T", wn], writes=[f"ps{pbk}"])
                            T.op("act", lambda e, pbk=pbk, ub=ub, hb_=hb_, nt_=nt_: e.activation(
                                out=ub[:, hb_ * 4:hb_ * 4 + nt_, 2:130], in_=ps[pbk][:, 0:nt_ * 128].rearrange("p (t n) -> p t n", n=128), func=AF.Copy),
                                reads=[f"ps{pbk}"], writes=[f"upsb{part}"])
                        def ft(e, wbuf=wbuf, t0=t0):
                            r = None
                            for kc in range(8):
                                r = e.matmul(out=ps[4][:, 0:TP * 2].rearrange("p (t n) -> p t n", n=2), lhsT=wbuf[:, kc, :], rhs=halT[:, kc, t0:t0 + TP, :], start=(kc == 0), stop=(kc == 7))
                            return r
                        T.op("pe", ft, reads=["halT", wn], writes=["ps4"])
                        T.op("act", lambda e, ub=ub: e.activation(out=ub[:, 0:TP, 0:2], in_=ps[4][:, 0:TP * 2].rearrange("p (t n) -> p t n", n=2), func=AF.Copy),
                             reads=["ps4"], writes=[f"upsb{part}"])
                        ac = acc[part]
                        T.op("dve", lambda e, ub=ub, ac=ac, fc=fc: e.tensor_scalar(out=ac[:, 0:TP, :], in0=ub[:, 0:TP, 2:130], scalar1=fcw[:, fc, 2:3], scalar2=fcw[:, fc, 3:4], op0=ALU.mult, op1=ALU.add),
                             reads=[f"upsb{part}", "fcw"], writes=[f"acc{part}"])
                        for k in range(2):
                            T.op("dve", lambda e, ub=ub, ac=ac, fc=fc, k=k: e.scalar_tensor_tensor(out=ac[:, 0:TP, :], in0=ub[:, 0:TP, k:128 + k], scalar=fcw[:, fc, k:k + 1], in1=ac[:, 0:TP, :], op0=ALU.mult, op1=ALU.add),
                                 reads=[f"upsb{part}", "fcw", f"acc{part}"], writes=[f"acc{part}"])
                        if part == 0:
                            T.op("act", lambda e, ac=ac: e.activation(out=gact[:, 0:TP, :], in_=ac[:, 0:TP, :], func=AF.Gelu_apprx_tanh), reads=["acc0"], writes=["gact"])
                        else:
                            T.op("pool", lambda e, ac=ac, j=j: e.tensor_tensor(out=GT[:, j, 0:TP * 128].rearrange("p (t n) -> p t n", n=128), in0=gact[:, 0:TP, :], in1=ac[:, 0:TP, :], op=ALU.mult),
                                 reads=["gact", "acc1"], writes=["GT"])
                for tt in range(TP):
                    m = t0 + tt
                    b = m % 2
                    T.dma("sp", lambda e, m=m, b=b: e.dma_start(out=xt[b][:], in_=xsrc[m]), reads=[f"xmid{m}"], writes=[f"xt{b}"], slot=f"xt{b}")
                    for nh in range(2):
                        def fd(e, nh=nh, tt=tt):
                            r = None
                            for j in range(22):
                                r = e.matmul(out=ps[5 + nh][:], lhsT=GT[:, j, tt * 128:(tt + 1) * 128], rhs=WDN[:, j, nh * 512:(nh + 1) * 512], start=(j == 0), stop=(j == 21))
                            return r
                        T.op("pe", fd, reads=["GT", "wdn0", "wdn1", "wdn2", "wdn3"], writes=[f"ps{5 + nh}"])
                        T.op("dve", lambda e, nh=nh, b=b: e.tensor_copy(out=w1[b][:, nh * 512:(nh + 1) * 512], in_=ps[5 + nh][:]), reads=[f"ps{5 + nh}"], writes=[f"w1_{b}"])
                    r, rn = rstd_from(w1[b][:], f"w1_{b}", D)
                    T.op("dve", lambda e, b=b, r=r: e.scalar_tensor_tensor(out=w1[b][:], in0=w1[b][:], scalar=r, in1=gv[:, 0, :], op0=ALU.mult, op1=ALU.mult),
                         reads=[f"w1_{b}", rn, "gv0"], writes=[f"w1_{b}"])
                    T.op("dve", lambda e, b=b: e.tensor_tensor(out=xt[b][:], in0=xt[b][:], in1=w1[b][:], op=ALU.add), reads=[f"xt{b}", f"w1_{b}"], writes=[f"xt{b}"])
                    T.dma("sp", lambda e, m=m, b=b: e.dma_start(out=xdst[m], in_=xt[b][:]), reads=[f"xt{b}"], writes=[f"xo{m}"], slot=f"xo{b}")

        wup_t = [sb(f"wup{i}", [128, 8, 128], BF16) for i in range(3)]

        def dump(which):
            T.barrier()
            if which in ("A0", "A1"):
                T.dma("sp", lambda e: e.dma_start(out=dbg["q"], in_=QT[:]), reads=["QT"], writes=["dq"], slot="dbg0")
                T.dma("sp", lambda e: e.dma_start(out=dbg["ct"], in_=CT[:]), reads=["CT"], writes=["dc"], slot="dbg1")
                for i in range(2):
                    T.dma("sp", lambda e, i=i: e.dma_start(out=dbg["gl"].rearrange("p i (t n) -> p i t n", n=128)[:, i], in_=GL[:, i, :, 32:160]), reads=["GL"], writes=[f"dg{i}"], slot=f"dbg2{i}")
                T.dma("sp", lambda e: e.dma_start(out=dbg["kg"], in_=Kg), reads=["Kg"], writes=["dk"], slot="dbg3")
                T.dma("sp", lambda e: e.dma_start(out=dbg["vg"], in_=Vg), reads=["Vg"], writes=["dv"], slot="dbg4")
            if which in ("B0", "B1"):
                T.dma("sp", lambda e: e.dma_start(out=dbg["ct"], in_=CT[:]), reads=["CT"], writes=["dc"], slot="dbg1")
                T.dma("sp", lambda e: e.dma_start(out=dbg["h2"], in_=H2T[:]), reads=["H2T"], writes=["dh"], slot="dbg2")
                for m in range(NT):
                    T.dma("sp", lambda e, m=m: e.dma_start(out=y_out[m], in_=xbuf[0][m]), reads=[f"xmid{m}"], writes=[f"y{m}"], slot=f"dy{m % 4}")
            T.barrier()

        done = False
        for l in range(DEPTH):
            xsrc = x_in if l == 0 else xbuf[1]
            layer_consts(l)
            phase_A(l, xsrc)
            if l == 0:
                for i in range(11):
                    T.dma("pool", lambda e, i=i: e.dma_start(out=Wc[i * 128:(i + 1) * 128, :].rearrange("p (a b) -> p a b", b=512),
                                                           in_=wup_part[i].rearrange("p (a b) -> p a b", b=512)),
                          writes=[f"Wc{i}"], slot=f"wc{i % 4}")
                gather(Wc, Wg, [f"Wc{i}" for i in range(11)], "Wg")
            T.barrier()
            gather(Kc, Kg, [f"Kc{g}_{h}" for g in range(NT // 4) for h in range(4)], "Kg")
            gather(Vc, Vg, [f"Vc{m}" for m in range(NT)], "Vg")
            gather(Hc, Hg, ["Hc0", "Hc1"], "Hg")
            T.barrier()
            if stop_after == f"A{l}":
                dump(stop_after); done = True; break
            if stop_after in (f"P{l}", f"Q{l}"):
                phase_B(l, xsrc, xbuf[0], upto=1 if stop_after[0] == "P" else 2)
                T.barrier()
                T.dma("sp", lambda e: e.dma_start(out=dbg["ct"], in_=CT[:]), reads=["CT"], writes=["dc"], slot="dbg1")
                T.barrier()
                done = True
                break
            phase_B(l, xsrc, xbuf[0])
            T.barrier()
            gather(Tc, Tg, ["Tc"], "Tg")
            T.barrier()
            if stop_after == f"B{l}":
                dump(stop_after); done = True; break
            phase_C(l, xbuf[0], y_out if l == DEPTH - 1 else xbuf[1])
            T.barrier()
            if stop_after == f"C{l}":
                for m in range(NT):
                    T.dma("sp", lambda e, m=m: e.dma_start(out=y_out[m], in_=xbuf[1][m]), reads=[f"xo{m}"], writes=[f"y{m}"], slot=f"dy{m % 4}")
                T.barrier()
                done = True
                break
        T.barrier()
        T.run(block)
    return nc


def host_inputs(inputs, NT=16):
    f = lambda a: np.ascontiguousarray(np.asarray(a, dtype=np.float32))
    x = f(inputs["x"])[0]
    S = x.shape[0]
    nblk = S // 128
    assert nblk == NT * 8
    xb = x.reshape(NT, 8, 128, D)
    bc = lambda a, shape: np.ascontiguousarray(np.broadcast_to(a, shape))
    gvec = np.stack([f(inputs[k]) for k in ("pre_mix_g", "post_mix_g", "pre_ffn_g", "post_ffn_g")], 1)
    gvec = bc(gvec[:, None], (DEPTH, 128, 4, D))
    gmv = np.stack([f(inputs["gm_ln_g"]), f(inputs["gm_ln_b"])], 1)
    gmv = bc(gmv[:, None], (DEPTH, 128, 2, 256))
    gm_wT = np.ascontiguousarray(f(inputs["gm_w_s"]).transpose(0, 3, 1, 2))
    gm_bs = np.ascontiguousarray(np.repeat(f(inputs["gm_b_s"]).transpose(0, 2, 1), 64, axis=2))
    tri = (np.arange(128)[:, None] <= np.arange(128)[None, :]).astype(np.float32)
    lam_in = np.stack([f(inputs[k]) for k in ("da_lq1", "da_lk1", "da_lq2", "da_lk2")], 1)
    lam_in = bc(lam_in[:, None], (DEPTH, 128, 4, 64))
    subln = bc(f(inputs["da_subln_g"])[:, None], (DEPTH, 128, 128))
    rb = f(inputs["rel_bias"])
    cfar = bc(rb[31][None], (128, 4))
    cvw = np.ascontiguousarray(f(inputs["cv_dw_w"]).reshape(DEPTH, 31, 2, 128).transpose(0, 3, 2, 1))
    cvp = np.stack([f(inputs[k]).reshape(DEPTH, 2, 128) for k in ("cv_dw_b", "cv_ln_g", "cv_ln_b")], -1)
    cvp = np.ascontiguousarray(cvp.transpose(0, 2, 1, 3))
    gavg = np.zeros((128, 128), np.float32)
    gavg[:64, :64] = 1.0 / 64
    gavg[64:, 64:] = 1.0 / 64
    fw = f(inputs["ffn_conv_w"]).reshape(DEPTH, 3, NFC, 128)
    fb = f(inputs["ffn_conv_b"]).reshape(DEPTH, 1, NFC, 128)
    fcw = np.ascontiguousarray(np.concatenate([fw, fb], 1).transpose(0, 3, 2, 1))
    ident = np.eye(128, dtype=np.float32)
    wup = f(inputs["ffn_w_up"])
    common = dict(w_in=f(inputs["w_in"]), w_out=f(inputs["w_out"]), ffn_w_down=f(inputs["ffn_w_down"]),
                  gvec=gvec, gmv=gmv, gm_wT=gm_wT, gm_bs=gm_bs, tri=tri, lam_in=lam_in, subln=subln, cfar=cfar,
                  cvw=cvw, cvp=cvp, gavg=gavg, fcw=fcw, ident=ident)
    k = np.arange(128)[:, None, None]
    n = np.arange(12)[None, :, None]
    q = np.arange(128)[None, None, :]
    maps = []
    for c in range(NCORES):
        rel = (c + 4 - n) * 128 + q - k
        idx = t5_bucket(rel)
        tab = rb[idx]
        tab = np.where((rel >= 0)[..., None], tab, np.float32(NEG)).astype(np.float32)
        btab = np.ascontiguousarray(tab.transpose(3, 0, 1, 2))
        sel = np.zeros((128, 8), np.float32)
        sel[:, c] = 1.0
        parts = []
        for i in range(11):
            l_, fc = divmod(c * 11 + i, NFC)
            part, j = divmod(fc, 22)
            col = part * DFF + j * 128
            parts.append(wup[l_][:, col:col + 128].reshape(8, 128, 128).transpose(1, 0, 2).reshape(128, 1024))
        d = dict(common)
        d.update(x=np.ascontiguousarray(xb[:, c]), btab=btab, sel=sel, wup_part=np.ascontiguousarray(np.stack(parts)))
        maps.append(d)
    return maps


_CACHE = {}


def kernel(**inputs):
    NT = 16
    if "nc" not in _CACHE:
        _CACHE["nc"] = build(NT)
    nc = _CACHE["nc"]
    maps = host_inputs(inputs, NT)
    res = run_bass_kernel_spmd(nc, maps, core_ids=list(range(NCORES)))
    out = np.zeros((NT, 8, 128, D), np.float32)
    for c in range(NCORES):
        out[:, c] = np.asarray(res.results[c]["y"])
    return out.reshape(1, NT * 8 * 128, D)
```

```python
import math
from contextlib import ExitStack
import numpy as np
import concourse.bass as bass
import concourse.mybir as mybir
from concourse.bass_utils import run_bass_kernel_spmd

F32, BF16 = mybir.dt.float32, mybir.dt.bfloat16
AF = mybir.ActivationFunctionType
ALU = mybir.AluOpType
NCORES = 8
D = 1024
INW = 2560
DFF = 2816
NFC = 44
EPS = 1e-6
NEG = -30000.0
DEPTH = 2
ENG = ("pe", "act", "dve", "pool", "sp")


class Tracker:
    def __init__(self, nc, stack):
        self.nc, self.stack = nc, stack
        self.ops = {e: [] for e in ENG}
        self.sem, self.cnt = {}, {}
        self.waited = {e: {} for e in ENG}
        self.lastw, self.readers = {}, {}
        for e in ENG:
            self._mk("E" + e)

    def _mk(self, name):
        if name not in self.sem:
            self.sem[name] = self.stack.enter_context(self.nc.semaphore(name))
            self.cnt[name] = 0
        return name

    def _deps(self, reads, writes):
        deps = {}
        def add(tok):
            if tok is not None:
                deps[tok[0]] = max(deps.get(tok[0], 0), tok[1])
        for b in reads:
            add(self.lastw.get(b))
        for b in writes:
            add(self.lastw.get(b))
            for s, v in self.readers.get(b, {}).items():
                add((s, v))
        return deps

    def _waits(self, eng, deps):
        w = []
        for s, v in deps.items():
            if eng == "pe" and s == "Epe":
                continue
            if self.waited[eng].get(s, 0) < v:
                self.waited[eng][s] = v
                w.append((self.sem[s], v))
        return w

    def _commit(self, tok, reads, writes):
        for b in reads:
            self.readers.setdefault(b, {})[tok[0]] = tok[1]
        for b in writes:
            self.lastw[b] = tok
            self.readers[b] = {}

    def op(self, eng, fn, reads=(), writes=()):
        w = self._waits(eng, self._deps(reads, writes))
        s = "E" + eng
        self.cnt[s] += 1
        tok = (s, self.cnt[s])
        sem = self.sem[s]
        def emit(e, fn=fn, w=w, sem=sem):
            for sh, v in w:
                e.wait_ge(sh, v)
            fn(e).then_inc(sem, 1)
        self.ops[eng].append(emit)
        self._commit(tok, reads, writes)

    def dma(self, eng, fn, reads=(), writes=(), slot=None, inc=16):
        w = self._waits(eng, self._deps(reads, writes))
        s = self._mk("D" + slot)
        self.cnt[s] += inc
        tok = (s, self.cnt[s])
        sem = self.sem[s]
        def emit(e, fn=fn, w=w, sem=sem, inc=inc):
            for sh, v in w:
                e.wait_ge(sh, v)
            fn(e).then_inc(sem, inc)
        self.ops[eng].append(emit)
        self._commit(tok, reads, writes)

    def barrier(self):
        allv = {s: v for s, v in self.cnt.items() if v > 0}
        for eng in ENG:
            w = self._waits(eng, dict(allv))
            def emit(e, w=w):
                for sh, v in w:
                    e.wait_ge(sh, v)
            self.ops[eng].append(emit)

    def run(self, block):
        for name, meth in (("pe", block.tensor), ("act", block.scalar), ("dve", block.vector),
                           ("pool", block.gpsimd), ("sp", block.sync)):
            ops = self.ops[name]
            def body(e, ops=ops):
                for o in ops:
                    o(e)
            meth(body)


def t5_bucket(n):
    n = np.maximum(n, 0)
    nf = np.maximum(n, 1).astype(np.float32)
    large = 16 + (np.log(nf / np.float32(16)) / np.float32(math.log(128 / 16)) * np.float32(16)).astype(np.int32)
    large = np.minimum(large, 31)
    return np.where(n < 16, n, large)


def build(NT=16, stop_after=None):
    S_LOC = NT * 128
    NBLK = NT * 8
    nc = bass.Bass("TRN2", target_bir_lowering=False)
    dt_in = lambda name, shape, dt=F32: nc.dram_tensor(name, list(shape), dt, kind="ExternalInput").ap()
    x_in = dt_in("x", [NT, 128, D])
    win_part = dt_in("win_part", [256, INW])
    wout_part = dt_in("wout_part", [256, D])
    wup_part = dt_in("wup_part", [11, 128, 1024])
    wdn_part = dt_in("wdn_part", [704, D])
    gvec = dt_in("gvec", [DEPTH, 128, 4, D])
    gmv = dt_in("gmv", [DEPTH, 128, 2, 256])
    gm_wT = dt_in("gm_wT", [DEPTH, 128, 4, 128])
    gm_bs = dt_in("gm_bs", [DEPTH, 128, 256])
    tri_in = dt_in("tri", [128, 128])
    lam_in = dt_in("lam_in", [DEPTH, 128, 4, 64])
    subln_in = dt_in("subln", [DEPTH, 128, 128])
    btab_in = dt_in("btab", [4, 128, 12, 128])
    cfar_in = dt_in("cfar", [128, 4])
    cvw_in = dt_in("cvw", [DEPTH, 128, 2, 31])
    cvp_in = dt_in("cvp", [DEPTH, 128, 2, 3])
    gavg_in = dt_in("gavg", [128, 128])
    fcw_in = dt_in("fcw", [DEPTH, 128, NFC, 4])
    sel_in = dt_in("sel", [128, 8])
    ident_in = dt_in("ident", [128, 128])
    y_out = nc.dram_tensor("y", [NT, 128, D], F32, kind="ExternalOutput").ap()
    dbg = {}
    if stop_after is not None:
        dbg["q"] = nc.dram_tensor("dbg_q", [128, 4, S_LOC], BF16, kind="ExternalOutput").ap()
        dbg["ct"] = nc.dram_tensor("dbg_ct", [128, 8, S_LOC], BF16, kind="ExternalOutput").ap()
        dbg["gl"] = nc.dram_tensor("dbg_gl", [128, 2, S_LOC], F32, kind="ExternalOutput").ap()
        dbg["kg"] = nc.dram_tensor("dbg_kg", [NCORES * 512, S_LOC], BF16, kind="ExternalOutput").ap()
        dbg["vg"] = nc.dram_tensor("dbg_vg", [NCORES * S_LOC, 512], BF16, kind="ExternalOutput").ap()
        dbg["h2"] = nc.dram_tensor("dbg_h2", [128, 8, S_LOC], BF16, kind="ExternalOutput").ap()

    xbuf = [nc.dram_tensor(f"xbuf{i}", [NT, 128, D], F32).ap() for i in range(2)]
    Kc = nc.dram_tensor("Kc", [512, S_LOC], BF16).ap()
    Vc = nc.dram_tensor("Vc", [S_LOC, 512], BF16).ap()
    Hc = nc.dram_tensor("Hc", [128, 2 * NT * 32], F32).ap()
    Tc = nc.dram_tensor("Tc", [128, 8 * NT * 2], BF16).ap()
    Wc_in = nc.dram_tensor("Wc_in", [256, INW], BF16).ap()
    Wg_in = nc.dram_tensor("Wg_in", [NCORES * 256, INW], BF16, addr_space="Shared").ap()
    Wc_out = nc.dram_tensor("Wc_out", [256, D], BF16).ap()
    Wg_out = nc.dram_tensor("Wg_out", [NCORES * 256, D], BF16, addr_space="Shared").ap()
    Wc_dn = nc.dram_tensor("Wc_dn", [704, D], BF16).ap()
    Wg_dn = nc.dram_tensor("Wg_dn", [NCORES * 704, D], BF16, addr_space="Shared").ap()
    Wc = nc.dram_tensor("Wc", [11 * 128, 1024], BF16).ap()
    Wg = nc.dram_tensor("Wg", [NCORES * 11 * 128, 1024], BF16, addr_space="Shared").ap()
    Kg = nc.dram_tensor("Kg", [NCORES * 512, S_LOC], BF16, addr_space="Shared").ap()
    Vg = nc.dram_tensor("Vg", [NCORES * S_LOC, 512], BF16, addr_space="Shared").ap()
    Hg = nc.dram_tensor("Hg", [NCORES * 128, 2 * NT * 32], F32, addr_space="Shared").ap()
    Tg = nc.dram_tensor("Tg", [NCORES * 128, 8 * NT * 2], BF16, addr_space="Shared").ap()

    with ExitStack() as st:
        sb = lambda name, shape, dt=F32: st.enter_context(nc.sbuf_tensor("s_" + name, list(shape), dt))
        psS = [st.enter_context(nc.psum_tensor(f"psS{i}", [128, 1024], F32)) for i in range(2)]
        ps = [psS[0][:, 0:512], psS[0][:, 512:1024], psS[1][:, 0:512], psS[1][:, 512:1024]] + \
             [st.enter_context(nc.psum_tensor(f"ps{i}", [128, 512], F32)) for i in range(4, 7)]
        psT = st.enter_context(nc.psum_tensor("psT", [128, 1024], BF16))
        ARENA = sb("ARENA", [128, 63488], BF16)
        ident = sb("ident", [128, 128], BF16)
        identf = sb("identf", [128, 128], F32)
        gv = sb("gv", [128, 2, D], F32)
        wT = sb("wT", [128, 4, 128], BF16)
        wTf = sb("wTf", [128, 4, 128], F32)
        tri = sb("tri", [128, 128], F32)
        gmvt = sb("gmvt", [128, 2, 256], F32)
        gbs = sb("gbs", [128, 256], F32)
        lamt = sb("lamt", [128, 4, 64], F32)
        lamw = sb("lamw", [128, 8], F32)
        gsub = sb("gsub", [128, 128], F32)
        cfar = sb("cfar", [128, 4], F32)
        btab = sb("btab", [128, 12, 128], F32)
        cvw = sb("cvw", [128, 2, 31], F32)
        cvp = sb("cvp", [128, 2, 3], F32)
        gavg = sb("gavg", [128, 128], F32)
        fcw = sb("fcw", [128, NFC, 4], F32)
        sel = sb("sel", [128, 8], F32)
        stt = sb("stt", [128, 64], F32)
        bnst = sb("bnst", [128, 2, 8], F32)
        xt = [sb(f"xt{i}", [128, D], F32) for i in range(2)]
        junk = sb("junk", [128, D], BF16)
        hb = [sb(f"hb{i}", [128, D], BF16) for i in range(2)]
        w1 = [sb(f"w1_{i}", [128, D], F32) for i in range(2)]
        w2all = sb("w2all", [128, 3, 512], F32)
        w2 = [w2all[:, i, :] for i in range(3)]
        wball = sb("wball", [128, 2, 1024], BF16)
        wb = [wball[:, 0, 0:512], wball[:, 0, 512:1024], wball[:, 1, 0:512], wball[:, 1, 512:1024]]
        candT = sb("candT", [128, 8, 8 * NT * 2], BF16)
        halT = sb("halT", [128, 8, NT, 2], BF16)
        halTf = sb("halTf", [128, 8 * NT * 2], F32)
        tailT = sb("tailT", [128, 8, NT, 2], BF16)
        upsb = [sb(f"upsb{i}", [128, 8, 130], F32) for i in range(2)]
        acc1 = sb("acc1", [128, 8, 128], F32)
        acc = [w2all[:, 0:2, :].rearrange("p a (t n) -> p (a t) n", n=128), acc1[:]]
        gact = btab[:, 0:8, :]
        cvq = w2[2]

        block = st.enter_context(nc.Block())
        T = Tracker(nc, st)

        WIN = ARENA[:, 0:8 * INW].rearrange("p (k n) -> p k n", k=8)
        HT = [ARENA[:, 20480 + i * 4096: 20480 + (i + 1) * 4096].rearrange("p (k n) -> p k n", k=8) for i in range(2)]
        GL = ARENA[:, 28672:38912].bitcast(F32).rearrange("p (i t n) -> p i t n", i=2, n=160)[:, :, 0:NT, :]
        CT = ARENA[:, 38912:55296].rearrange("p (k n) -> p k n", k=8)[:, :, 0:S_LOC]
        QT = ARENA[:, 55296:63488].rearrange("p (k n) -> p k n", k=4)[:, :, 0:S_LOC]
        cand = ARENA[:, 0:16384].bitcast(F32).rearrange("p (r n) -> p r n", r=8)[:, :, 0:2 * NT * 32]
        cvy = ARENA[:, 16384:24576].bitcast(F32).rearrange("p (i n) -> p i n", i=2)[:, :, 0:S_LOC]
        KT = ARENA[:, 0:NBLK * 128].rearrange("p (j n) -> p j n", n=128)
        VH = ARENA[:, 16384:16384 + NBLK * 130].rearrange("p (j n) -> p j n", n=130)
        QP = ARENA[:, 33280:37376].rearrange("p (a n) -> p a n", a=2)[:, :, 0:S_LOC]
        WOUT = ARENA[:, 16384:24576].rearrange("p (k n) -> p k n", k=8)
        H2T = ARENA[:, 0:16384].rearrange("p (k n) -> p k n", k=8)[:, :, 0:S_LOC]
        GT = ARENA[:, 16384:38912].rearrange("p (j n) -> p j n", j=22)
        WDN = ARENA[:, 38912:61440].rearrange("p (j n) -> p j n", j=22)

        ARW = [f"arenaW{k}" for k in range(8)]
        KTN = [f"KT{c}" for c in range(8)]
        VHN = [f"VH{c}" for c in range(8)]
        CANDN = ["cand00", "cand01"] + [f"cand{r}" for r in range(1, 8)]
        stat_i = [0]
        def newstat():
            stat_i[0] = (stat_i[0] + 1) % 64
            i = stat_i[0]
            return stt[:, i:i + 1], f"st{i}"

        def rstd_from(src_ap, srcname, n, eng_sq="act"):
            ss, ssn = newstat()
            T.op("act", lambda e: e.activation(out=junk[:, 0:n], in_=src_ap, func=AF.Square, accum_out=ss),
                 reads=[srcname], writes=["junk", ssn])
            return rstd_of(ss, ssn, 1.0 / n)

        def rstd_of(v, vn, scale):
            lnv, lnn = newstat()
            T.op("pool", lambda e: e.tensor_scalar(out=lnv, in0=v, scalar1=scale, scalar2=EPS, op0=ALU.mult, op1=ALU.add),
                 reads=[vn], writes=[lnn])
            r, rn = newstat()
            T.op("pool", lambda e: e.tensor_tensor(out=r, in0=lnv, in1=mhalf[:, 0:1], op=ALU.pow), reads=[lnn, "mhalf"], writes=[rn])
            return r, rn

        epsT = sb("epsT", [128, 1], F32)
        T.op("dve", lambda e: e.memset(epsT[:], EPS), writes=["epsT"])
        mhalf = sb("mhalf", [128, 1], F32)
        T.op("dve", lambda e: e.memset(mhalf[:], -0.5), writes=["mhalf"])

        def ld(dst, src, name, eng="sp"):
            T.dma(eng, lambda e: e.dma_start(out=dst, in_=src), writes=[name], slot="c_" + name)

        ld(identf[:], ident_in, "identf")
        T.op("dve", lambda e: e.tensor_copy(out=ident[:], in_=identf[:]), reads=["identf"], writes=["ident"])
        ld(tri[:], tri_in, "tri")
        ld(cfar[:], cfar_in, "cfar")
        ld(gavg[:], gavg_in, "gavg")
        ld(sel[:], sel_in, "sel")

        def transpose_to(dst_ap, dstname, src_tile, srcname, nchunk, psname="psT"):
            def f(e):
                r = None
                for k in range(nchunk):
                    r = e.transpose(out=psT[:, k * 128:(k + 1) * 128], in_=src_tile[:, k * 128:(k + 1) * 128], identity=ident[:])
                return r
            T.op("pe", f, reads=[srcname, "ident"], writes=[psname])
            T.op("act", lambda e: e.activation(out=dst_ap, in_=psT[:, 0:nchunk * 128].rearrange("p (k n) -> p k n", k=nchunk), func=AF.Copy),
                 reads=[psname], writes=[dstname])

        def layer_consts(l):
            ld(wTf[:], gm_wT[l], "wTf")
            for h in range(4):
                T.op("dve", lambda e, h=h: e.tensor_tensor(out=wT[:, h, :], in0=wTf[:, h, :], in1=tri[:], op=ALU.mult),
                     reads=["wTf", "tri"], writes=["wT"])
            ld(gmvt[:], gmv[l], "gmvt")
            ld(gbs[:], gm_bs[l], "gbs")
            ld(lamt[:], lam_in[l], "lamt")
            ld(gsub[:], subln_in[l], "gsub")
            ld(cvw[:], cvw_in[l], "cvw")
            ld(cvp[:], cvp_in[l], "cvp")
            ld(fcw[:], fcw_in[l], "fcw")
            lam_init = 0.8 - 0.6 * math.exp(-0.3 * l)
            for i in range(2):
                T.op("dve", lambda e, i=i: e.tensor_tensor(out=junk[:, i * 64:(i + 1) * 64], in0=lamt[:, 2 * i, :], in1=lamt[:, 2 * i + 1, :], op=ALU.mult),
                     reads=["lamt"], writes=["junk"])
                T.op("dve", lambda e, i=i: e.reduce_sum(out=lamw[:, i:i + 1], in_=junk[:, i * 64:(i + 1) * 64], axis=mybir.AxisListType.X),
                     reads=["junk"], writes=["lamw"])
            T.op("act", lambda e: e.activation(out=lamw[:, 4:6], in_=lamw[:, 0:2], func=AF.Exp), reads=["lamw"], writes=["lamw"])
            T.op("dve", lambda e: e.tensor_tensor(out=lamw[:, 6:7], in0=lamw[:, 5:6], in1=lamw[:, 4:5], op=ALU.subtract),
                 reads=["lamw"], writes=["lamw"])
            T.op("dve", lambda e: e.tensor_scalar(out=lamw[:, 2:3], in0=lamw[:, 6:7], scalar1=-lam_init, scalar2=None, op0=ALU.add),
                 reads=["lamw"], writes=["lamw"])
            T.op("dve", lambda e: e.tensor_scalar(out=gsub[:], in0=gsub[:], scalar1=1.0 - lam_init, scalar2=None, op0=ALU.mult),
                 reads=["gsub"], writes=["gsub"])

        def phase_A(l, xsrc):
            ld(gv[:, 0, :], gvec[l][:, 0, :], "gv0")
            for kc in range(8):
                T.dma("sp", lambda e, kc=kc: e.dma_start(out=WIN[:, kc, :], in_=Wg_in[l * 1024 + kc * 128:l * 1024 + (kc + 1) * 128, :]),
                      reads=["Wg_in"], writes=[f"arenaW{kc}"], slot=f"win{kc}")
            def normsA(g):
                hT = HT[g % 2]
                hTn = f"hT{g % 2}"
                for t in range(4):
                    m = 4 * g + t
                    b = m % 2
                    T.dma("sp", lambda e, m=m, b=b: e.dma_start(out=xt[b][:], in_=xsrc[m]), writes=[f"xt{b}"], slot=f"xt{b}")
                    r, rn = rstd_from(xt[b][:], f"xt{b}", D)
                    T.op("dve", lambda e, b=b, r=r: e.scalar_tensor_tensor(out=hb[b][:], in0=xt[b][:], scalar=r, in1=gv[:, 0, :], op0=ALU.mult, op1=ALU.mult),
                         reads=[f"xt{b}", rn, "gv0"], writes=[f"hb{b}"])
                    transpose_to(hT[:, :, t * 128:(t + 1) * 128], hTn, hb[b], f"hb{b}", 8)
            def mmA(g):
                hT = HT[g % 2]
                hTn = f"hT{g % 2}"
                order = [("q", h, 512 + 128 * h) for h in range(4)] + [("k", h, 1024 + 128 * h) for h in range(4)] + \
                        [("cg", i, 2304 + 128 * i) for i in range(2)] + [("ca", i, 2048 + 128 * i) for i in range(2)]
                for oi, (kind, idx, col) in enumerate(order):
                    pb = oi % 2
                    def f(e, col=col, pb=pb, hT=hT):
                        r = None
                        for kc in range(8):
                            r = e.matmul(out=ps[pb][:], lhsT=WIN[:, kc, col:col + 128], rhs=hT[:, kc, :], start=(kc == 0), stop=(kc == 7))
                        return r
                    T.op("pe", f, reads=[hTn] + ARW, writes=[f"ps{pb}"])
                    tok = slice(g * 512, (g + 1) * 512)
                    if kind == "q":
                        T.op("act", lambda e, pb=pb, idx=idx, tok=tok: e.activation(out=QT[:, idx, tok], in_=ps[pb][:], func=AF.Copy),
                             reads=[f"ps{pb}"], writes=["QT"])
                    elif kind == "k":
                        wi = idx % 3
                        T.op("dve", lambda e, pb=pb, wi=wi: e.tensor_copy(out=wb[wi][:], in_=ps[pb][:]), reads=[f"ps{pb}"], writes=[f"wb{wi}"])
                        T.dma("sp", lambda e, wi=wi, idx=idx, tok=tok: e.dma_start(out=Kc[idx * 128:(idx + 1) * 128, tok], in_=wb[wi][:]),
                              reads=[f"wb{wi}"], writes=[f"Kc{g}_{idx}"], slot=f"wb{wi}")
                    elif kind == "cg":
                        T.op("act", lambda e, pb=pb, idx=idx: e.activation(out=w2[idx][:], in_=ps[pb][:], func=AF.Sigmoid),
                             reads=[f"ps{pb}"], writes=[f"w2{idx}"])
                    else:
                        T.op("dve", lambda e, pb=pb, idx=idx, g=g: e.tensor_tensor(
                            out=GL[:, idx, 4 * g:4 * g + 4, 32:160], in0=ps[pb][:].rearrange("p (t n) -> p t n", t=4),
                            in1=w2[idx][:].rearrange("p (t n) -> p t n", t=4), op=ALU.mult),
                            reads=[f"ps{pb}", f"w2{idx}"], writes=["GL"])
                for t in range(4):
                    m = 4 * g + t
                    tcols = slice(t * 128, (t + 1) * 128)
                    def fg(e, tcols=tcols, hT=hT):
                        r = None
                        for kc in range(8):
                            r = e.matmul(out=ps[2][:], lhsT=hT[:, kc, tcols], rhs=WIN[:, kc, 0:512], start=(kc == 0), stop=(kc == 7))
                        return r
                    T.op("pe", fg, reads=[hTn] + ARW, writes=["ps2"])
                    def fv(e, tcols=tcols, hT=hT):
                        r = None
                        for kc in range(8):
                            r = e.matmul(out=ps[3][:], lhsT=hT[:, kc, tcols], rhs=WIN[:, kc, 1536:2048], start=(kc == 0), stop=(kc == 7))
                        return r
                    T.op("pe", fv, reads=[hTn] + ARW, writes=["ps3"])
                    T.op("dve", lambda e: e.tensor_copy(out=wb[2][:], in_=ps[3][:]), reads=["ps3"], writes=["wb2"])
                    T.dma("sp", lambda e, m=m: e.dma_start(out=Vc[m * 128:(m + 1) * 128, :], in_=wb[2][:]),
                          reads=["wb2"], writes=[f"Vc{m}"], slot="wb2")
                    ug = w1[0]
                    T.op("act", lambda e: e.activation(out=ug[:, 0:512], in_=ps[2][:], func=AF.Gelu_apprx_tanh), reads=["ps2"], writes=["w1_0"])
                    T.op("dve", lambda e: e.bn_stats(out=bnst[:, 0, 0:6], in_=ug[:, 256:512]), reads=["w1_0"], writes=["bnst0"])
                    T.op("dve", lambda e: e.bn_aggr(out=bnst[:, 1, 0:2], in_=bnst[:, 0, 0:6]), reads=["bnst0"], writes=["bnst1"])
                    r, rn = rstd_of(bnst[:, 1, 1:2], "bnst1", 1.0)
                    vn = w1[0][:, 512:768]
                    T.op("dve", lambda e, r=r: e.tensor_scalar(out=vn, in0=ug[:, 256:512], scalar1=bnst[:, 1, 0:1], scalar2=r, op0=ALU.subtract, op1=ALU.mult),
                         reads=["w1_0", "bnst1", rn], writes=["w1_0b"])
                    T.op("dve", lambda e: e.tensor_tensor(out=vn, in0=vn, in1=gmvt[:, 0, :], op=ALU.mult), reads=["w1_0b", "gmvt"], writes=["w1_0b"])
                    T.op("dve", lambda e: e.tensor_tensor(out=wb[3][:, 0:256], in0=vn, in1=gmvt[:, 1, :], op=ALU.add), reads=["w1_0b", "gmvt"], writes=["wb3a"])
                    def fs(e):
                        r = None
                        for h in range(4):
                            r = e.matmul(out=ps[4][:, h * 64:(h + 1) * 64], lhsT=wT[:, h, :], rhs=wb[3][:, h * 64:(h + 1) * 64], start=True, stop=True)
                        return r
                    T.op("pe", fs, reads=["wb3a", "wT"], writes=["ps4"])
                    T.op("dve", lambda e: e.tensor_tensor(out=w1[0][:, 768:1024], in0=ps[4][:, 0:256], in1=gbs[:], op=ALU.add),
                         reads=["ps4", "gbs"], writes=["w1_0c"])
                    T.op("dve", lambda e: e.tensor_tensor(out=wb[3][:, 256:512], in0=w1[0][:, 768:1024], in1=ug[:, 0:256], op=ALU.mult),
                         reads=["w1_0c", "w1_0"], writes=["wb3b"])
                    transpose_to(CT[:, 0:2, m * 128:(m + 1) * 128], "CT", wb[3][:, 256:512], "wb3b", 2)
            normsA(0)
            for g in range(NT // 4):
                if g + 1 < NT // 4:
                    normsA(g + 1)
                mmA(g)
            for i in range(2):
                T.dma("sp", lambda e, i=i: e.dma_start(out=Hc.rearrange("p (i t n) -> p i t n", i=2, t=NT)[:, i], in_=GL[:, i, :, 128:160]),
                      reads=["GL"], writes=[f"Hc{i}"], slot=f"hc{i}")

        def gather(src, dst, reads, name):
            T.dma("pool", lambda e: e.collective_compute("AllGather", ALU.bypass, replica_groups=[list(range(NCORES))], ins=[src], outs=[dst]),
                  reads=reads, writes=[name], slot="cc_" + name, inc=1)

        def halo_select(dst_f32, dstname, cand_t, candname, width):
            T.op("dve", lambda e: e.tensor_scalar(out=dst_f32, in0=cand_t[:, 0, :], scalar1=sel[:, 0:1], scalar2=None, op0=ALU.mult),
                 reads=candname + ["sel"], writes=[dstname])
            for r_ in range(1, 8):
                T.op("dve", lambda e, r_=r_: e.scalar_tensor_tensor(out=dst_f32, in0=cand_t[:, r_, :], scalar=sel[:, r_:r_ + 1], in1=dst_f32, op0=ALU.mult, op1=ALU.add),
                     reads=candname + ["sel", dstname], writes=[dstname])

        def phase_B(l, xsrc, xdst, upto=3):
            kg_names = [f"Kg"]
            T.op("pool", lambda e: e.memset(cand[:, 0, :], 0.0), writes=["cand00", "cand01"])
            cview = cand[:].rearrange("p r (i t n) -> p r i t n", i=2, t=NT)
            hgv = Hg.rearrange("(r p) (i t n) -> r p i t n", p=128, i=2, t=NT)
            if NT > 1:
                for i in range(2):
                    T.dma("sp", lambda e, i=i: e.dma_start(out=cview[:, 0, i, 1:NT, :], in_=hgv[7, :, i, 0:NT - 1, :]), reads=["Hg"], writes=[f"cand0{i}"], slot=f"cand0{i}")
            for r_ in range(1, 8):
                T.dma("sp", lambda e, r_=r_: e.dma_start(out=cand[:, r_, :], in_=Hg[(r_ - 1) * 128:r_ * 128, :]), reads=["Hg"], writes=[f"cand{r_}"], slot=f"cand{r_}")
            hsel = cvy[:, 0, 0:2 * NT * 32]
            halo_select(hsel, "cvy", cand, CANDN, 2 * NT * 32)
            T.op("dve", lambda e: e.tensor_copy(out=GL[:, :, :, 0:32], in_=hsel.rearrange("p (i t n) -> p i t n", i=2, t=NT)),
                 reads=["cvy"], writes=["GL"])
            for i in range(2):
                ceng = "dve"
                T.op(ceng, lambda e, i=i: e.tensor_scalar(out=cvy[:, i, :].rearrange("p (t n) -> p t n", t=NT), in0=GL[:, i, :, 2:130],
                                                           scalar1=cvw[:, i, 0:1], scalar2=cvp[:, i, 0:1], op0=ALU.mult, op1=ALU.add),
                     reads=["GL", "cvw", "cvp"], writes=[f"cvy{i}"])
                for k in range(1, 31):
                    T.op(ceng, lambda e, i=i, k=k: e.scalar_tensor_tensor(
                        out=cvy[:, i, :].rearrange("p (t n) -> p t n", t=NT), in0=GL[:, i, :, 2 + k:130 + k], scalar=cvw[:, i, k:k + 1],
                        in1=cvy[:, i, :].rearrange("p (t n) -> p t n", t=NT), op0=ALU.mult, op1=ALU.add),
                        reads=["GL", "cvw", f"cvy{i}"], writes=[f"cvy{i}"])
            for i in range(2):
                for q4 in range(S_LOC // 512):
                    cs = slice(q4 * 512, (q4 + 1) * 512)
                    T.op("pe", lambda e, i=i, cs=cs: e.matmul(out=ps[0][:], lhsT=gavg[:], rhs=cvy[:, i, cs], start=True, stop=True),
                         reads=[f"cvy{i}", "gavg"], writes=["ps0"])
                    T.op("act", lambda e, i=i, cs=cs: e.activation(out=cvq[:], in_=cvy[:, i, cs], func=AF.Square), reads=[f"cvy{i}"], writes=["cvq"])
                    T.op("pe", lambda e: e.matmul(out=ps[1][:], lhsT=gavg[:], rhs=cvq[:], start=True, stop=True), reads=["cvq", "gavg"], writes=["ps1"])
                    T.op("act", lambda e: e.activation(out=w2[0][:], in_=ps[0][:], func=AF.Square), reads=["ps0"], writes=["w20"])
                    T.op("dve", lambda e: e.tensor_tensor(out=w2[0][:], in0=ps[1][:], in1=w2[0][:], op=ALU.subtract), reads=["ps1", "w20"], writes=["w20"])
                    T.op("act", lambda e: e.activation(out=w2[0][:], in_=w2[0][:], func=AF.Ln, bias=epsT[:, 0:1]), reads=["w20"], writes=["w20"])
                    T.op("act", lambda e: e.activation(out=w2[0][:], in_=w2[0][:], func=AF.Exp, scale=-0.5), reads=["w20"], writes=["w20"])
                    T.op("dve", lambda e, i=i, cs=cs: e.tensor_tensor(out=w2[1][:], in0=cvy[:, i, cs], in1=ps[0][:], op=ALU.subtract), reads=[f"cvy{i}", "ps0"], writes=["w21"])
                    T.op("dve", lambda e: e.tensor_tensor(out=w2[1][:], in0=w2[1][:], in1=w2[0][:], op=ALU.mult), reads=["w21", "w20"], writes=["w21"])
                    T.op("act", lambda e, i=i, cs=cs: e.activation(out=CT[:, 6 + i, cs], in_=w2[1][:], func=AF.Silu, bias=cvp[:, i, 2:3], scale=cvp[:, i, 1:2]),
                         reads=["w21", "cvp"], writes=["CT"])
            T.barrier()
            if upto == 1:
                return
            T.op("pool", lambda e: e.memset(QP[:, :, :], 0.0), writes=["QP"])
            for h in range(4):
                for mp in range(2):
                    T.op("pool", lambda e, mp=mp, h=h: e.tensor_copy(out=QP[mp * 64:(mp + 1) * 64, mp, :], in_=QT[mp * 64:(mp + 1) * 64, h, :]),
                         reads=["QT", "QP"], writes=["QP"])
                for c_ in range(8):
                    T.dma("sp", lambda e, c_=c_, h=h: e.dma_start(
                        out=KT[:, :, :].rearrange("p (m c) n -> p m c n", c=8)[:, :, c_, :],
                        in_=Kg[c_ * 512 + h * 128:c_ * 512 + (h + 1) * 128, :].rearrange("p (m n) -> p m n", n=128)),
                        reads=["Kg"], writes=[f"KT{c_}"], slot=f"kt{c_}")
                    T.dma("sp", lambda e, c_=c_, h=h: e.dma_start(
                        out=VH[:, :, 0:128].rearrange("p (m c) n -> p m c n", c=8)[:, :, c_, :],
                        in_=Vg[c_ * S_LOC:(c_ + 1) * S_LOC, h * 128:(h + 1) * 128].rearrange("(m p) n -> p m n", p=128)),
                        reads=["Vg"], writes=[f"VH{c_}"], slot=f"vh{c_}")
                T.op("pool", lambda e: e.memset(VH[:, :, 128:130], 1.0), writes=["VH1"])
                T.dma("sp", lambda e, h=h: e.dma_start(out=btab[:], in_=btab_in[h]), writes=["btab"], slot="btab")
                units = [(m, qd) for m in range(NT) for qd in range(2 * m + 2)]

                def score(ui, m, qd, h=h):
                    sb_ = ui % 2
                    S = psS[sb_]
                    qs = slice(m * 128, (m + 1) * 128)
                    def fsc(e):
                        r = None
                        for kb in range(4):
                            r = e.matmul(out=S[:, kb * 256:(kb + 1) * 256].rearrange("p (a n) -> p a n", a=2),
                                         lhsT=KT[:, 4 * qd + kb, :], rhs=QP[:, 0:2, qs], start=True, stop=True)
                        return r
                    T.op("pe", fsc, reads=KTN + ["QP"], writes=[f"ps{2 * sb_}", f"ps{2 * sb_ + 1}"])

                def softmax(ui, m, qd, h=h):
                    sb_ = ui % 2
                    S = psS[sb_]
                    P = wball[:, sb_, :]
                    sn = [f"ps{2 * sb_}", f"ps{2 * sb_ + 1}"]
                    pn = [f"wb{2 * sb_}", f"wb{2 * sb_ + 1}"]
                    near = (4 * qd + 3 >= 8 * m - 2)
                    if not near:
                        T.op("act", lambda e: e.activation(out=P, in_=S[:], func=AF.Exp, bias=cfar[:, h:h + 1], scale=0.125),
                             reads=sn + ["cfar"], writes=pn)
                    else:
                        n0 = 4 * qd - (8 * m - 4)
                        stg = w2all[:, 0:2, :]
                        for mp in range(2):
                            T.op("dve", lambda e, mp=mp: e.scalar_tensor_tensor(
                                out=stg.rearrange("p x (k2 a n) -> p (x k2) a n", a=2, n=128)[:, :, mp, :],
                                in0=S[:].rearrange("p (k a n) -> p k a n", k=4, a=2)[:, :, mp, :], scalar=0.125,
                                in1=btab[:, n0:n0 + 4, :], op0=ALU.mult, op1=ALU.add),
                                reads=sn + ["btab"], writes=["w20", "w21"])
                        T.op("act", lambda e: e.activation(out=P, in_=stg.rearrange("p x n -> p (x n)"), func=AF.Exp), reads=["w20", "w21"], writes=pn)

                def pv(ui, m, qd):
                    sb_ = ui % 2
                    P = wball[:, sb_, :]
                    ob = 4
                    nblk = 8 * m + 8
                    def fpv(e):
                        r = None
                        for kb in range(4):
                            j = 4 * qd + kb
                            for mp in range(2):
                                r = e.matmul(out=ps[ob + mp][:, 0:130], lhsT=P[:, kb * 256 + mp * 128: kb * 256 + (mp + 1) * 128], rhs=VH[:, j, :],
                                             start=(j == 0), stop=(j == nblk - 1))
                        return r
                    T.op("pe", fpv, reads=[f"wb{2 * sb_}", f"wb{2 * sb_ + 1}", "VH1"] + VHN, writes=[f"ps{ob}", f"ps{ob + 1}"])

                def fin1(m):
                    ob = 4
                    o1, o2 = ps[ob], ps[ob + 1]
                    o1n, o2n = f"ps{ob}", f"ps{ob + 1}"
                    osb1, osb2 = w1[1][:, 512:642], w1[1][:, 768:898]
                    T.op("act", lambda e: e.activation(out=osb1, in_=ps[ob][:, 0:130], func=AF.Copy), reads=[o1n], writes=["osb1"])
                    T.op("dve", lambda e: e.tensor_copy(out=osb2, in_=ps[ob + 1][:, 0:130]), reads=[o2n], writes=["osb2"])
                    o1, o2, o1n, o2n = osb1, osb2, "osb1", "osb2"
                    r1, r1n = newstat()
                    r2, r2n = newstat()
                    T.op("dve", lambda e: e.reciprocal(out=r1, in_=o1[:, 128:129]), reads=[o1n], writes=[r1n])
                    T.op("dve", lambda e: e.reciprocal(out=r2, in_=o2[:, 128:129]), reads=[o2n], writes=[r2n])
                    T.op("dve", lambda e: e.tensor_tensor(out=r2, in0=r2, in1=lamw[:, 2:3], op=ALU.mult), reads=[r2n, "lamw"], writes=[r2n])
                    at = w1[1]
                    T.op("dve", lambda e: e.tensor_scalar(out=at[:, 0:128], in0=o2[:, 0:128], scalar1=r2, scalar2=None, op0=ALU.mult),
                         reads=[o2n, r2n], writes=["w1_1"])
                    T.op("dve", lambda e: e.scalar_tensor_tensor(out=at[:, 128:256], in0=o1[:, 0:128], scalar=r1, in1=at[:, 0:128], op0=ALU.mult, op1=ALU.add),
                         reads=[o1n, r1n, "w1_1"], writes=["w1_1b"])
                    rr, rrn = rstd_from(at[:, 128:256], "w1_1b", 128)
                    hbm = hb[m % 2]
                    T.op("dve", lambda e: e.scalar_tensor_tensor(out=hbm[:, 0:128], in0=at[:, 128:256], scalar=rr, in1=gsub[:], op0=ALU.mult, op1=ALU.mult),
                         reads=["w1_1b", rrn, "gsub"], writes=[f"hb{m % 2}"])

                def fin2(m, h=h):
                    transpose_to(CT[:, 2 + h:3 + h, m * 128:(m + 1) * 128], "CT", hb[m % 2], f"hb{m % 2}", 1)

                pending = []
                score(0, *units[0])
                for ui, (m, qd) in enumerate(units):
                    if ui + 1 < len(units):
                        score(ui + 1, *units[ui + 1])
                    softmax(ui, m, qd)
                    pv(ui, m, qd)
                    for item in list(pending):
                        item[0] -= 1
                        if item[0] <= 0:
                            fin2(item[1])
                            pending.remove(item)
                    if qd == 2 * m + 1:
                        fin1(m)
                        pending.append([3, m])
                for item in pending:
                    fin2(item[1])
            T.barrier()
            if upto == 2:
                return
            ld(gv[:, 0, :], gvec[l][:, 1, :], "gv0")
            ld(gv[:, 1, :], gvec[l][:, 2, :], "gv1")
            for kc in range(8):
                T.dma("sp", lambda e, kc=kc: e.dma_start(out=WOUT[:, kc, :], in_=Wg_out[l * 1024 + kc * 128:l * 1024 + (kc + 1) * 128, :]),
                      reads=["Wg_out"], writes=[f"arenaW{kc}"], slot=f"win{kc}")
            def mmO(m):
                qs = slice(m * 128, (m + 1) * 128)
                pbase = 5 if m % 2 == 0 else 0
                for nh in range(2):
                    def fo(e, nh=nh, qs=qs):
                        r = None
                        for kc in range(8):
                            r = e.matmul(out=ps[pbase + nh][:], lhsT=CT[:, kc, qs], rhs=WOUT[:, kc, nh * 512:(nh + 1) * 512], start=(kc == 0), stop=(kc == 7))
                        return r
                    T.op("pe", fo, reads=["CT"] + ARW, writes=[f"ps{pbase + nh}"])
            mmO(0)
            for m in range(NT):
                b = m % 2
                qs = slice(m * 128, (m + 1) * 128)
                pbase = 5 if m % 2 == 0 else 0
                T.dma("sp", lambda e, m=m, b=b: e.dma_start(out=xt[b][:], in_=xsrc[m]), writes=[f"xt{b}"], slot=f"xt{b}")
                if m + 1 < NT:
                    mmO(m + 1)
                for nh in range(2):
                    T.op("dve", lambda e, nh=nh, b=b, pbase=pbase: e.tensor_copy(out=w1[b][:, nh * 512:(nh + 1) * 512], in_=ps[pbase + nh][:]),
                         reads=[f"ps{pbase + nh}"], writes=[f"w1_{b}"])
                r, rn = rstd_from(w1[b][:], f"w1_{b}", D)
                T.op("dve", lambda e, b=b, r=r: e.scalar_tensor_tensor(out=w1[b][:], in0=w1[b][:], scalar=r, in1=gv[:, 0, :], op0=ALU.mult, op1=ALU.mult),
                     reads=[f"w1_{b}", rn, "gv0"], writes=[f"w1_{b}"])
                T.op("dve", lambda e, b=b: e.tensor_tensor(out=xt[b][:], in0=xt[b][:], in1=w1[b][:], op=ALU.add), reads=[f"xt{b}", f"w1_{b}"], writes=[f"xt{b}"])
                T.dma("sp", lambda e, m=m, b=b: e.dma_start(out=xdst[m], in_=xt[b][:]), reads=[f"xt{b}"], writes=[f"xmid{m}"], slot=f"xo{b}")
                r, rn = rstd_from(xt[b][:], f"xt{b}", D)
                T.op("dve", lambda e, b=b, r=r: e.scalar_tensor_tensor(out=hb[b][:], in0=xt[b][:], scalar=r, in1=gv[:, 1, :], op0=ALU.mult, op1=ALU.mult),
                     reads=[f"xt{b}", rn, "gv1"], writes=[f"hb{b}"])
                transpose_to(H2T[:, :, qs], "H2T", hb[b], f"hb{b}", 8)
            T.op("dve", lambda e: e.tensor_copy(out=tailT[:], in_=H2T[:].rearrange("p k (t n) -> p k t n", n=128)[:, :, :, 126:128]),
                 reads=["H2T"], writes=["tailT"])
            T.dma("sp", lambda e: e.dma_start(out=Tc, in_=tailT[:].rearrange("p k t n -> p (k t n)")), reads=["tailT"], writes=["Tc"], slot="tc")

        def phase_C(l, xsrc, xdst):
            ld(gv[:, 0, :], gvec[l][:, 3, :], "gv0")
            T.op("pool", lambda e: e.memset(candT[:, 0, :], 0.0), writes=["cand00", "cand01"])
            cv_ = candT[:].rearrange("p r (k t n) -> p r k t n", k=8, t=NT)
            tgv = Tg.rearrange("(r p) (k t n) -> r p k t n", p=128, k=8, t=NT)
            if NT > 1:
                for k in range(8):
                    T.dma("sp", lambda e, k=k: e.dma_start(out=cv_[:, 0, k, 1:NT, :], in_=tgv[7, :, k, 0:NT - 1, :]), reads=["Tg"], writes=[f"cand0{k % 2}"], slot=f"cand0{k % 2}")
            for r_ in range(1, 8):
                T.dma("sp", lambda e, r_=r_: e.dma_start(out=candT[:, r_, :], in_=Tg[(r_ - 1) * 128:r_ * 128, :]), reads=["Tg"], writes=[f"cand{r_}"], slot=f"cand{r_}")
            halo_select(halTf[:], "halTf", candT, CANDN, 8 * NT * 2)
            T.op("dve", lambda e: e.tensor_copy(out=halT[:].rearrange("p k t n -> p (k t n)"), in_=halTf[:]), reads=["halTf"], writes=["halT"])
            for j in range(22):
                T.dma("sp", lambda e, j=j: e.dma_start(out=WDN[:, j, :], in_=Wg_dn[l * DFF + j * 128:l * DFF + (j + 1) * 128, :]),
                      reads=["Wg_dn"], writes=[f"wdn{j % 4}"], slot=f"wdn{j % 4}")
            NPASS = max(1, NT // 8)
            TP = NT // NPASS
            for pa in range(NPASS):
                t0 = pa * TP
                for j in range(22):
                    for part in range(2):
                        fc = part * 22 + j
                        col = part * DFF + j * 128
                        wbuf = wup_t[fc % 3]
                        wn = f"wup{fc % 3}"
                        gch = l * 44 + fc
                        T.dma("sp", lambda e, gch=gch, wbuf=wbuf: e.dma_start(out=wbuf[:].rearrange("p k n -> p (k n)"), in_=Wg[gch * 128:(gch + 1) * 128, :]),
                              reads=["Wg"], writes=[wn], slot=wn)
                        ub = upsb[part]
                        nbank = (TP + 3) // 4
                        for hb_ in range(nbank):
                            nt_ = min(4, TP - hb_ * 4)
                            cs = slice((t0 + hb_ * 4) * 128, (t0 + hb_ * 4 + nt_) * 128)
                            pbk = (fc * 2 + hb_) % 4
                            def fu(e, cs=cs, pbk=pbk, wbuf=wbuf, nt_=nt_):
                                r = None
                                for kc in range(8):
                                    r = e.matmul(out=ps[pbk][:, 0:nt_ * 128], lhsT=wbuf[:, kc, :], rhs=H2T[:, kc, cs], start=(kc == 0), stop=(kc == 7))
                                return r
                            T.op("pe", fu, reads=["H2T", wn], writes=[f"ps{pbk}"])
                            T.op("act", lambda e, pbk=pbk, ub=ub, hb_=hb_, nt_=nt_: e.activation(
                                out=ub[:, hb_ * 4:hb_ * 4 + nt_, 2:130], in_=ps[pbk][:, 0:nt_ * 128].rearrange("p (t n) -> p t n", n=128), func=AF.Copy),
                                reads=[f"ps{pbk}"], writes=[f"upsb{part}"])
                        def ft(e, wbuf=wbuf, t0=t0):
                            r = None
                            for kc in range(8):
                                r = e.matmul(out=ps[4][:, 0:TP * 2].rearrange("p (t n) -> p t n", n=2), lhsT=wbuf[:, kc, :], rhs=halT[:, kc, t0:t0 + TP, :], start=(kc == 0), stop=(kc == 7))
                            return r
                        T.op("pe", ft, reads=["halT", wn], writes=["ps4"])
                        T.op("act", lambda e, ub=ub: e.activation(out=ub[:, 0:TP, 0:2], in_=ps[4][:, 0:TP * 2].rearrange("p (t n) -> p t n", n=2), func=AF.Copy),
                             reads=["ps4"], writes=[f"upsb{part}"])
                        ac = acc[part]
                        T.op("dve", lambda e, ub=ub, ac=ac, fc=fc: e.tensor_scalar(out=ac[:, 0:TP, :], in0=ub[:, 0:TP, 2:130], scalar1=fcw[:, fc, 2:3], scalar2=fcw[:, fc, 3:4], op0=ALU.mult, op1=ALU.add),
                             reads=[f"upsb{part}", "fcw"], writes=[f"acc{part}"])
                        for k in range(2):
                            T.op("dve", lambda e, ub=ub, ac=ac, fc=fc, k=k: e.scalar_tensor_tensor(out=ac[:, 0:TP, :], in0=ub[:, 0:TP, k:128 + k], scalar=fcw[:, fc, k:k + 1], in1=ac[:, 0:TP, :], op0=ALU.mult, op1=ALU.add),
                                 reads=[f"upsb{part}", "fcw", f"acc{part}"], writes=[f"acc{part}"])
                        if part == 0:
                            T.op("act", lambda e, ac=ac: e.activation(out=gact[:, 0:TP, :], in_=ac[:, 0:TP, :], func=AF.Gelu_apprx_tanh), reads=["acc0"], writes=["gact"])
                        else:
                            T.op("pool", lambda e, ac=ac, j=j: e.tensor_tensor(out=GT[:, j, 0:TP * 128].rearrange("p (t n) -> p t n", n=128), in0=gact[:, 0:TP, :], in1=ac[:, 0:TP, :], op=ALU.mult),
                                 reads=["gact", "acc1"], writes=["GT"])
                for tt in range(TP):
                    m = t0 + tt
                    b = m % 2
                    T.dma("sp", lambda e, m=m, b=b: e.dma_start(out=xt[b][:], in_=xsrc[m]), reads=[f"xmid{m}"], writes=[f"xt{b}"], slot=f"xt{b}")
                    for nh in range(2):
                        def fd(e, nh=nh, tt=tt):
                            r = None
                            for j in range(22):
                                r = e.matmul(out=ps[5 + nh][:], lhsT=GT[:, j, tt * 128:(tt + 1) * 128], rhs=WDN[:, j, nh * 512:(nh + 1) * 512], start=(j == 0), stop=(j == 21))
                            return r
                        T.op("pe", fd, reads=["GT", "wdn0", "wdn1", "wdn2", "wdn3"], writes=[f"ps{5 + nh}"])
                        T.op("dve", lambda e, nh=nh, b=b: e.tensor_copy(out=w1[b][:, nh * 512:(nh + 1) * 512], in_=ps[5 + nh][:]), reads=[f"ps{5 + nh}"], writes=[f"w1_{b}"])
                    r, rn = rstd_from(w1[b][:], f"w1_{b}", D)
                    T.op("dve", lambda e, b=b, r=r: e.scalar_tensor_tensor(out=w1[b][:], in0=w1[b][:], scalar=r, in1=gv[:, 0, :], op0=ALU.mult, op1=ALU.mult),
                         reads=[f"w1_{b}", rn, "gv0"], writes=[f"w1_{b}"])
                    T.op("dve", lambda e, b=b: e.tensor_tensor(out=xt[b][:], in0=xt[b][:], in1=w1[b][:], op=ALU.add), reads=[f"xt{b}", f"w1_{b}"], writes=[f"xt{b}"])
                    T.dma("sp", lambda e, m=m, b=b: e.dma_start(out=xdst[m], in_=xt[b][:]), reads=[f"xt{b}"], writes=[f"xo{m}"], slot=f"xo{b}")

        wup_t = [sb(f"wup{i}", [128, 8, 128], BF16) for i in range(3)]

        def dump(which):
            T.barrier()
            if which in ("A0", "A1"):
                T.dma("sp", lambda e: e.dma_start(out=dbg["q"], in_=QT[:]), reads=["QT"], writes=["dq"], slot="dbg0")
                T.dma("sp", lambda e: e.dma_start(out=dbg["ct"], in_=CT[:]), reads=["CT"], writes=["dc"], slot="dbg1")
                for i in range(2):
                    T.dma("sp", lambda e, i=i: e.dma_start(out=dbg["gl"].rearrange("p i (t n) -> p i t n", n=128)[:, i], in_=GL[:, i, :, 32:160]), reads=["GL"], writes=[f"dg{i}"], slot=f"dbg2{i}")
                T.dma("sp", lambda e: e.dma_start(out=dbg["kg"], in_=Kg), reads=["Kg"], writes=["dk"], slot="dbg3")
                T.dma("sp", lambda e: e.dma_start(out=dbg["vg"], in_=Vg), reads=["Vg"], writes=["dv"], slot="dbg4")
            if which in ("B0", "B1"):
                T.dma("sp", lambda e: e.dma_start(out=dbg["ct"], in_=CT[:]), reads=["CT"], writes=["dc"], slot="dbg1")
                T.dma("sp", lambda e: e.dma_start(out=dbg["h2"], in_=H2T[:]), reads=["H2T"], writes=["dh"], slot="dbg2")
                for m in range(NT):
                    T.dma("sp", lambda e, m=m: e.dma_start(out=y_out[m], in_=xbuf[0][m]), reads=[f"xmid{m}"], writes=[f"y{m}"], slot=f"dy{m % 4}")
            T.barrier()

        def convert(src, dst_c, dst_g, rows, step, name):
            pieces = []
            for i, r0 in enumerate(range(0, rows, step)):
                T.dma("pool", lambda e, r0=r0: e.dma_start(out=dst_c[r0:r0 + step, :].rearrange("p (a b) -> p a b", b=512),
                                                         in_=src[r0:r0 + step, :].rearrange("p (a b) -> p a b", b=512)),
                      writes=[f"{name}c{i}"], slot=f"cv{name}{i % 4}")
                pieces.append(f"{name}c{i}")
            gather(dst_c, dst_g, pieces, name)
        convert(win_part, Wc_in, Wg_in, 256, 128, "Wg_in")

        done = False
        for l in range(DEPTH):
            xsrc = x_in if l == 0 else xbuf[1]
            layer_consts(l)
            phase_A(l, xsrc)
            if l == 0:
                for i in range(11):
                    T.dma("pool", lambda e, i=i: e.dma_start(out=Wc[i * 128:(i + 1) * 128, :].rearrange("p (a b) -> p a b", b=512),
                                                           in_=wup_part[i].rearrange("p (a b) -> p a b", b=512)),
                          writes=[f"Wc{i}"], slot=f"wc{i % 4}")
                gather(Wc, Wg, [f"Wc{i}" for i in range(11)], "Wg")
                convert(wout_part, Wc_out, Wg_out, 256, 128, "Wg_out")
                convert(wdn_part, Wc_dn, Wg_dn, 704, 64, "Wg_dn")
            T.barrier()
            gather(Kc, Kg, [f"Kc{g}_{h}" for g in range(NT // 4) for h in range(4)], "Kg")
            gather(Vc, Vg, [f"Vc{m}" for m in range(NT)], "Vg")
            gather(Hc, Hg, ["Hc0", "Hc1"], "Hg")
            T.barrier()
            if stop_after == f"A{l}":
                dump(stop_after); done = True; break
            if stop_after in (f"P{l}", f"Q{l}"):
                phase_B(l, xsrc, xbuf[0], upto=1 if stop_after[0] == "P" else 2)
                T.barrier()
                T.dma("sp", lambda e: e.dma_start(out=dbg["ct"], in_=CT[:]), reads=["CT"], writes=["dc"], slot="dbg1")
                T.barrier()
                done = True
                break
            phase_B(l, xsrc, xbuf[0])
            T.barrier()
            gather(Tc, Tg, ["Tc"], "Tg")
            T.barrier()
            if stop_after == f"B{l}":
                dump(stop_after); done = True; break
            phase_C(l, xbuf[0], y_out if l == DEPTH - 1 else xbuf[1])
            T.barrier()
            if stop_after == f"C{l}":
                for m in range(NT):
                    T.dma("sp", lambda e, m=m: e.dma_start(out=y_out[m], in_=xbuf[1][m]), reads=[f"xo{m}"], writes=[f"y{m}"], slot=f"dy{m % 4}")
                T.barrier()
                done = True
                break
        T.barrier()
        T.run(block)
    return nc


def host_inputs(inputs, NT=16):
    f = lambda a: np.ascontiguousarray(np.asarray(a, dtype=np.float32))
    x = f(inputs["x"])[0]
    S = x.shape[0]
    nblk = S // 128
    assert nblk == NT * 8
    xb = x.reshape(NT, 8, 128, D)
    bc = lambda a, shape: np.ascontiguousarray(np.broadcast_to(a, shape))
    gvec = np.stack([f(inputs[k]) for k in ("pre_mix_g", "post_mix_g", "pre_ffn_g", "post_ffn_g")], 1)
    gvec = bc(gvec[:, None], (DEPTH, 128, 4, D))
    gmv = np.stack([f(inputs["gm_ln_g"]), f(inputs["gm_ln_b"])], 1)
    gmv = bc(gmv[:, None], (DEPTH, 128, 2, 256))
    gm_wT = np.ascontiguousarray(f(inputs["gm_w_s"]).transpose(0, 3, 1, 2))
    gm_bs = np.ascontiguousarray(np.repeat(f(inputs["gm_b_s"]).transpose(0, 2, 1), 64, axis=2))
    tri = (np.arange(128)[:, None] <= np.arange(128)[None, :]).astype(np.float32)
    lam_in = np.stack([f(inputs[k]) for k in ("da_lq1", "da_lk1", "da_lq2", "da_lk2")], 1)
    lam_in = bc(lam_in[:, None], (DEPTH, 128, 4, 64))
    subln = bc(f(inputs["da_subln_g"])[:, None], (DEPTH, 128, 128))
    rb = f(inputs["rel_bias"])
    cfar = bc(rb[31][None], (128, 4))
    cvw = np.ascontiguousarray(f(inputs["cv_dw_w"]).reshape(DEPTH, 31, 2, 128).transpose(0, 3, 2, 1))
    cvp = np.stack([f(inputs[k]).reshape(DEPTH, 2, 128) for k in ("cv_dw_b", "cv_ln_g", "cv_ln_b")], -1)
    cvp = np.ascontiguousarray(cvp.transpose(0, 2, 1, 3))
    gavg = np.zeros((128, 128), np.float32)
    gavg[:64, :64] = 1.0 / 64
    gavg[64:, 64:] = 1.0 / 64
    fw = f(inputs["ffn_conv_w"]).reshape(DEPTH, 3, NFC, 128)
    fb = f(inputs["ffn_conv_b"]).reshape(DEPTH, 1, NFC, 128)
    fcw = np.ascontiguousarray(np.concatenate([fw, fb], 1).transpose(0, 3, 2, 1))
    ident = np.eye(128, dtype=np.float32)
    wup = f(inputs["ffn_w_up"])
    win_f = f(inputs["w_in"]).reshape(DEPTH * D, INW)
    wout_f = f(inputs["w_out"]).reshape(DEPTH * D, D)
    wdn_f = f(inputs["ffn_w_down"]).reshape(DEPTH * DFF, D)
    common = dict(
                  gvec=gvec, gmv=gmv, gm_wT=gm_wT, gm_bs=gm_bs, tri=tri, lam_in=lam_in, subln=subln, cfar=cfar,
                  cvw=cvw, cvp=cvp, gavg=gavg, fcw=fcw, ident=ident)
    k = np.arange(128)[:, None, None]
    n = np.arange(12)[None, :, None]
    q = np.arange(128)[None, None, :]
    maps = []
    for c in range(NCORES):
        rel = (c + 4 - n) * 128 + q - k
        idx = t5_bucket(rel)
        tab = rb[idx]
        tab = np.where((rel >= 0)[..., None], tab, np.float32(NEG)).astype(np.float32)
        btab = np.ascontiguousarray(tab.transpose(3, 0, 1, 2))
        sel = np.zeros((128, 8), np.float32)
        sel[:, c] = 1.0
        parts = []
        for i in range(11):
            l_, fc = divmod(c * 11 + i, NFC)
            part, j = divmod(fc, 22)
            col = part * DFF + j * 128
            parts.append(wup[l_][:, col:col + 128].reshape(8, 128, 128).transpose(1, 0, 2).reshape(128, 1024))
        d = dict(common)
        d.update(x=np.ascontiguousarray(xb[:, c]), btab=btab, sel=sel, wup_part=np.ascontiguousarray(np.stack(parts)),
                 win_part=np.ascontiguousarray(win_f[c * 256:(c + 1) * 256]), wout_part=np.ascontiguousarray(wout_f[c * 256:(c + 1) * 256]),
                 wdn_part=np.ascontiguousarray(wdn_f[c * 704:(c + 1) * 704]))
        maps.append(d)
    return maps


_CACHE = {}


def kernel(**inputs):
    NT = 16
    if "nc" not in _CACHE:
        _CACHE["nc"] = build(NT)
    nc = _CACHE["nc"]
    maps = host_inputs(inputs, NT)
    res = run_bass_kernel_spmd(nc, maps, core_ids=list(range(NCORES)))
    out = np.zeros((NT, 8, 128, D), np.float32)
    for c in range(NCORES):
        out[:, c] = np.asarray(res.results[c]["y"])
    return out.reshape(1, NT * 8 * 128, D)
```

```python
import math
from contextlib import ExitStack
import numpy as np
import concourse.bass as bass
import concourse.mybir as mybir
from concourse.bass_utils import run_bass_kernel_spmd

F32, BF16 = mybir.dt.float32, mybir.dt.bfloat16
AF = mybir.ActivationFunctionType
ALU = mybir.AluOpType
NCORES = 8
D = 1024
INW = 2560
DFF = 2816
NFC = 44
EPS = 1e-6
NEG = -30000.0
DEPTH = 2
ENG = ("pe", "act", "dve", "pool", "sp")


class Tracker:
    def __init__(self, nc, stack):
        self.nc, self.stack = nc, stack
        self.ops = {e: [] for e in ENG}
        self.sem, self.cnt = {}, {}
        self.waited = {e: {} for e in ENG}
        self.lastw, self.readers = {}, {}
        self.mute = False
        for e in ENG:
            self._mk("E" + e)

    def _mk(self, name):
        if name not in self.sem:
            self.sem[name] = self.stack.enter_context(self.nc.semaphore(name))
            self.cnt[name] = 0
        return name

    def _deps(self, reads, writes):
        deps = {}
        def add(tok):
            if tok is not None:
                deps[tok[0]] = max(deps.get(tok[0], 0), tok[1])
        for b in reads:
            add(self.lastw.get(b))
        for b in writes:
            add(self.lastw.get(b))
            for s, v in self.readers.get(b, {}).items():
                add((s, v))
        return deps

    def _waits(self, eng, deps):
        w = []
        for s, v in deps.items():
            if eng == "pe" and s == "Epe":
                continue
            if self.waited[eng].get(s, 0) < v:
                self.waited[eng][s] = v
                w.append((self.sem[s], v))
        return w

    def _commit(self, tok, reads, writes):
        for b in reads:
            self.readers.setdefault(b, {})[tok[0]] = tok[1]
        for b in writes:
            self.lastw[b] = tok
            self.readers[b] = {}

    def op(self, eng, fn, reads=(), writes=()):
        if self.mute:
            return
        w = self._waits(eng, self._deps(reads, writes))
        s = "E" + eng
        self.cnt[s] += 1
        tok = (s, self.cnt[s])
        sem = self.sem[s]
        def emit(e, fn=fn, w=w, sem=sem):
            for sh, v in w:
                e.wait_ge(sh, v)
            fn(e).then_inc(sem, 1)
        self.ops[eng].append(emit)
        self._commit(tok, reads, writes)

    def dma(self, eng, fn, reads=(), writes=(), slot=None, inc=16):
        if self.mute:
            return
        w = self._waits(eng, self._deps(reads, writes))
        s = self._mk("D" + slot)
        self.cnt[s] += inc
        tok = (s, self.cnt[s])
        sem = self.sem[s]
        def emit(e, fn=fn, w=w, sem=sem, inc=inc):
            for sh, v in w:
                e.wait_ge(sh, v)
            fn(e).then_inc(sem, inc)
        self.ops[eng].append(emit)
        self._commit(tok, reads, writes)

    def barrier(self):
        allv = {s: v for s, v in self.cnt.items() if v > 0}
        for eng in ENG:
            w = self._waits(eng, dict(allv))
            def emit(e, w=w):
                for sh, v in w:
                    e.wait_ge(sh, v)
            self.ops[eng].append(emit)

    def run(self, block):
        for name, meth in (("pe", block.tensor), ("act", block.scalar), ("dve", block.vector),
                           ("pool", block.gpsimd), ("sp", block.sync)):
            ops = self.ops[name]
            def body(e, ops=ops):
                for o in ops:
                    o(e)
            meth(body)


def t5_bucket(n):
    n = np.maximum(n, 0)
    nf = np.maximum(n, 1).astype(np.float32)
    large = 16 + (np.log(nf / np.float32(16)) / np.float32(math.log(128 / 16)) * np.float32(16)).astype(np.int32)
    large = np.minimum(large, 31)
    return np.where(n < 16, n, large)


def build(NT=16, stop_after=None):
    S_LOC = NT * 128
    NBLK = NT * 8
    nc = bass.Bass("TRN2", target_bir_lowering=False)
    dt_in = lambda name, shape, dt=F32: nc.dram_tensor(name, list(shape), dt, kind="ExternalInput").ap()
    x_in = dt_in("x", [NT, 128, D])
    win_part = dt_in("win_part", [256, INW])
    wout_part = dt_in("wout_part", [256, D])
    wup_part = dt_in("wup_part", [11, 128, 1024])
    wdn_part = dt_in("wdn_part", [704, D])
    gvec = dt_in("gvec", [DEPTH, 128, 4, D])
    gmv = dt_in("gmv", [DEPTH, 128, 2, 256])
    gm_wT = dt_in("gm_wT", [DEPTH, 128, 4, 128])
    gm_bs = dt_in("gm_bs", [DEPTH, 128, 256])
    tri_in = dt_in("tri", [128, 128])
    lam_in = dt_in("lam_in", [DEPTH, 128, 4, 64])
    subln_in = dt_in("subln", [DEPTH, 128, 128])
    btab_in = dt_in("btab", [4, 128, 12, 128])
    cfar_in = dt_in("cfar", [128, 4])
    cvw_in = dt_in("cvw", [DEPTH, 128, 2, 31])
    cvp_in = dt_in("cvp", [DEPTH, 128, 2, 3])
    gavg_in = dt_in("gavg", [128, 128])
    fcw_in = dt_in("fcw", [DEPTH, 128, NFC, 4])
    sel_in = dt_in("sel", [128, 8])
    ident_in = dt_in("ident", [128, 128])
    y_out = nc.dram_tensor("y", [NT, 128, D], F32, kind="ExternalOutput").ap()
    dbg = {}
    if stop_after is not None:
        dbg["q"] = nc.dram_tensor("dbg_q", [128, 4, S_LOC], BF16, kind="ExternalOutput").ap()
        dbg["ct"] = nc.dram_tensor("dbg_ct", [128, 8, S_LOC], BF16, kind="ExternalOutput").ap()
        dbg["gl"] = nc.dram_tensor("dbg_gl", [128, 2, S_LOC], F32, kind="ExternalOutput").ap()
        dbg["kg"] = nc.dram_tensor("dbg_kg", [NCORES * 512, S_LOC], BF16, kind="ExternalOutput").ap()
        dbg["vg"] = nc.dram_tensor("dbg_vg", [NCORES * S_LOC, 512], BF16, kind="ExternalOutput").ap()
        dbg["h2"] = nc.dram_tensor("dbg_h2", [128, 8, S_LOC], BF16, kind="ExternalOutput").ap()

    xbuf = [nc.dram_tensor(f"xbuf{i}", [NT, 128, D], F32).ap() for i in range(2)]
    Kc = nc.dram_tensor("Kc", [512, S_LOC], BF16).ap()
    Vc = nc.dram_tensor("Vc", [S_LOC, 512], BF16).ap()
    Hc = nc.dram_tensor("Hc", [128, 2 * NT * 32], F32).ap()
    Tc = nc.dram_tensor("Tc", [128, 8 * NT * 2], BF16).ap()
    Wc_in = nc.dram_tensor("Wc_in", [256, INW], BF16).ap()
    Wg_in = nc.dram_tensor("Wg_in", [NCORES * 256, INW], BF16, addr_space="Shared").ap()
    Wc_out = nc.dram_tensor("Wc_out", [256, D], BF16).ap()
    Wg_out = nc.dram_tensor("Wg_out", [NCORES * 256, D], BF16, addr_space="Shared").ap()
    Wc_dn = nc.dram_tensor("Wc_dn", [704, D], BF16).ap()
    Wg_dn = nc.dram_tensor("Wg_dn", [NCORES * 704, D], BF16, addr_space="Shared").ap()
    Wc = nc.dram_tensor("Wc", [11 * 128, 1024], BF16).ap()
    Wg = nc.dram_tensor("Wg", [NCORES * 11 * 128, 1024], BF16, addr_space="Shared").ap()
    Kg = nc.dram_tensor("Kg", [NCORES * 512, S_LOC], BF16, addr_space="Shared").ap()
    Vg = nc.dram_tensor("Vg", [NCORES * S_LOC, 512], BF16, addr_space="Shared").ap()
    Hg = nc.dram_tensor("Hg", [NCORES * 128, 2 * NT * 32], F32, addr_space="Shared").ap()
    Tg = nc.dram_tensor("Tg", [NCORES * 128, 8 * NT * 2], BF16, addr_space="Shared").ap()

    with ExitStack() as st:
        sb = lambda name, shape, dt=F32: st.enter_context(nc.sbuf_tensor("s_" + name, list(shape), dt))
        psS = [st.enter_context(nc.psum_tensor(f"psS{i}", [128, 1024], F32)) for i in range(2)]
        ps = [psS[0][:, 0:512], psS[0][:, 512:1024], psS[1][:, 0:512], psS[1][:, 512:1024]] + \
             [st.enter_context(nc.psum_tensor(f"ps{i}", [128, 512], F32)) for i in range(4, 7)]
        psT = st.enter_context(nc.psum_tensor("psT", [128, 1024], BF16))
        ARENA = sb("ARENA", [128, 63488], BF16)
        ident = sb("ident", [128, 128], BF16)
        identf = sb("identf", [128, 128], F32)
        gv = sb("gv", [128, 2, D], F32)
        wT = sb("wT", [128, 4, 128], BF16)
        wTf = sb("wTf", [128, 4, 128], F32)
        tri = sb("tri", [128, 128], F32)
        gmvt = sb("gmvt", [128, 2, 256], F32)
        gbs = sb("gbs", [128, 256], F32)
        lamt = sb("lamt", [128, 4, 64], F32)
        lamw = sb("lamw", [128, 8], F32)
        gsub = sb("gsub", [128, 128], F32)
        cfar = sb("cfar", [128, 4], F32)
        btab = sb("btab", [128, 12, 128], F32)
        cvw = sb("cvw", [128, 2, 31], F32)
        cvp = sb("cvp", [128, 2, 3], F32)
        gavg = sb("gavg", [128, 128], F32)
        fcw = sb("fcw", [128, NFC, 4], F32)
        sel = sb("sel", [128, 8], F32)
        stt = sb("stt", [128, 64], F32)
        bnst = sb("bnst", [128, 2, 8], F32)
        xt = [sb(f"xt{i}", [128, D], F32) for i in range(2)]
        junk = sb("junk", [128, D], BF16)
        hb = [sb(f"hb{i}", [128, D], BF16) for i in range(2)]
        w1 = [sb(f"w1_{i}", [128, D], F32) for i in range(2)]
        w2all = sb("w2all", [128, 3, 512], F32)
        w2 = [w2all[:, i, :] for i in range(3)]
        wball = sb("wball", [128, 2, 1024], BF16)
        wb = [wball[:, 0, 0:512], wball[:, 0, 512:1024], wball[:, 1, 0:512], wball[:, 1, 512:1024]]
        candT = sb("candT", [128, 8, 8 * NT * 2], BF16)
        halT = sb("halT", [128, 8, NT, 2], BF16)
        halTf = sb("halTf", [128, 8 * NT * 2], F32)
        tailT = sb("tailT", [128, 8, NT, 2], BF16)
        upsb = [sb(f"upsb{i}", [128, 8, 130], F32) for i in range(2)]
        acc1 = sb("acc1", [128, 8, 128], F32)
        acc = [w2all[:, 0:2, :].rearrange("p a (t n) -> p (a t) n", n=128), acc1[:]]
        gact = btab[:, 0:8, :]
        cvq = w2[2]

        block = st.enter_context(nc.Block())
        T = Tracker(nc, st)

        WIN = ARENA[:, 0:8 * INW].rearrange("p (k n) -> p k n", k=8)
        HT = [ARENA[:, 20480 + i * 4096: 20480 + (i + 1) * 4096].rearrange("p (k n) -> p k n", k=8) for i in range(2)] + \
             [ARENA[:, 43008 + i * 4096: 43008 + (i + 1) * 4096].rearrange("p (k n) -> p k n", k=8) for i in range(2)]
        GL = ARENA[:, 28672:38912].bitcast(F32).rearrange("p (i t n) -> p i t n", i=2, n=160)[:, :, 0:NT, :]
        CT = ARENA[:, 38912:55296].rearrange("p (k n) -> p k n", k=8)[:, :, 0:S_LOC]
        QT = ARENA[:, 55296:63488].rearrange("p (k n) -> p k n", k=4)[:, :, 0:S_LOC]
        cand = ARENA[:, 0:16384].bitcast(F32).rearrange("p (r n) -> p r n", r=8)[:, :, 0:2 * NT * 32]
        cvy = ARENA[:, 16384:24576].bitcast(F32).rearrange("p (i n) -> p i n", i=2)[:, :, 0:S_LOC]
        KT = ARENA[:, 0:NBLK * 128].rearrange("p (j n) -> p j n", n=128)
        VH = ARENA[:, 16384:16384 + NBLK * 130].rearrange("p (j n) -> p j n", n=130)
        QP = ARENA[:, 33280:37376].rearrange("p (a n) -> p a n", a=2)[:, :, 0:S_LOC]
        WOUT = ARENA[:, 16384:24576].rearrange("p (k n) -> p k n", k=8)
        H2T = ARENA[:, 0:16384].rearrange("p (k n) -> p k n", k=8)[:, :, 0:S_LOC]
        GT = ARENA[:, 16384:38912].rearrange("p (j n) -> p j n", j=22)
        WDN = ARENA[:, 38912:61440].rearrange("p (j n) -> p j n", j=22)

        ARW = [f"arenaW{k}" for k in range(8)]
        KTN = [f"KT{c}" for c in range(8)]
        VHN = [f"VH{c}" for c in range(8)]
        CANDN = ["cand00", "cand01"] + [f"cand{r}" for r in range(1, 8)]
        stat_i = [0]
        def newstat():
            stat_i[0] = (stat_i[0] + 1) % 64
            i = stat_i[0]
            return stt[:, i:i + 1], f"st{i}"

        def rstd_from(src_ap, srcname, n, eng_sq="act"):
            ss, ssn = newstat()
            T.op("act", lambda e: e.activation(out=junk[:, 0:n], in_=src_ap, func=AF.Square, accum_out=ss),
                 reads=[srcname], writes=["junk", ssn])
            return rstd_of(ss, ssn, 1.0 / n)

        def rstd_of(v, vn, scale):
            lnv, lnn = newstat()
            T.op("pool", lambda e: e.tensor_scalar(out=lnv, in0=v, scalar1=scale, scalar2=EPS, op0=ALU.mult, op1=ALU.add),
                 reads=[vn], writes=[lnn])
            r, rn = newstat()
            T.op("pool", lambda e: e.tensor_tensor(out=r, in0=lnv, in1=mhalf[:, 0:1], op=ALU.pow), reads=[lnn, "mhalf"], writes=[rn])
            return r, rn

        epsT = sb("epsT", [128, 1], F32)
        T.op("dve", lambda e: e.memset(epsT[:], EPS), writes=["epsT"])
        mhalf = sb("mhalf", [128, 1], F32)
        T.op("dve", lambda e: e.memset(mhalf[:], -0.5), writes=["mhalf"])

        def ld(dst, src, name, eng="sp"):
            T.dma(eng, lambda e: e.dma_start(out=dst, in_=src), writes=[name], slot="c_" + name)

        ld(identf[:], ident_in, "identf")
        T.op("dve", lambda e: e.tensor_copy(out=ident[:], in_=identf[:]), reads=["identf"], writes=["ident"])
        ld(tri[:], tri_in, "tri")
        ld(cfar[:], cfar_in, "cfar")
        ld(gavg[:], gavg_in, "gavg")
        ld(sel[:], sel_in, "sel")

        def transpose_to(dst_ap, dstname, src_tile, srcname, nchunk, psname="psT"):
            def f(e):
                r = None
                for k in range(nchunk):
                    r = e.transpose(out=psT[:, k * 128:(k + 1) * 128], in_=src_tile[:, k * 128:(k + 1) * 128], identity=ident[:])
                return r
            T.op("pe", f, reads=[srcname, "ident"], writes=[psname])
            T.op("act", lambda e: e.activation(out=dst_ap, in_=psT[:, 0:nchunk * 128].rearrange("p (k n) -> p k n", k=nchunk), func=AF.Copy),
                 reads=[psname], writes=[dstname])

        def layer_consts(l):
            ld(wTf[:], gm_wT[l], "wTf")
            for h in range(4):
                T.op("dve", lambda e, h=h: e.tensor_tensor(out=wT[:, h, :], in0=wTf[:, h, :], in1=tri[:], op=ALU.mult),
                     reads=["wTf", "tri"], writes=["wT"])
            ld(gmvt[:], gmv[l], "gmvt")
            ld(gbs[:], gm_bs[l], "gbs")
            ld(lamt[:], lam_in[l], "lamt")
            ld(gsub[:], subln_in[l], "gsub")
            ld(cvw[:], cvw_in[l], "cvw")
            ld(cvp[:], cvp_in[l], "cvp")
            ld(fcw[:], fcw_in[l], "fcw")
            lam_init = 0.8 - 0.6 * math.exp(-0.3 * l)
            for i in range(2):
                T.op("dve", lambda e, i=i: e.tensor_tensor(out=junk[:, i * 64:(i + 1) * 64], in0=lamt[:, 2 * i, :], in1=lamt[:, 2 * i + 1, :], op=ALU.mult),
                     reads=["lamt"], writes=["junk"])
                T.op("dve", lambda e, i=i: e.reduce_sum(out=lamw[:, i:i + 1], in_=junk[:, i * 64:(i + 1) * 64], axis=mybir.AxisListType.X),
                     reads=["junk"], writes=["lamw"])
            T.op("act", lambda e: e.activation(out=lamw[:, 4:6], in_=lamw[:, 0:2], func=AF.Exp), reads=["lamw"], writes=["lamw"])
            T.op("dve", lambda e: e.tensor_tensor(out=lamw[:, 6:7], in0=lamw[:, 5:6], in1=lamw[:, 4:5], op=ALU.subtract),
                 reads=["lamw"], writes=["lamw"])
            T.op("dve", lambda e: e.tensor_scalar(out=lamw[:, 2:3], in0=lamw[:, 6:7], scalar1=-lam_init, scalar2=None, op0=ALU.add),
                 reads=["lamw"], writes=["lamw"])
            T.op("dve", lambda e: e.tensor_scalar(out=gsub[:], in0=gsub[:], scalar1=1.0 - lam_init, scalar2=None, op0=ALU.mult),
                 reads=["gsub"], writes=["gsub"])

        def phase_A(l, xsrc):
            ld(gv[:, 0, :], gvec[l][:, 0, :], "gv0")
            for kc in range(8):
                T.dma("sp", lambda e, kc=kc: e.dma_start(out=WIN[:, kc, :], in_=Wg_in[l * 1024 + kc * 128:l * 1024 + (kc + 1) * 128, :]),
                      reads=["Wg_in"], writes=[f"arenaW{kc}"], slot=f"win{kc}")
            def normsA(g):
                hT = HT[g]
                hTn = f"hT{g}"
                for t in range(4):
                    m = 4 * g + t
                    b = m % 2
                    T.dma("sp", lambda e, m=m, b=b: e.dma_start(out=xt[b][:], in_=xsrc[m]), writes=[f"xt{b}"], slot=f"xt{b}")
                    r, rn = rstd_from(xt[b][:], f"xt{b}", D)
                    T.op("dve", lambda e, b=b, r=r: e.scalar_tensor_tensor(out=hb[b][:], in0=xt[b][:], scalar=r, in1=gv[:, 0, :], op0=ALU.mult, op1=ALU.mult),
                         reads=[f"xt{b}", rn, "gv0"], writes=[f"hb{b}"])
                    transpose_to(hT[:, :, t * 128:(t + 1) * 128], hTn, hb[b], f"hb{b}", 8)
            def mmA(g, part):
                hT = HT[g]
                hTn = f"hT{g}"
                order = [("q", h, 512 + 128 * h) for h in range(4)] + [("k", h, 1024 + 128 * h) for h in range(4)] + \
                        [("cg", i, 2304 + 128 * i) for i in range(2)] + [("ca", i, 2048 + 128 * i) for i in range(2)]
                order = [o for o in order if (o[0] == "k") == (part == 1)]
                for oi, (kind, idx, col) in enumerate(order):
                    pb = oi % 2
                    def f(e, col=col, pb=pb, hT=hT):
                        r = None
                        for kc in range(8):
                            r = e.matmul(out=ps[pb][:], lhsT=WIN[:, kc, col:col + 128], rhs=hT[:, kc, :], start=(kc == 0), stop=(kc == 7))
                        return r
                    T.op("pe", f, reads=[hTn] + ARW, writes=[f"ps{pb}"])
                    tok = slice(g * 512, (g + 1) * 512)
                    if kind == "q":
                        T.op("act", lambda e, pb=pb, idx=idx, tok=tok: e.activation(out=QT[:, idx, tok], in_=ps[pb][:], func=AF.Copy),
                             reads=[f"ps{pb}"], writes=["QT"])
                    elif kind == "k":
                        wi = idx % 3
                        T.op("dve", lambda e, pb=pb, wi=wi: e.tensor_copy(out=wb[wi][:], in_=ps[pb][:]), reads=[f"ps{pb}"], writes=[f"wb{wi}"])
                        T.dma("sp", lambda e, wi=wi, idx=idx, tok=tok: e.dma_start(out=Kc[idx * 128:(idx + 1) * 128, tok], in_=wb[wi][:]),
                              reads=[f"wb{wi}"], writes=[f"Kc{g}_{idx}"], slot=f"wb{wi}")
                    elif kind == "cg":
                        T.op("act", lambda e, pb=pb, idx=idx: e.activation(out=w2[idx][:], in_=ps[pb][:], func=AF.Sigmoid),
                             reads=[f"ps{pb}"], writes=[f"w2{idx}"])
                    else:
                        T.op("dve", lambda e, pb=pb, idx=idx, g=g: e.tensor_tensor(
                            out=GL[:, idx, 4 * g:4 * g + 4, 32:160], in0=ps[pb][:].rearrange("p (t n) -> p t n", t=4),
                            in1=w2[idx][:].rearrange("p (t n) -> p t n", t=4), op=ALU.mult),
                            reads=[f"ps{pb}", f"w2{idx}"], writes=["GL"])
                for t in range(4):
                    m = 4 * g + t
                    tcols = slice(t * 128, (t + 1) * 128)
                    def fg(e, tcols=tcols, hT=hT):
                        r = None
                        for kc in range(8):
                            r = e.matmul(out=ps[2][:], lhsT=hT[:, kc, tcols], rhs=WIN[:, kc, 0:512], start=(kc == 0), stop=(kc == 7))
                        return r
                    T.mute = (part != 2)
                    T.op("pe", fg, reads=[hTn] + ARW, writes=["ps2"])
                    T.mute = (part != 1)
                    def fv(e, tcols=tcols, hT=hT):
                        r = None
                        for kc in range(8):
                            r = e.matmul(out=ps[3][:], lhsT=hT[:, kc, tcols], rhs=WIN[:, kc, 1536:2048], start=(kc == 0), stop=(kc == 7))
                        return r
                    T.op("pe", fv, reads=[hTn] + ARW, writes=["ps3"])
                    T.op("dve", lambda e: e.tensor_copy(out=wb[2][:], in_=ps[3][:]), reads=["ps3"], writes=["wb2"])
                    T.dma("sp", lambda e, m=m: e.dma_start(out=Vc[m * 128:(m + 1) * 128, :], in_=wb[2][:]),
                          reads=["wb2"], writes=[f"Vc{m}"], slot="wb2")
                    T.mute = (part != 2)
                    ug = w1[0]
                    T.op("act", lambda e: e.activation(out=ug[:, 0:512], in_=ps[2][:], func=AF.Gelu_apprx_tanh), reads=["ps2"], writes=["w1_0"])
                    T.op("dve", lambda e: e.bn_stats(out=bnst[:, 0, 0:6], in_=ug[:, 256:512]), reads=["w1_0"], writes=["bnst0"])
                    T.op("dve", lambda e: e.bn_aggr(out=bnst[:, 1, 0:2], in_=bnst[:, 0, 0:6]), reads=["bnst0"], writes=["bnst1"])
                    r, rn = rstd_of(bnst[:, 1, 1:2], "bnst1", 1.0)
                    vn = w1[0][:, 512:768]
                    T.op("dve", lambda e, r=r: e.tensor_scalar(out=vn, in0=ug[:, 256:512], scalar1=bnst[:, 1, 0:1], scalar2=r, op0=ALU.subtract, op1=ALU.mult),
                         reads=["w1_0", "bnst1", rn], writes=["w1_0b"])
                    T.op("dve", lambda e: e.tensor_tensor(out=vn, in0=vn, in1=gmvt[:, 0, :], op=ALU.mult), reads=["w1_0b", "gmvt"], writes=["w1_0b"])
                    T.op("dve", lambda e: e.tensor_tensor(out=wb[3][:, 0:256], in0=vn, in1=gmvt[:, 1, :], op=ALU.add), reads=["w1_0b", "gmvt"], writes=["wb3a"])
                    def fs(e):
                        r = None
                        for h in range(4):
                            r = e.matmul(out=ps[4][:, h * 64:(h + 1) * 64], lhsT=wT[:, h, :], rhs=wb[3][:, h * 64:(h + 1) * 64], start=True, stop=True)
                        return r
                    T.op("pe", fs, reads=["wb3a", "wT"], writes=["ps4"])
                    T.op("dve", lambda e: e.tensor_tensor(out=w1[0][:, 768:1024], in0=ps[4][:, 0:256], in1=gbs[:], op=ALU.add),
                         reads=["ps4", "gbs"], writes=["w1_0c"])
                    T.op("dve", lambda e: e.tensor_tensor(out=wb[3][:, 256:512], in0=w1[0][:, 768:1024], in1=ug[:, 0:256], op=ALU.mult),
                         reads=["w1_0c", "w1_0"], writes=["wb3b"])
                    transpose_to(CT[:, 0:2, m * 128:(m + 1) * 128], "CT", wb[3][:, 256:512], "wb3b", 2)
                    T.mute = False
            NG = NT // 4
            normsA(0)
            for g in range(NG):
                if g + 1 < NG:
                    normsA(g + 1)
                mmA(g, 1)
            gather(Kc, Kg, [f"Kc{g}_{h}" for g in range(NG) for h in range(4)], "Kg")
            gather(Vc, Vg, [f"Vc{m}" for m in range(NT)], "Vg")
            for g in range(NG):
                mmA(g, 2)
            for i in range(2):
                T.dma("sp", lambda e, i=i: e.dma_start(out=Hc.rearrange("p (i t n) -> p i t n", i=2, t=NT)[:, i], in_=GL[:, i, :, 128:160]),
                      reads=["GL"], writes=[f"Hc{i}"], slot=f"hc{i}")

        def gather(src, dst, reads, name):
            T.dma("pool", lambda e: e.collective_compute("AllGather", ALU.bypass, replica_groups=[list(range(NCORES))], ins=[src], outs=[dst]),
                  reads=reads, writes=[name], slot="cc_" + name, inc=1)

        def halo_select(dst_f32, dstname, cand_t, candname, width):
            T.op("dve", lambda e: e.tensor_scalar(out=dst_f32, in0=cand_t[:, 0, :], scalar1=sel[:, 0:1], scalar2=None, op0=ALU.mult),
                 reads=candname + ["sel"], writes=[dstname])
            for r_ in range(1, 8):
                T.op("dve", lambda e, r_=r_: e.scalar_tensor_tensor(out=dst_f32, in0=cand_t[:, r_, :], scalar=sel[:, r_:r_ + 1], in1=dst_f32, op0=ALU.mult, op1=ALU.add),
                     reads=candname + ["sel", dstname], writes=[dstname])

        def phase_B(l, xsrc, xdst, upto=3):
            kg_names = [f"Kg"]
            T.op("pool", lambda e: e.memset(cand[:, 0, :], 0.0), writes=["cand00", "cand01"])
            cview = cand[:].rearrange("p r (i t n) -> p r i t n", i=2, t=NT)
            hgv = Hg.rearrange("(r p) (i t n) -> r p i t n", p=128, i=2, t=NT)
            if NT > 1:
                for i in range(2):
                    T.dma("sp", lambda e, i=i: e.dma_start(out=cview[:, 0, i, 1:NT, :], in_=hgv[7, :, i, 0:NT - 1, :]), reads=["Hg"], writes=[f"cand0{i}"], slot=f"cand0{i}")
            for r_ in range(1, 8):
                T.dma("sp", lambda e, r_=r_: e.dma_start(out=cand[:, r_, :], in_=Hg[(r_ - 1) * 128:r_ * 128, :]), reads=["Hg"], writes=[f"cand{r_}"], slot=f"cand{r_}")
            hsel = cvy[:, 0, 0:2 * NT * 32]
            halo_select(hsel, "cvy", cand, CANDN, 2 * NT * 32)
            T.op("dve", lambda e: e.tensor_copy(out=GL[:, :, :, 0:32], in_=hsel.rearrange("p (i t n) -> p i t n", i=2, t=NT)),
                 reads=["cvy"], writes=["GL"])
            GLb = ARENA[:, 0:5120].rearrange("p (i t n) -> p i t n", i=2, n=160)[:, :, 0:NT, :]
            DIAG = ARENA[:, 5120:5120 + 31 * 128].rearrange("p (k n) -> p k n", k=31)
            T.op("dve", lambda e: e.tensor_copy(out=GLb[:, 0], in_=GL[:, 0]), reads=["GL"], writes=["GLb0"] + CANDN)
            T.op("act", lambda e: e.activation(out=GLb[:, 1], in_=GL[:, 1], func=AF.Copy), reads=["GL"], writes=["GLb1"] + CANDN)
            for i in range(2):
                for k in range(31):
                    T.op("dve", lambda e, i=i, k=k: e.tensor_scalar(out=DIAG[:, k, :], in0=identf[:], scalar1=cvw[:, i, k:k + 1], scalar2=None, op0=ALU.mult),
                         reads=["identf", "cvw"], writes=["diag"] + CANDN)
                for g4 in range(NT // 4):
                    pb = 2 + (g4 % 2)
                    def fcv(e, i=i, g4=g4, pb=pb):
                        r = None
                        for k in range(31):
                            r = e.matmul(out=ps[pb][:].rearrange("p (t n) -> p t n", t=4), lhsT=DIAG[:, k, :],
                                         rhs=GLb[:, i, 4 * g4:4 * g4 + 4, 2 + k:130 + k], start=(k == 0), stop=(k == 30))
                        return r
                    T.op("pe", fcv, reads=["diag", f"GLb{i}"], writes=[f"ps{pb}"])
                    T.op("act", lambda e, i=i, g4=g4, pb=pb: e.activation(out=cvy[:, i, g4 * 512:(g4 + 1) * 512], in_=ps[pb][:], func=AF.Identity, bias=cvp[:, i, 0:1]),
                         reads=[f"ps{pb}", "cvp"], writes=[f"cvy{i}"])
            for i in range(2):
                for q4 in range(S_LOC // 512):
                    cs = slice(q4 * 512, (q4 + 1) * 512)
                    T.op("pe", lambda e, i=i, cs=cs: e.matmul(out=ps[0][:], lhsT=gavg[:], rhs=cvy[:, i, cs], start=True, stop=True),
                         reads=[f"cvy{i}", "gavg"], writes=["ps0"])
                    T.op("act", lambda e, i=i, cs=cs: e.activation(out=cvq[:], in_=cvy[:, i, cs], func=AF.Square), reads=[f"cvy{i}"], writes=["cvq"])
                    T.op("pe", lambda e: e.matmul(out=ps[1][:], lhsT=gavg[:], rhs=cvq[:], start=True, stop=True), reads=["cvq", "gavg"], writes=["ps1"])
                    T.op("act", lambda e: e.activation(out=w2[0][:], in_=ps[0][:], func=AF.Square), reads=["ps0"], writes=["w20"])
                    T.op("dve", lambda e: e.tensor_tensor(out=w2[0][:], in0=ps[1][:], in1=w2[0][:], op=ALU.subtract), reads=["ps1", "w20"], writes=["w20"])
                    T.op("act", lambda e: e.activation(out=w2[0][:], in_=w2[0][:], func=AF.Ln, bias=epsT[:, 0:1]), reads=["w20"], writes=["w20"])
                    T.op("act", lambda e: e.activation(out=w2[0][:], in_=w2[0][:], func=AF.Exp, scale=-0.5), reads=["w20"], writes=["w20"])
                    T.op("dve", lambda e, i=i, cs=cs: e.tensor_tensor(out=w2[1][:], in0=cvy[:, i, cs], in1=ps[0][:], op=ALU.subtract), reads=[f"cvy{i}", "ps0"], writes=["w21"])
                    T.op("dve", lambda e: e.tensor_tensor(out=w2[1][:], in0=w2[1][:], in1=w2[0][:], op=ALU.mult), reads=["w21", "w20"], writes=["w21"])
                    T.op("act", lambda e, i=i, cs=cs: e.activation(out=CT[:, 6 + i, cs], in_=w2[1][:], func=AF.Silu, bias=cvp[:, i, 2:3], scale=cvp[:, i, 1:2]),
                         reads=["w21", "cvp"], writes=["CT"])
            T.barrier()
            if upto == 1:
                return
            T.op("pool", lambda e: e.memset(QP[:, :, :], 0.0), writes=["QP"])
            for h in range(4):
                for mp in range(2):
                    T.op("pool", lambda e, mp=mp, h=h: e.tensor_copy(out=QP[mp * 64:(mp + 1) * 64, mp, :], in_=QT[mp * 64:(mp + 1) * 64, h, :]),
                         reads=["QT", "QP"], writes=["QP"])
                for c_ in range(8):
                    T.dma("sp", lambda e, c_=c_, h=h: e.dma_start(
                        out=KT[:, :, :].rearrange("p (m c) n -> p m c n", c=8)[:, :, c_, :],
                        in_=Kg[c_ * 512 + h * 128:c_ * 512 + (h + 1) * 128, :].rearrange("p (m n) -> p m n", n=128)),
                        reads=["Kg"], writes=[f"KT{c_}"], slot=f"kt{c_}")
                    T.dma("sp", lambda e, c_=c_, h=h: e.dma_start(
                        out=VH[:, :, 0:128].rearrange("p (m c) n -> p m c n", c=8)[:, :, c_, :],
                        in_=Vg[c_ * S_LOC:(c_ + 1) * S_LOC, h * 128:(h + 1) * 128].rearrange("(m p) n -> p m n", p=128)),
                        reads=["Vg"], writes=[f"VH{c_}"], slot=f"vh{c_}")
                T.op("pool", lambda e: e.memset(VH[:, :, 128:130], 1.0), writes=["VH1"])
                T.dma("sp", lambda e, h=h: e.dma_start(out=btab[:], in_=btab_in[h]), writes=["btab"], slot="btab")
                units = [(m, qd) for m in range(NT) for qd in range(2 * m + 2)]

                def score(ui, m, qd, h=h):
                    sb_ = ui % 2
                    S = psS[sb_]
                    qs = slice(m * 128, (m + 1) * 128)
                    def fsc(e):
                        r = None
                        for kb in range(4):
                            r = e.matmul(out=S[:, kb * 256:(kb + 1) * 256].rearrange("p (a n) -> p a n", a=2),
                                         lhsT=KT[:, 4 * qd + kb, :], rhs=QP[:, 0:2, qs], start=True, stop=True)
                        return r
                    T.op("pe", fsc, reads=KTN + ["QP"], writes=[f"ps{2 * sb_}", f"ps{2 * sb_ + 1}"])

                def softmax(ui, m, qd, h=h):
                    sb_ = ui % 2
                    S = psS[sb_]
                    P = wball[:, sb_, :]
                    sn = [f"ps{2 * sb_}", f"ps{2 * sb_ + 1}"]
                    pn = [f"wb{2 * sb_}", f"wb{2 * sb_ + 1}"]
                    near = (4 * qd + 3 >= 8 * m - 2)
                    if not near:
                        T.op("act", lambda e: e.activation(out=P, in_=S[:], func=AF.Exp, bias=cfar[:, h:h + 1], scale=0.125),
                             reads=sn + ["cfar"], writes=pn)
                    else:
                        n0 = 4 * qd - (8 * m - 4)
                        stg = w2all[:, 0:2, :]
                        for mp in range(2):
                            T.op("dve", lambda e, mp=mp: e.scalar_tensor_tensor(
                                out=stg.rearrange("p x (k2 a n) -> p (x k2) a n", a=2, n=128)[:, :, mp, :],
                                in0=S[:].rearrange("p (k a n) -> p k a n", k=4, a=2)[:, :, mp, :], scalar=0.125,
                                in1=btab[:, n0:n0 + 4, :], op0=ALU.mult, op1=ALU.add),
                                reads=sn + ["btab"], writes=["w20", "w21"])
                        T.op("act", lambda e: e.activation(out=P, in_=stg.rearrange("p x n -> p (x n)"), func=AF.Exp), reads=["w20", "w21"], writes=pn)

                def pv(ui, m, qd):
                    sb_ = ui % 2
                    P = wball[:, sb_, :]
                    ob = 4
                    nblk = 8 * m + 8
                    def fpv(e):
                        r = None
                        for kb in range(4):
                            j = 4 * qd + kb
                            for mp in range(2):
                                r = e.matmul(out=ps[ob + mp][:, 0:130], lhsT=P[:, kb * 256 + mp * 128: kb * 256 + (mp + 1) * 128], rhs=VH[:, j, :],
                                             start=(j == 0), stop=(j == nblk - 1))
                        return r
                    T.op("pe", fpv, reads=[f"wb{2 * sb_}", f"wb{2 * sb_ + 1}", "VH1"] + VHN, writes=[f"ps{ob}", f"ps{ob + 1}"])

                def fin1(m):
                    ob = 4
                    o1, o2 = ps[ob], ps[ob + 1]
                    o1n, o2n = f"ps{ob}", f"ps{ob + 1}"
                    osb1, osb2 = w1[1][:, 512:642], w1[1][:, 768:898]
                    T.op("act", lambda e: e.activation(out=osb1, in_=ps[ob][:, 0:130], func=AF.Copy), reads=[o1n], writes=["osb1"])
                    T.op("dve", lambda e: e.tensor_copy(out=osb2, in_=ps[ob + 1][:, 0:130]), reads=[o2n], writes=["osb2"])
                    o1, o2, o1n, o2n = osb1, osb2, "osb1", "osb2"
                    r1, r1n = newstat()
                    r2, r2n = newstat()
                    T.op("dve", lambda e: e.reciprocal(out=r1, in_=o1[:, 128:129]), reads=[o1n], writes=[r1n])
                    T.op("dve", lambda e: e.reciprocal(out=r2, in_=o2[:, 128:129]), reads=[o2n], writes=[r2n])
                    T.op("dve", lambda e: e.tensor_tensor(out=r2, in0=r2, in1=lamw[:, 2:3], op=ALU.mult), reads=[r2n, "lamw"], writes=[r2n])
                    at = w1[1]
                    T.op("dve", lambda e: e.tensor_scalar(out=at[:, 0:128], in0=o2[:, 0:128], scalar1=r2, scalar2=None, op0=ALU.mult),
                         reads=[o2n, r2n], writes=["w1_1"])
                    T.op("dve", lambda e: e.scalar_tensor_tensor(out=at[:, 128:256], in0=o1[:, 0:128], scalar=r1, in1=at[:, 0:128], op0=ALU.mult, op1=ALU.add),
                         reads=[o1n, r1n, "w1_1"], writes=["w1_1b"])
                    rr, rrn = rstd_from(at[:, 128:256], "w1_1b", 128)
                    hbm = hb[m % 2]
                    T.op("dve", lambda e: e.scalar_tensor_tensor(out=hbm[:, 0:128], in0=at[:, 128:256], scalar=rr, in1=gsub[:], op0=ALU.mult, op1=ALU.mult),
                         reads=["w1_1b", rrn, "gsub"], writes=[f"hb{m % 2}"])

                def fin2(m, h=h):
                    transpose_to(CT[:, 2 + h:3 + h, m * 128:(m + 1) * 128], "CT", hb[m % 2], f"hb{m % 2}", 1)

                pending = []
                score(0, *units[0])
                for ui, (m, qd) in enumerate(units):
                    if ui + 1 < len(units):
                        score(ui + 1, *units[ui + 1])
                    softmax(ui, m, qd)
                    pv(ui, m, qd)
                    for item in list(pending):
                        item[0] -= 1
                        if item[0] <= 0:
                            fin2(item[1])
                            pending.remove(item)
                    if qd == 2 * m + 1:
                        fin1(m)
                        pending.append([3, m])
                for item in pending:
                    fin2(item[1])
            T.barrier()
            if upto == 2:
                return
            ld(gv[:, 0, :], gvec[l][:, 1, :], "gv0")
            ld(gv[:, 1, :], gvec[l][:, 2, :], "gv1")
            for kc in range(8):
                T.dma("sp", lambda e, kc=kc: e.dma_start(out=WOUT[:, kc, :], in_=Wg_out[l * 1024 + kc * 128:l * 1024 + (kc + 1) * 128, :]),
                      reads=["Wg_out"], writes=[f"arenaW{kc}"], slot=f"win{kc}")
            def mmO(m):
                qs = slice(m * 128, (m + 1) * 128)
                pbase = 5 if m % 2 == 0 else 0
                for nh in range(2):
                    def fo(e, nh=nh, qs=qs):
                        r = None
                        for kc in range(8):
                            r = e.matmul(out=ps[pbase + nh][:], lhsT=CT[:, kc, qs], rhs=WOUT[:, kc, nh * 512:(nh + 1) * 512], start=(kc == 0), stop=(kc == 7))
                        return r
                    T.op("pe", fo, reads=["CT"] + ARW, writes=[f"ps{pbase + nh}"])
            mmO(0)
            for m in range(NT):
                b = m % 2
                qs = slice(m * 128, (m + 1) * 128)
                pbase = 5 if m % 2 == 0 else 0
                T.dma("sp", lambda e, m=m, b=b: e.dma_start(out=xt[b][:], in_=xsrc[m]), writes=[f"xt{b}"], slot=f"xt{b}")
                if m + 1 < NT:
                    mmO(m + 1)
                for nh in range(2):
                    T.op("dve", lambda e, nh=nh, b=b, pbase=pbase: e.tensor_copy(out=w1[b][:, nh * 512:(nh + 1) * 512], in_=ps[pbase + nh][:]),
                         reads=[f"ps{pbase + nh}"], writes=[f"w1_{b}"])
                r, rn = rstd_from(w1[b][:], f"w1_{b}", D)
                T.op("dve", lambda e, b=b, r=r: e.scalar_tensor_tensor(out=w1[b][:], in0=w1[b][:], scalar=r, in1=gv[:, 0, :], op0=ALU.mult, op1=ALU.mult),
                     reads=[f"w1_{b}", rn, "gv0"], writes=[f"w1_{b}"])
                T.op("dve", lambda e, b=b: e.tensor_tensor(out=xt[b][:], in0=xt[b][:], in1=w1[b][:], op=ALU.add), reads=[f"xt{b}", f"w1_{b}"], writes=[f"xt{b}"])
                T.dma("sp", lambda e, m=m, b=b: e.dma_start(out=xdst[m], in_=xt[b][:]), reads=[f"xt{b}"], writes=[f"xmid{m}"], slot=f"xo{b}")
                r, rn = rstd_from(xt[b][:], f"xt{b}", D)
                T.op("dve", lambda e, b=b, r=r: e.scalar_tensor_tensor(out=hb[b][:], in0=xt[b][:], scalar=r, in1=gv[:, 1, :], op0=ALU.mult, op1=ALU.mult),
                     reads=[f"xt{b}", rn, "gv1"], writes=[f"hb{b}"])
                transpose_to(H2T[:, :, qs], "H2T", hb[b], f"hb{b}", 8)
            T.op("dve", lambda e: e.tensor_copy(out=tailT[:], in_=H2T[:].rearrange("p k (t n) -> p k t n", n=128)[:, :, :, 126:128]),
                 reads=["H2T"], writes=["tailT"])
            T.dma("sp", lambda e: e.dma_start(out=Tc, in_=tailT[:].rearrange("p k t n -> p (k t n)")), reads=["tailT"], writes=["Tc"], slot="tc")

        def phase_C(l, xsrc, xdst):
            ld(gv[:, 0, :], gvec[l][:, 3, :], "gv0")
            T.op("pool", lambda e: e.memset(candT[:, 0, :], 0.0), writes=["cand00", "cand01"])
            cv_ = candT[:].rearrange("p r (k t n) -> p r k t n", k=8, t=NT)
            tgv = Tg.rearrange("(r p) (k t n) -> r p k t n", p=128, k=8, t=NT)
            if NT > 1:
                for k in range(8):
                    T.dma("sp", lambda e, k=k: e.dma_start(out=cv_[:, 0, k, 1:NT, :], in_=tgv[7, :, k, 0:NT - 1, :]), reads=["Tg"], writes=[f"cand0{k % 2}"], slot=f"cand0{k % 2}")
            for r_ in range(1, 8):
                T.dma("sp", lambda e, r_=r_: e.dma_start(out=candT[:, r_, :], in_=Tg[(r_ - 1) * 128:r_ * 128, :]), reads=["Tg"], writes=[f"cand{r_}"], slot=f"cand{r_}")
            halo_select(halTf[:], "halTf", candT, CANDN, 8 * NT * 2)
            T.op("dve", lambda e: e.tensor_copy(out=halT[:].rearrange("p k t n -> p (k t n)"), in_=halTf[:]), reads=["halTf"], writes=["halT"])
            for j in range(22):
                T.dma("sp", lambda e, j=j: e.dma_start(out=WDN[:, j, :], in_=Wg_dn[l * DFF + j * 128:l * DFF + (j + 1) * 128, :]),
                      reads=["Wg_dn"], writes=[f"wdn{j % 4}"], slot=f"wdn{j % 4}")
            NPASS = max(1, NT // 8)
            TP = NT // NPASS
            for pa in range(NPASS):
                t0 = pa * TP
                for j in range(22):
                    for part in range(2):
                        fc = part * 22 + j
                        col = part * DFF + j * 128
                        wbuf = wup_t[fc % 3]
                        wn = f"wup{fc % 3}"
                        gch = l * 44 + fc
                        T.dma("sp", lambda e, gch=gch, wbuf=wbuf: e.dma_start(out=wbuf[:].rearrange("p k n -> p (k n)"), in_=Wg[gch * 128:(gch + 1) * 128, :]),
                              reads=["Wg"], writes=[wn], slot=wn)
                        ub = upsb[part]
                        nbank = (TP + 3) // 4
                        for hb_ in range(nbank):
                            nt_ = min(4, TP - hb_ * 4)
                            cs = slice((t0 + hb_ * 4) * 128, (t0 + hb_ * 4 + nt_) * 128)
                            pbk = (fc * 2 + hb_) % 4
                            def fu(e, cs=cs, pbk=pbk, wbuf=wbuf, nt_=nt_):
                                r = None
                                for kc in range(8):
                                    r = e.matmul(out=ps[pbk][:, 0:nt_ * 128], lhsT=wbuf[:, kc, :], rhs=H2T[:, kc, cs], start=(kc == 0), stop=(kc == 7))
                                return r
                            T.op("pe", fu, reads=["H2T", wn], writes=[f"ps{pbk}"])
                            T.op("act", lambda e, pbk=pbk, ub=ub, hb_=hb_, nt_=nt_: e.activation(
                                out=ub[:, hb_ * 4:hb_ * 4 + nt_, 2:130], in_=ps[pbk][:, 0:nt_ * 128].rearrange("p (t n) -> p t n", n=128), func=AF.Copy),
                                reads=[f"ps{pbk}"], writes=[f"upsb{part}"])
                        def ft(e, wbuf=wbuf, t0=t0):
                            r = None
                            for kc in range(8):
                                r = e.matmul(out=ps[4][:, 0:TP * 2].rearrange("p (t n) -> p t n", n=2), lhsT=wbuf[:, kc, :], rhs=halT[:, kc, t0:t0 + TP, :], start=(kc == 0), stop=(kc == 7))
                            return r
                        T.op("pe", ft, reads=["halT", wn], writes=["ps4"])
                        T.op("act", lambda e, ub=ub: e.activation(out=ub[:, 0:TP, 0:2], in_=ps[4][:, 0:TP * 2].rearrange("p (t n) -> p t n", n=2), func=AF.Copy),
                             reads=["ps4"], writes=[f"upsb{part}"])
                        ac = acc[part]
                        T.op("dve", lambda e, ub=ub, ac=ac, fc=fc: e.tensor_scalar(out=ac[:, 0:TP, :], in0=ub[:, 0:TP, 2:130], scalar1=fcw[:, fc, 2:3], scalar2=fcw[:, fc, 3:4], op0=ALU.mult, op1=ALU.add),
                             reads=[f"upsb{part}", "fcw"], writes=[f"acc{part}"])
                        for k in range(2):
                            T.op("dve", lambda e, ub=ub, ac=ac, fc=fc, k=k: e.scalar_tensor_tensor(out=ac[:, 0:TP, :], in0=ub[:, 0:TP, k:128 + k], scalar=fcw[:, fc, k:k + 1], in1=ac[:, 0:TP, :], op0=ALU.mult, op1=ALU.add),
                                 reads=[f"upsb{part}", "fcw", f"acc{part}"], writes=[f"acc{part}"])
                        if part == 0:
                            T.op("act", lambda e, ac=ac: e.activation(out=gact[:, 0:TP, :], in_=ac[:, 0:TP, :], func=AF.Gelu_apprx_tanh), reads=["acc0"], writes=["gact"])
                        else:
                            T.op("pool", lambda e, ac=ac, j=j: e.tensor_tensor(out=GT[:, j, 0:TP * 128].rearrange("p (t n) -> p t n", n=128), in0=gact[:, 0:TP, :], in1=ac[:, 0:TP, :], op=ALU.mult),
                                 reads=["gact", "acc1"], writes=["GT"])
                for tt in range(TP):
                    m = t0 + tt
                    b = m % 2
                    T.dma("sp", lambda e, m=m, b=b: e.dma_start(out=xt[b][:], in_=xsrc[m]), reads=[f"xmid{m}"], writes=[f"xt{b}"], slot=f"xt{b}")
                    for nh in range(2):
                        def fd(e, nh=nh, tt=tt):
                            r = None
                            for j in range(22):
                                r = e.matmul(out=ps[5 + nh][:], lhsT=GT[:, j, tt * 128:(tt + 1) * 128], rhs=WDN[:, j, nh * 512:(nh + 1) * 512], start=(j == 0), stop=(j == 21))
                            return r
                        T.op("pe", fd, reads=["GT", "wdn0", "wdn1", "wdn2", "wdn3"], writes=[f"ps{5 + nh}"])
                        T.op("dve", lambda e, nh=nh, b=b: e.tensor_copy(out=w1[b][:, nh * 512:(nh + 1) * 512], in_=ps[5 + nh][:]), reads=[f"ps{5 + nh}"], writes=[f"w1_{b}"])
                    r, rn = rstd_from(w1[b][:], f"w1_{b}", D)
                    T.op("dve", lambda e, b=b, r=r: e.scalar_tensor_tensor(out=w1[b][:], in0=w1[b][:], scalar=r, in1=gv[:, 0, :], op0=ALU.mult, op1=ALU.mult),
                         reads=[f"w1_{b}", rn, "gv0"], writes=[f"w1_{b}"])
                    T.op("dve", lambda e, b=b: e.tensor_tensor(out=xt[b][:], in0=xt[b][:], in1=w1[b][:], op=ALU.add), reads=[f"xt{b}", f"w1_{b}"], writes=[f"xt{b}"])
                    T.dma("sp", lambda e, m=m, b=b: e.dma_start(out=xdst[m], in_=xt[b][:]), reads=[f"xt{b}"], writes=[f"xo{m}"], slot=f"xo{b}")

        wup_t = [sb(f"wup{i}", [128, 8, 128], BF16) for i in range(3)]

        def dump(which):
            T.barrier()
            if which in ("A0", "A1"):
                T.dma("sp", lambda e: e.dma_start(out=dbg["q"], in_=QT[:]), reads=["QT"], writes=["dq"], slot="dbg0")
                T.dma("sp", lambda e: e.dma_start(out=dbg["ct"], in_=CT[:]), reads=["CT"], writes=["dc"], slot="dbg1")
                for i in range(2):
                    T.dma("sp", lambda e, i=i: e.dma_start(out=dbg["gl"].rearrange("p i (t n) -> p i t n", n=128)[:, i], in_=GL[:, i, :, 32:160]), reads=["GL"], writes=[f"dg{i}"], slot=f"dbg2{i}")
                T.dma("sp", lambda e: e.dma_start(out=dbg["kg"], in_=Kg), reads=["Kg"], writes=["dk"], slot="dbg3")
                T.dma("sp", lambda e: e.dma_start(out=dbg["vg"], in_=Vg), reads=["Vg"], writes=["dv"], slot="dbg4")
            if which in ("B0", "B1"):
                T.dma("sp", lambda e: e.dma_start(out=dbg["ct"], in_=CT[:]), reads=["CT"], writes=["dc"], slot="dbg1")
                T.dma("sp", lambda e: e.dma_start(out=dbg["h2"], in_=H2T[:]), reads=["H2T"], writes=["dh"], slot="dbg2")
                for m in range(NT):
                    T.dma("sp", lambda e, m=m: e.dma_start(out=y_out[m], in_=xbuf[0][m]), reads=[f"xmid{m}"], writes=[f"y{m}"], slot=f"dy{m % 4}")
            T.barrier()

        def convert(src, dst_c, dst_g, rows, step, name):
            pieces = []
            for i, r0 in enumerate(range(0, rows, step)):
                T.dma("pool", lambda e, r0=r0: e.dma_start(out=dst_c[r0:r0 + step, :].rearrange("p (a b) -> p a b", b=512),
                                                         in_=src[r0:r0 + step, :].rearrange("p (a b) -> p a b", b=512)),
                      writes=[f"{name}c{i}"], slot=f"cv{name}{i % 4}")
                pieces.append(f"{name}c{i}")
            gather(dst_c, dst_g, pieces, name)
        convert(win_part, Wc_in, Wg_in, 256, 128, "Wg_in")

        done = False
        for l in range(DEPTH):
            xsrc = x_in if l == 0 else xbuf[1]
            layer_consts(l)
            phase_A(l, xsrc)
            if l == 0:
                for i in range(11):
                    T.dma("pool", lambda e, i=i: e.dma_start(out=Wc[i * 128:(i + 1) * 128, :].rearrange("p (a b) -> p a b", b=512),
                                                           in_=wup_part[i].rearrange("p (a b) -> p a b", b=512)),
                          writes=[f"Wc{i}"], slot=f"wc{i % 4}")
                gather(Wc, Wg, [f"Wc{i}" for i in range(11)], "Wg")
                convert(wout_part, Wc_out, Wg_out, 256, 128, "Wg_out")
                convert(wdn_part, Wc_dn, Wg_dn, 704, 64, "Wg_dn")
            T.barrier()
            gather(Hc, Hg, ["Hc0", "Hc1"], "Hg")
            T.barrier()
            if stop_after == f"A{l}":
                dump(stop_after); done = True; break
            if stop_after in (f"P{l}", f"Q{l}"):
                phase_B(l, xsrc, xbuf[0], upto=1 if stop_after[0] == "P" else 2)
                T.barrier()
                T.dma("sp", lambda e: e.dma_start(out=dbg["ct"], in_=CT[:]), reads=["CT"], writes=["dc"], slot="dbg1")
                T.barrier()
                done = True
                break
            phase_B(l, xsrc, xbuf[0])
            T.barrier()
            gather(Tc, Tg, ["Tc"], "Tg")
            T.barrier()
            if stop_after == f"B{l}":
                dump(stop_after); done = True; break
            phase_C(l, xbuf[0], y_out if l == DEPTH - 1 else xbuf[1])
            T.barrier()
            if stop_after == f"C{l}":
                for m in range(NT):
                    T.dma("sp", lambda e, m=m: e.dma_start(out=y_out[m], in_=xbuf[1][m]), reads=[f"xo{m}"], writes=[f"y{m}"], slot=f"dy{m % 4}")
                T.barrier()
                done = True
                break
        T.barrier()
        T.run(block)
    return nc


def host_inputs(inputs, NT=16):
    f = lambda a: np.ascontiguousarray(np.asarray(a, dtype=np.float32))
    x = f(inputs["x"])[0]
    S = x.shape[0]
    nblk = S // 128
    assert nblk == NT * 8
    xb = x.reshape(NT, 8, 128, D)
    bc = lambda a, shape: np.ascontiguousarray(np.broadcast_to(a, shape))
    gvec = np.stack([f(inputs[k]) for k in ("pre_mix_g", "post_mix_g", "pre_ffn_g", "post_ffn_g")], 1)
    gvec = bc(gvec[:, None], (DEPTH, 128, 4, D))
    gmv = np.stack([f(inputs["gm_ln_g"]), f(inputs["gm_ln_b"])], 1)
    gmv = bc(gmv[:, None], (DEPTH, 128, 2, 256))
    gm_wT = np.ascontiguousarray(f(inputs["gm_w_s"]).transpose(0, 3, 1, 2))
    gm_bs = np.ascontiguousarray(np.repeat(f(inputs["gm_b_s"]).transpose(0, 2, 1), 64, axis=2))
    tri = (np.arange(128)[:, None] <= np.arange(128)[None, :]).astype(np.float32)
    lam_in = np.stack([f(inputs[k]) for k in ("da_lq1", "da_lk1", "da_lq2", "da_lk2")], 1)
    lam_in = bc(lam_in[:, None], (DEPTH, 128, 4, 64))
    subln = bc(f(inputs["da_subln_g"])[:, None], (DEPTH, 128, 128))
    rb = f(inputs["rel_bias"])
    cfar = bc(rb[31][None], (128, 4))
    cvw = np.ascontiguousarray(f(inputs["cv_dw_w"]).reshape(DEPTH, 31, 2, 128).transpose(0, 3, 2, 1))
    cvp = np.stack([f(inputs[k]).reshape(DEPTH, 2, 128) for k in ("cv_dw_b", "cv_ln_g", "cv_ln_b")], -1)
    cvp = np.ascontiguousarray(cvp.transpose(0, 2, 1, 3))
    gavg = np.zeros((128, 128), np.float32)
    gavg[:64, :64] = 1.0 / 64
    gavg[64:, 64:] = 1.0 / 64
    fw = f(inputs["ffn_conv_w"]).reshape(DEPTH, 3, NFC, 128)
    fb = f(inputs["ffn_conv_b"]).reshape(DEPTH, 1, NFC, 128)
    fcw = np.ascontiguousarray(np.concatenate([fw, fb], 1).transpose(0, 3, 2, 1))
    ident = np.eye(128, dtype=np.float32)
    wup = f(inputs["ffn_w_up"])
    win_f = f(inputs["w_in"]).reshape(DEPTH * D, INW)
    wout_f = f(inputs["w_out"]).reshape(DEPTH * D, D)
    wdn_f = f(inputs["ffn_w_down"]).reshape(DEPTH * DFF, D)
    common = dict(
                  gvec=gvec, gmv=gmv, gm_wT=gm_wT, gm_bs=gm_bs, tri=tri, lam_in=lam_in, subln=subln, cfar=cfar,
                  cvw=cvw, cvp=cvp, gavg=gavg, fcw=fcw, ident=ident)
    k = np.arange(128)[:, None, None]
    n = np.arange(12)[None, :, None]
    q = np.arange(128)[None, None, :]
    maps = []
    for c in range(NCORES):
        rel = (c + 4 - n) * 128 + q - k
        idx = t5_bucket(rel)
        tab = rb[idx]
        tab = np.where((rel >= 0)[..., None], tab, np.float32(NEG)).astype(np.float32)
        btab = np.ascontiguousarray(tab.transpose(3, 0, 1, 2))
        sel = np.zeros((128, 8), np.float32)
        sel[:, c] = 1.0
        parts = []
        for i in range(11):
            l_, fc = divmod(c * 11 + i, NFC)
            part, j = divmod(fc, 22)
            col = part * DFF + j * 128
            parts.append(wup[l_][:, col:col + 128].reshape(8, 128, 128).transpose(1, 0, 2).reshape(128, 1024))
        d = dict(common)
        d.update(x=np.ascontiguousarray(xb[:, c]), btab=btab, sel=sel, wup_part=np.ascontiguousarray(np.stack(parts)),
                 win_part=np.ascontiguousarray(win_f[c * 256:(c + 1) * 256]), wout_part=np.ascontiguousarray(wout_f[c * 256:(c + 1) * 256]),
                 wdn_part=np.ascontiguousarray(wdn_f[c * 704:(c + 1) * 704]))
        maps.append(d)
    return maps


_CACHE = {}


def kernel(**inputs):
    NT = 16
    if "nc" not in _CACHE:
        _CACHE["nc"] = build(NT)
    nc = _CACHE["nc"]
    maps = host_inputs(inputs, NT)
    res = run_bass_kernel_spmd(nc, maps, core_ids=list(range(NCORES)))
    out = np.zeros((NT, 8, 128, D), np.float32)
    for c in range(NCORES):
        out[:, c] = np.asarray(res.results[c]["y"])
    return out.reshape(1, NT * 8 * 128, D)
```

```python
import math
from contextlib import ExitStack
import numpy as np
import concourse.bass as bass
import concourse.mybir as mybir
from concourse.bass_utils import run_bass_kernel_spmd

F32, BF16 = mybir.dt.float32, mybir.dt.bfloat16
AF = mybir.ActivationFunctionType
ALU = mybir.AluOpType
NCORES = 8
D = 1024
INW = 2560
DFF = 2816
NFC = 44
EPS = 1e-6
NEG = -30000.0
DEPTH = 2
ENG = ("pe", "act", "dve", "pool", "sp")


class Tracker:
    def __init__(self, nc, stack):
        self.nc, self.stack = nc, stack
        self.ops = {e: [] for e in ENG}
        self.sem, self.cnt = {}, {}
        self.waited = {e: {} for e in ENG}
        self.lastw, self.readers = {}, {}
        self.mute = False
        for e in ENG:
            self._mk("E" + e)

    def _mk(self, name):
        if name not in self.sem:
            self.sem[name] = self.stack.enter_context(self.nc.semaphore(name))
            self.cnt[name] = 0
        return name

    def _deps(self, reads, writes):
        deps = {}
        def add(tok):
            if tok is not None:
                deps[tok[0]] = max(deps.get(tok[0], 0), tok[1])
        for b in reads:
            add(self.lastw.get(b))
        for b in writes:
            add(self.lastw.get(b))
            for s, v in self.readers.get(b, {}).items():
                add((s, v))
        return deps

    def _waits(self, eng, deps):
        w = []
        for s, v in deps.items():
            if eng == "pe" and s == "Epe":
                continue
            if self.waited[eng].get(s, 0) < v:
                self.waited[eng][s] = v
                w.append((self.sem[s], v))
        return w

    def _commit(self, tok, reads, writes):
        for b in reads:
            self.readers.setdefault(b, {})[tok[0]] = tok[1]
        for b in writes:
            self.lastw[b] = tok
            self.readers[b] = {}

    def op(self, eng, fn, reads=(), writes=()):
        if self.mute:
            return
        w = self._waits(eng, self._deps(reads, writes))
        s = "E" + eng
        self.cnt[s] += 1
        tok = (s, self.cnt[s])
        sem = self.sem[s]
        def emit(e, fn=fn, w=w, sem=sem):
            for sh, v in w:
                e.wait_ge(sh, v)
            fn(e).then_inc(sem, 1)
        self.ops[eng].append(emit)
        self._commit(tok, reads, writes)

    def dma(self, eng, fn, reads=(), writes=(), slot=None, inc=16):
        if self.mute:
            return
        w = self._waits(eng, self._deps(reads, writes))
        s = self._mk("D" + slot)
        self.cnt[s] += inc
        tok = (s, self.cnt[s])
        sem = self.sem[s]
        def emit(e, fn=fn, w=w, sem=sem, inc=inc):
            for sh, v in w:
                e.wait_ge(sh, v)
            fn(e).then_inc(sem, inc)
        self.ops[eng].append(emit)
        self._commit(tok, reads, writes)

    def barrier(self):
        allv = {s: v for s, v in self.cnt.items() if v > 0}
        for eng in ENG:
            w = self._waits(eng, dict(allv))
            def emit(e, w=w):
                for sh, v in w:
                    e.wait_ge(sh, v)
            self.ops[eng].append(emit)

    def run(self, block):
        for name, meth in (("pe", block.tensor), ("act", block.scalar), ("dve", block.vector),
                           ("pool", block.gpsimd), ("sp", block.sync)):
            ops = self.ops[name]
            def body(e, ops=ops):
                for o in ops:
                    o(e)
            meth(body)


def t5_bucket(n):
    n = np.maximum(n, 0)
    nf = np.maximum(n, 1).astype(np.float32)
    large = 16 + (np.log(nf / np.float32(16)) / np.float32(math.log(128 / 16)) * np.float32(16)).astype(np.int32)
    large = np.minimum(large, 31)
    return np.where(n < 16, n, large)


def build(NT=16, stop_after=None):
    S_LOC = NT * 128
    NBLK = NT * 8
    nc = bass.Bass("TRN2", target_bir_lowering=False)
    dt_in = lambda name, shape, dt=F32: nc.dram_tensor(name, list(shape), dt, kind="ExternalInput").ap()
    x_in = dt_in("x", [NT, 128, D])
    win_part = dt_in("win_part", [256, INW])
    wout_part = dt_in("wout_part", [256, D])
    wup_part = dt_in("wup_part", [11, 128, 1024])
    wdn_part = dt_in("wdn_part", [704, D])
    gvec = dt_in("gvec", [DEPTH, 128, 4, D])
    gmv = dt_in("gmv", [DEPTH, 128, 2, 256])
    gm_wT = dt_in("gm_wT", [DEPTH, 128, 4, 128])
    gm_bs = dt_in("gm_bs", [DEPTH, 128, 256])
    tri_in = dt_in("tri", [128, 128])
    lam_in = dt_in("lam_in", [DEPTH, 128, 4, 64])
    subln_in = dt_in("subln", [DEPTH, 128, 128])
    btab_in = dt_in("btab", [4, 128, 12, 128])
    cfar_in = dt_in("cfar", [128, 4])
    cvw_in = dt_in("cvw", [DEPTH, 128, 2, 31])
    cvp_in = dt_in("cvp", [DEPTH, 128, 2, 3])
    gavg_in = dt_in("gavg", [128, 128])
    fcw_in = dt_in("fcw", [DEPTH, 128, NFC, 4])
    sel_in = dt_in("sel", [128, 8])
    ident_in = dt_in("ident", [128, 128])
    y_out = nc.dram_tensor("y", [NT, 128, D], F32, kind="ExternalOutput").ap()
    dbg = {}
    if stop_after is not None:
        dbg["q"] = nc.dram_tensor("dbg_q", [128, 4, S_LOC], BF16, kind="ExternalOutput").ap()
        dbg["ct"] = nc.dram_tensor("dbg_ct", [128, 8, S_LOC], BF16, kind="ExternalOutput").ap()
        dbg["gl"] = nc.dram_tensor("dbg_gl", [128, 2, S_LOC], F32, kind="ExternalOutput").ap()
        dbg["kg"] = nc.dram_tensor("dbg_kg", [NCORES * 512, S_LOC], BF16, kind="ExternalOutput").ap()
        dbg["vg"] = nc.dram_tensor("dbg_vg", [NCORES * S_LOC, 512], BF16, kind="ExternalOutput").ap()
        dbg["h2"] = nc.dram_tensor("dbg_h2", [128, 8, S_LOC], BF16, kind="ExternalOutput").ap()

    xbuf = [nc.dram_tensor(f"xbuf{i}", [NT, 128, D], F32).ap() for i in range(2)]
    Kc = nc.dram_tensor("Kc", [512, S_LOC], BF16).ap()
    Vc = nc.dram_tensor("Vc", [S_LOC, 512], BF16).ap()
    Hc = nc.dram_tensor("Hc", [128, 2 * NT * 32], F32).ap()
    Tc = nc.dram_tensor("Tc", [128, 8 * NT * 2], BF16).ap()
    Wc_in = nc.dram_tensor("Wc_in", [256, INW], BF16).ap()
    Wg_in = nc.dram_tensor("Wg_in", [NCORES * 256, INW], BF16, addr_space="Shared").ap()
    Wc_out = nc.dram_tensor("Wc_out", [256, D], BF16).ap()
    Wg_out = nc.dram_tensor("Wg_out", [NCORES * 256, D], BF16, addr_space="Shared").ap()
    Wc_dn = nc.dram_tensor("Wc_dn", [704, D], BF16).ap()
    Wg_dn = nc.dram_tensor("Wg_dn", [NCORES * 704, D], BF16, addr_space="Shared").ap()
    Wc = nc.dram_tensor("Wc", [11 * 128, 1024], BF16).ap()
    Wg = nc.dram_tensor("Wg", [NCORES * 11 * 128, 1024], BF16, addr_space="Shared").ap()
    Kg = nc.dram_tensor("Kg", [NCORES * 512, S_LOC], BF16, addr_space="Shared").ap()
    Vg = nc.dram_tensor("Vg", [NCORES * S_LOC, 512], BF16, addr_space="Shared").ap()
    Hg = nc.dram_tensor("Hg", [NCORES * 128, 2 * NT * 32], F32, addr_space="Shared").ap()
    Tg = nc.dram_tensor("Tg", [NCORES * 128, 8 * NT * 2], BF16, addr_space="Shared").ap()

    with ExitStack() as st:
        sb = lambda name, shape, dt=F32: st.enter_context(nc.sbuf_tensor("s_" + name, list(shape), dt))
        psS = [st.enter_context(nc.psum_tensor(f"psS{i}", [128, 1024], F32)) for i in range(2)]
        ps = [psS[0][:, 0:512], psS[0][:, 512:1024], psS[1][:, 0:512], psS[1][:, 512:1024]] + \
             [st.enter_context(nc.psum_tensor(f"ps{i}", [128, 512], F32)) for i in range(4, 7)]
        psT = st.enter_context(nc.psum_tensor("psT", [128, 1024], BF16))
        ARENA = sb("ARENA", [128, 63488], BF16)
        ident = sb("ident", [128, 128], BF16)
        identf = sb("identf", [128, 128], F32)
        gv = sb("gv", [128, 2, D], F32)
        wT = sb("wT", [128, 4, 128], BF16)
        wTf = sb("wTf", [128, 4, 128], F32)
        tri = sb("tri", [128, 128], F32)
        gmvt = sb("gmvt", [128, 2, 256], F32)
        gbs = sb("gbs", [128, 256], F32)
        lamt = sb("lamt", [128, 4, 64], F32)
        lamw = sb("lamw", [128, 8], F32)
        gsub = sb("gsub", [128, 128], F32)
        cfar = sb("cfar", [128, 4], F32)
        btab = sb("btab", [128, 12, 128], F32)
        cvw = sb("cvw", [128, 2, 31], F32)
        cvp = sb("cvp", [128, 2, 3], F32)
        gavg = sb("gavg", [128, 128], F32)
        fcw = sb("fcw", [128, NFC, 4], F32)
        sel = sb("sel", [128, 8], F32)
        stt = sb("stt", [128, 64], F32)
        bnst = sb("bnst", [128, 2, 8], F32)
        xt = [sb(f"xt{i}", [128, D], F32) for i in range(2)]
        junk = sb("junk", [128, D], BF16)
        hb = [sb(f"hb{i}", [128, D], BF16) for i in range(2)]
        w1 = [sb(f"w1_{i}", [128, D], F32) for i in range(2)]
        w2all = sb("w2all", [128, 3, 512], F32)
        w2 = [w2all[:, i, :] for i in range(3)]
        wball = sb("wball", [128, 2, 1024], BF16)
        wb = [wball[:, 0, 0:512], wball[:, 0, 512:1024], wball[:, 1, 0:512], wball[:, 1, 512:1024]]
        candT = sb("candT", [128, 8, 8 * NT * 2], BF16)
        halT = sb("halT", [128, 8, NT, 2], BF16)
        halTf = sb("halTf", [128, 8 * NT * 2], F32)
        tailT = sb("tailT", [128, 8, NT, 2], BF16)
        upsb = [sb(f"upsb{i}", [128, 8, 130], F32) for i in range(2)]
        acc1 = sb("acc1", [128, 8, 128], F32)
        acc = [w2all[:, 0:2, :].rearrange("p a (t n) -> p (a t) n", n=128), acc1[:]]
        gact = btab[:, 0:8, :]
        cvq = w2[2]

        block = st.enter_context(nc.Block())
        T = Tracker(nc, st)

        WIN = ARENA[:, 0:8 * INW].rearrange("p (k n) -> p k n", k=8)
        HT = [ARENA[:, 20480 + i * 4096: 20480 + (i + 1) * 4096].rearrange("p (k n) -> p k n", k=8) for i in range(2)] + \
             [ARENA[:, 43008 + i * 4096: 43008 + (i + 1) * 4096].rearrange("p (k n) -> p k n", k=8) for i in range(2)]
        GL = ARENA[:, 28672:38912].bitcast(F32).rearrange("p (i t n) -> p i t n", i=2, n=160)[:, :, 0:NT, :]
        CT = ARENA[:, 38912:55296].rearrange("p (k n) -> p k n", k=8)[:, :, 0:S_LOC]
        QT = ARENA[:, 55296:63488].rearrange("p (k n) -> p k n", k=4)[:, :, 0:S_LOC]
        cand = ARENA[:, 0:16384].bitcast(F32).rearrange("p (r n) -> p r n", r=8)[:, :, 0:2 * NT * 32]
        cvy = ARENA[:, 16384:24576].bitcast(F32).rearrange("p (i n) -> p i n", i=2)[:, :, 0:S_LOC]
        KT = ARENA[:, 0:NBLK * 128].rearrange("p (j n) -> p j n", n=128)
        VH = ARENA[:, 16384:16384 + NBLK * 130].rearrange("p (j n) -> p j n", n=130)
        QP = ARENA[:, 33280:37376].rearrange("p (a n) -> p a n", a=2)[:, :, 0:S_LOC]
        WOUT = ARENA[:, 16384:24576].rearrange("p (k n) -> p k n", k=8)
        H2T = ARENA[:, 0:16384].rearrange("p (k n) -> p k n", k=8)[:, :, 0:S_LOC]
        GT = ARENA[:, 16384:38912].rearrange("p (j n) -> p j n", j=22)
        WDN = ARENA[:, 38912:61440].rearrange("p (j n) -> p j n", j=22)

        ARW = [f"arenaW{k}" for k in range(8)]
        KTN = [f"KT{c}" for c in range(8)]
        VHN = [f"VH{c}" for c in range(8)]
        CANDN = ["cand00", "cand01"] + [f"cand{r}" for r in range(1, 8)]
        stat_i = [0]
        def newstat():
            stat_i[0] = (stat_i[0] + 1) % 64
            i = stat_i[0]
            return stt[:, i:i + 1], f"st{i}"

        def rstd_from(src_ap, srcname, n, eng_sq="act"):
            ss, ssn = newstat()
            T.op("act", lambda e: e.activation(out=junk[:, 0:n], in_=src_ap, func=AF.Square, accum_out=ss),
                 reads=[srcname], writes=["junk", ssn])
            return rstd_of(ss, ssn, 1.0 / n)

        def rstd_of(v, vn, scale):
            lnv, lnn = newstat()
            T.op("pool", lambda e: e.tensor_scalar(out=lnv, in0=v, scalar1=scale, scalar2=EPS, op0=ALU.mult, op1=ALU.add),
                 reads=[vn], writes=[lnn])
            r, rn = newstat()
            T.op("pool", lambda e: e.tensor_tensor(out=r, in0=lnv, in1=mhalf[:, 0:1], op=ALU.pow), reads=[lnn, "mhalf"], writes=[rn])
            return r, rn

        epsT = sb("epsT", [128, 1], F32)
        T.op("dve", lambda e: e.memset(epsT[:], EPS), writes=["epsT"])
        mhalf = sb("mhalf", [128, 1], F32)
        T.op("dve", lambda e: e.memset(mhalf[:], -0.5), writes=["mhalf"])

        def ld(dst, src, name, eng="sp"):
            T.dma(eng, lambda e: e.dma_start(out=dst, in_=src), writes=[name], slot="c_" + name)

        ld(identf[:], ident_in, "identf")
        T.op("dve", lambda e: e.tensor_copy(out=ident[:], in_=identf[:]), reads=["identf"], writes=["ident"])
        ld(tri[:], tri_in, "tri")
        ld(cfar[:], cfar_in, "cfar")
        ld(gavg[:], gavg_in, "gavg")
        ld(sel[:], sel_in, "sel")

        def transpose_to(dst_ap, dstname, src_tile, srcname, nchunk, psname="psT", evac="act"):
            def f(e):
                r = None
                for k in range(nchunk):
                    r = e.transpose(out=psT[:, k * 128:(k + 1) * 128], in_=src_tile[:, k * 128:(k + 1) * 128], identity=ident[:])
                return r
            T.op("pe", f, reads=[srcname, "ident"], writes=[psname])
            if evac == "act":
                T.op("act", lambda e: e.activation(out=dst_ap, in_=psT[:, 0:nchunk * 128].rearrange("p (k n) -> p k n", k=nchunk), func=AF.Copy),
                     reads=[psname], writes=[dstname])
            else:
                T.op("dve", lambda e: e.tensor_copy(out=dst_ap, in_=psT[:, 0:nchunk * 128].rearrange("p (k n) -> p k n", k=nchunk)),
                     reads=[psname], writes=[dstname])

        def layer_consts(l):
            ld(wTf[:], gm_wT[l], "wTf")
            for h in range(4):
                T.op("dve", lambda e, h=h: e.tensor_tensor(out=wT[:, h, :], in0=wTf[:, h, :], in1=tri[:], op=ALU.mult),
                     reads=["wTf", "tri"], writes=["wT"])
            ld(gmvt[:], gmv[l], "gmvt")
            ld(gbs[:], gm_bs[l], "gbs")
            ld(lamt[:], lam_in[l], "lamt")
            ld(gsub[:], subln_in[l], "gsub")
            ld(cvw[:], cvw_in[l], "cvw")
            ld(cvp[:], cvp_in[l], "cvp")
            ld(fcw[:], fcw_in[l], "fcw")
            lam_init = 0.8 - 0.6 * math.exp(-0.3 * l)
            for i in range(2):
                T.op("dve", lambda e, i=i: e.tensor_tensor(out=junk[:, i * 64:(i + 1) * 64], in0=lamt[:, 2 * i, :], in1=lamt[:, 2 * i + 1, :], op=ALU.mult),
                     reads=["lamt"], writes=["junk"])
                T.op("dve", lambda e, i=i: e.reduce_sum(out=lamw[:, i:i + 1], in_=junk[:, i * 64:(i + 1) * 64], axis=mybir.AxisListType.X),
                     reads=["junk"], writes=["lamw"])
            T.op("act", lambda e: e.activation(out=lamw[:, 4:6], in_=lamw[:, 0:2], func=AF.Exp), reads=["lamw"], writes=["lamw"])
            T.op("dve", lambda e: e.tensor_tensor(out=lamw[:, 6:7], in0=lamw[:, 5:6], in1=lamw[:, 4:5], op=ALU.subtract),
                 reads=["lamw"], writes=["lamw"])
            T.op("dve", lambda e: e.tensor_scalar(out=lamw[:, 2:3], in0=lamw[:, 6:7], scalar1=-lam_init, scalar2=None, op0=ALU.add),
                 reads=["lamw"], writes=["lamw"])
            T.op("dve", lambda e: e.tensor_scalar(out=gsub[:], in0=gsub[:], scalar1=1.0 - lam_init, scalar2=None, op0=ALU.mult),
                 reads=["gsub"], writes=["gsub"])

        def phase_A(l, xsrc):
            ld(gv[:, 0, :], gvec[l][:, 0, :], "gv0")
            for kc in range(8):
                T.dma("sp", lambda e, kc=kc: e.dma_start(out=WIN[:, kc, :], in_=Wg_in[l * 1024 + kc * 128:l * 1024 + (kc + 1) * 128, :]),
                      reads=["Wg_in"], writes=[f"arenaW{kc}"], slot=f"win{kc}")
            def normsA(g):
                hT = HT[g]
                hTn = f"hT{g}"
                for t in range(4):
                    m = 4 * g + t
                    b = m % 2
                    T.dma("sp", lambda e, m=m, b=b: e.dma_start(out=xt[b][:], in_=xsrc[m]), writes=[f"xt{b}"], slot=f"xt{b}")
                    r, rn = rstd_from(xt[b][:], f"xt{b}", D)
                    T.op("dve", lambda e, b=b, r=r: e.scalar_tensor_tensor(out=hb[b][:], in0=xt[b][:], scalar=r, in1=gv[:, 0, :], op0=ALU.mult, op1=ALU.mult),
                         reads=[f"xt{b}", rn, "gv0"], writes=[f"hb{b}"])
                    transpose_to(hT[:, :, t * 128:(t + 1) * 128], hTn, hb[b], f"hb{b}", 8)
            def mmA(g, part):
                hT = HT[g]
                hTn = f"hT{g}"
                order = [("q", h, 512 + 128 * h) for h in range(4)] + [("k", h, 1024 + 128 * h) for h in range(4)] + \
                        [("cg", i, 2304 + 128 * i) for i in range(2)] + [("ca", i, 2048 + 128 * i) for i in range(2)]
                order = [o for o in order if (o[0] == "k") == (part == 1)]
                for oi, (kind, idx, col) in enumerate(order):
                    pb = oi % 2
                    def f(e, col=col, pb=pb, hT=hT):
                        r = None
                        for kc in range(8):
                            r = e.matmul(out=ps[pb][:], lhsT=WIN[:, kc, col:col + 128], rhs=hT[:, kc, :], start=(kc == 0), stop=(kc == 7))
                        return r
                    T.op("pe", f, reads=[hTn] + ARW, writes=[f"ps{pb}"])
                    tok = slice(g * 512, (g + 1) * 512)
                    if kind == "q":
                        T.op("act", lambda e, pb=pb, idx=idx, tok=tok: e.activation(out=QT[:, idx, tok], in_=ps[pb][:], func=AF.Copy),
                             reads=[f"ps{pb}"], writes=["QT"])
                    elif kind == "k":
                        wi = idx % 3
                        T.op("dve", lambda e, pb=pb, wi=wi: e.tensor_copy(out=wb[wi][:], in_=ps[pb][:]), reads=[f"ps{pb}"], writes=[f"wb{wi}"])
                        T.dma("sp", lambda e, wi=wi, idx=idx, tok=tok: e.dma_start(out=Kc[idx * 128:(idx + 1) * 128, tok], in_=wb[wi][:]),
                              reads=[f"wb{wi}"], writes=[f"Kc{g}_{idx}"], slot=f"wb{wi}")
                    elif kind == "cg":
                        T.op("act", lambda e, pb=pb, idx=idx: e.activation(out=w2[idx][:], in_=ps[pb][:], func=AF.Sigmoid),
                             reads=[f"ps{pb}"], writes=[f"w2{idx}"])
                    else:
                        T.op("dve", lambda e, pb=pb, idx=idx, g=g: e.tensor_tensor(
                            out=GL[:, idx, 4 * g:4 * g + 4, 32:160], in0=ps[pb][:].rearrange("p (t n) -> p t n", t=4),
                            in1=w2[idx][:].rearrange("p (t n) -> p t n", t=4), op=ALU.mult),
                            reads=[f"ps{pb}", f"w2{idx}"], writes=["GL"])
                for t in range(4):
                    m = 4 * g + t
                    tcols = slice(t * 128, (t + 1) * 128)
                    def fg(e, tcols=tcols, hT=hT):
                        r = None
                        for kc in range(8):
                            r = e.matmul(out=ps[2][:], lhsT=hT[:, kc, tcols], rhs=WIN[:, kc, 0:512], start=(kc == 0), stop=(kc == 7))
                        return r
                    T.mute = (part != 2)
                    T.op("pe", fg, reads=[hTn] + ARW, writes=["ps2"])
                    T.mute = (part != 1)
                    def fv(e, tcols=tcols, hT=hT):
                        r = None
                        for kc in range(8):
                            r = e.matmul(out=ps[3][:], lhsT=hT[:, kc, tcols], rhs=WIN[:, kc, 1536:2048], start=(kc == 0), stop=(kc == 7))
                        return r
                    T.op("pe", fv, reads=[hTn] + ARW, writes=["ps3"])
                    T.op("dve", lambda e: e.tensor_copy(out=wb[2][:], in_=ps[3][:]), reads=["ps3"], writes=["wb2"])
                    T.dma("sp", lambda e, m=m: e.dma_start(out=Vc[m * 128:(m + 1) * 128, :], in_=wb[2][:]),
                          reads=["wb2"], writes=[f"Vc{m}"], slot="wb2")
                    T.mute = (part != 2)
                    ug = w1[0]
                    T.op("act", lambda e: e.activation(out=ug[:, 0:512], in_=ps[2][:], func=AF.Gelu_apprx_tanh), reads=["ps2"], writes=["w1_0"])
                    T.op("dve", lambda e: e.bn_stats(out=bnst[:, 0, 0:6], in_=ug[:, 256:512]), reads=["w1_0"], writes=["bnst0"])
                    T.op("dve", lambda e: e.bn_aggr(out=bnst[:, 1, 0:2], in_=bnst[:, 0, 0:6]), reads=["bnst0"], writes=["bnst1"])
                    r, rn = rstd_of(bnst[:, 1, 1:2], "bnst1", 1.0)
                    vn = w1[0][:, 512:768]
                    T.op("dve", lambda e, r=r: e.tensor_scalar(out=vn, in0=ug[:, 256:512], scalar1=bnst[:, 1, 0:1], scalar2=r, op0=ALU.subtract, op1=ALU.mult),
                         reads=["w1_0", "bnst1", rn], writes=["w1_0b"])
                    T.op("dve", lambda e: e.tensor_tensor(out=vn, in0=vn, in1=gmvt[:, 0, :], op=ALU.mult), reads=["w1_0b", "gmvt"], writes=["w1_0b"])
                    T.op("dve", lambda e: e.tensor_tensor(out=wb[3][:, 0:256], in0=vn, in1=gmvt[:, 1, :], op=ALU.add), reads=["w1_0b", "gmvt"], writes=["wb3a"])
                    def fs(e):
                        r = None
                        for h in range(4):
                            r = e.matmul(out=ps[4][:, h * 64:(h + 1) * 64], lhsT=wT[:, h, :], rhs=wb[3][:, h * 64:(h + 1) * 64], start=True, stop=True)
                        return r
                    T.op("pe", fs, reads=["wb3a", "wT"], writes=["ps4"])
                    T.op("dve", lambda e: e.tensor_tensor(out=w1[0][:, 768:1024], in0=ps[4][:, 0:256], in1=gbs[:], op=ALU.add),
                         reads=["ps4", "gbs"], writes=["w1_0c"])
                    T.op("dve", lambda e: e.tensor_tensor(out=wb[3][:, 256:512], in0=w1[0][:, 768:1024], in1=ug[:, 0:256], op=ALU.mult),
                         reads=["w1_0c", "w1_0"], writes=["wb3b"])
                    transpose_to(CT[:, 0:2, m * 128:(m + 1) * 128], "CT", wb[3][:, 256:512], "wb3b", 2)
                    T.mute = False
            NG = NT // 4
            normsA(0)
            for g in range(NG):
                if g + 1 < NG:
                    normsA(g + 1)
                mmA(g, 1)
            gather(Kc, Kg, [f"Kc{g}_{h}" for g in range(NG) for h in range(4)], "Kg")
            gather(Vc, Vg, [f"Vc{m}" for m in range(NT)], "Vg")
            for g in range(NG):
                mmA(g, 2)
            for i in range(2):
                T.dma("sp", lambda e, i=i: e.dma_start(out=Hc.rearrange("p (i t n) -> p i t n", i=2, t=NT)[:, i], in_=GL[:, i, :, 128:160]),
                      reads=["GL"], writes=[f"Hc{i}"], slot=f"hc{i}")

        def gather(src, dst, reads, name):
            T.dma("pool", lambda e: e.collective_compute("AllGather", ALU.bypass, replica_groups=[list(range(NCORES))], ins=[src], outs=[dst]),
                  reads=reads, writes=[name], slot="cc_" + name, inc=1)

        def halo_select(dst_f32, dstname, cand_t, candname, width):
            T.op("dve", lambda e: e.tensor_scalar(out=dst_f32, in0=cand_t[:, 0, :], scalar1=sel[:, 0:1], scalar2=None, op0=ALU.mult),
                 reads=candname + ["sel"], writes=[dstname])
            for r_ in range(1, 8):
                T.op("dve", lambda e, r_=r_: e.scalar_tensor_tensor(out=dst_f32, in0=cand_t[:, r_, :], scalar=sel[:, r_:r_ + 1], in1=dst_f32, op0=ALU.mult, op1=ALU.add),
                     reads=candname + ["sel", dstname], writes=[dstname])

        def phase_B(l, xsrc, xdst, upto=3):
            kg_names = [f"Kg"]
            T.op("pool", lambda e: e.memset(cand[:, 0, :], 0.0), writes=["cand00", "cand01"])
            cview = cand[:].rearrange("p r (i t n) -> p r i t n", i=2, t=NT)
            hgv = Hg.rearrange("(r p) (i t n) -> r p i t n", p=128, i=2, t=NT)
            if NT > 1:
                for i in range(2):
                    T.dma("sp", lambda e, i=i: e.dma_start(out=cview[:, 0, i, 1:NT, :], in_=hgv[7, :, i, 0:NT - 1, :]), reads=["Hg"], writes=[f"cand0{i}"], slot=f"cand0{i}")
            for r_ in range(1, 8):
                T.dma("sp", lambda e, r_=r_: e.dma_start(out=cand[:, r_, :], in_=Hg[(r_ - 1) * 128:r_ * 128, :]), reads=["Hg"], writes=[f"cand{r_}"], slot=f"cand{r_}")
            hsel = cvy[:, 0, 0:2 * NT * 32]
            halo_select(hsel, "cvy", cand, CANDN, 2 * NT * 32)
            T.op("dve", lambda e: e.tensor_copy(out=GL[:, :, :, 0:32], in_=hsel.rearrange("p (i t n) -> p i t n", i=2, t=NT)),
                 reads=["cvy"], writes=["GL"])
            GLb = ARENA[:, 0:5120].rearrange("p (i t n) -> p i t n", i=2, n=160)[:, :, 0:NT, :]
            DIAG = ARENA[:, 5120:5120 + 31 * 128].rearrange("p (k n) -> p k n", k=31)
            T.op("dve", lambda e: e.tensor_copy(out=GLb[:, 0], in_=GL[:, 0]), reads=["GL"], writes=["GLb0"] + CANDN)
            T.op("act", lambda e: e.activation(out=GLb[:, 1], in_=GL[:, 1], func=AF.Copy), reads=["GL"], writes=["GLb1"] + CANDN)
            for i in range(2):
                for k in range(31):
                    T.op("dve", lambda e, i=i, k=k: e.tensor_scalar(out=DIAG[:, k, :], in0=identf[:], scalar1=cvw[:, i, k:k + 1], scalar2=None, op0=ALU.mult),
                         reads=["identf", "cvw"], writes=["diag"] + CANDN)
                for g4 in range(NT // 4):
                    pb = 2 + (g4 % 2)
                    def fcv(e, i=i, g4=g4, pb=pb):
                        r = None
                        for k in range(31):
                            r = e.matmul(out=ps[pb][:].rearrange("p (t n) -> p t n", t=4), lhsT=DIAG[:, k, :],
                                         rhs=GLb[:, i, 4 * g4:4 * g4 + 4, 2 + k:130 + k], start=(k == 0), stop=(k == 30))
                        return r
                    T.op("pe", fcv, reads=["diag", f"GLb{i}"], writes=[f"ps{pb}"])
                    T.op("act", lambda e, i=i, g4=g4, pb=pb: e.activation(out=cvy[:, i, g4 * 512:(g4 + 1) * 512], in_=ps[pb][:], func=AF.Identity, bias=cvp[:, i, 0:1]),
                         reads=[f"ps{pb}", "cvp"], writes=[f"cvy{i}"])
            for i in range(2):
                for q4 in range(S_LOC // 512):
                    cs = slice(q4 * 512, (q4 + 1) * 512)
                    T.op("pe", lambda e, i=i, cs=cs: e.matmul(out=ps[0][:], lhsT=gavg[:], rhs=cvy[:, i, cs], start=True, stop=True),
                         reads=[f"cvy{i}", "gavg"], writes=["ps0"])
                    T.op("act", lambda e, i=i, cs=cs: e.activation(out=cvq[:], in_=cvy[:, i, cs], func=AF.Square), reads=[f"cvy{i}"], writes=["cvq"])
                    T.op("pe", lambda e: e.matmul(out=ps[1][:], lhsT=gavg[:], rhs=cvq[:], start=True, stop=True), reads=["cvq", "gavg"], writes=["ps1"])
                    T.op("act", lambda e: e.activation(out=w2[0][:], in_=ps[0][:], func=AF.Square), reads=["ps0"], writes=["w20"])
                    T.op("dve", lambda e: e.tensor_tensor(out=w2[0][:], in0=ps[1][:], in1=w2[0][:], op=ALU.subtract), reads=["ps1", "w20"], writes=["w20"])
                    T.op("act", lambda e: e.activation(out=w2[0][:], in_=w2[0][:], func=AF.Ln, bias=epsT[:, 0:1]), reads=["w20"], writes=["w20"])
                    T.op("act", lambda e: e.activation(out=w2[0][:], in_=w2[0][:], func=AF.Exp, scale=-0.5), reads=["w20"], writes=["w20"])
                    T.op("dve", lambda e, i=i, cs=cs: e.tensor_tensor(out=w2[1][:], in0=cvy[:, i, cs], in1=ps[0][:], op=ALU.subtract), reads=[f"cvy{i}", "ps0"], writes=["w21"])
                    T.op("dve", lambda e: e.tensor_tensor(out=w2[1][:], in0=w2[1][:], in1=w2[0][:], op=ALU.mult), reads=["w21", "w20"], writes=["w21"])
                    T.op("act", lambda e, i=i, cs=cs: e.activation(out=CT[:, 6 + i, cs], in_=w2[1][:], func=AF.Silu, bias=cvp[:, i, 2:3], scale=cvp[:, i, 1:2]),
                         reads=["w21", "cvp"], writes=["CT"])
            T.barrier()
            if upto == 1:
                return
            T.op("pool", lambda e: e.memset(QP[:, :, :], 0.0), writes=["QP"])
            for h in range(4):
                for mp in range(2):
                    T.op("pool", lambda e, mp=mp, h=h: e.tensor_copy(out=QP[mp * 64:(mp + 1) * 64, mp, :], in_=QT[mp * 64:(mp + 1) * 64, h, :]),
                         reads=["QT", "QP"], writes=["QP"])
                for c_ in range(8):
                    T.dma("sp", lambda e, c_=c_, h=h: e.dma_start(
                        out=KT[:, :, :].rearrange("p (m c) n -> p m c n", c=8)[:, :, c_, :],
                        in_=Kg[c_ * 512 + h * 128:c_ * 512 + (h + 1) * 128, :].rearrange("p (m n) -> p m n", n=128)),
                        reads=["Kg"], writes=[f"KT{c_}"], slot=f"kt{c_}")
                    T.dma("act", lambda e, c_=c_, h=h: e.dma_start(
                        out=VH[:, :, 0:128].rearrange("p (m c) n -> p m c n", c=8)[:, :, c_, :],
                        in_=Vg[c_ * S_LOC:(c_ + 1) * S_LOC, h * 128:(h + 1) * 128].rearrange("(m p) n -> p m n", p=128)),
                        reads=["Vg"], writes=[f"VH{c_}"], slot=f"vh{c_}")
                T.op("pool", lambda e: e.memset(VH[:, :, 128:130], 1.0), writes=["VH1"])
                T.dma("sp", lambda e, h=h: e.dma_start(out=btab[:], in_=btab_in[h]), writes=["btab"], slot="btab")
                T.op("dve", lambda e, h=h: e.tensor_scalar(out=btab[:], in0=btab[:], scalar1=cfar[:, h:h + 1], scalar2=None, op0=ALU.subtract),
                     reads=["btab", "cfar"], writes=["btab"])
                units = [(m, qd) for m in range(NT) for qd in range(2 * m + 2)]

                def score(ui, m, qd, h=h):
                    sb_ = ui % 2
                    S = psS[sb_]
                    qs = slice(m * 128, (m + 1) * 128)
                    def fsc(e):
                        r = None
                        for kb in range(4):
                            r = e.matmul(out=S[:, kb * 256:(kb + 1) * 256].rearrange("p (a n) -> p a n", a=2),
                                         lhsT=KT[:, 4 * qd + kb, :], rhs=QP[:, 0:2, qs], start=True, stop=True)
                        return r
                    T.op("pe", fsc, reads=KTN + ["QP"], writes=[f"ps{2 * sb_}", f"ps{2 * sb_ + 1}"])

                def softmax(ui, m, qd, h=h):
                    sb_ = ui % 2
                    S = psS[sb_]
                    P = wball[:, sb_, :]
                    sn = [f"ps{2 * sb_}", f"ps{2 * sb_ + 1}"]
                    pn = [f"wb{2 * sb_}", f"wb{2 * sb_ + 1}"]
                    near = (4 * qd + 3 >= 8 * m - 2)
                    if not near:
                        T.op("act", lambda e: e.activation(out=P, in_=S[:], func=AF.Exp, scale=0.125),
                             reads=sn, writes=pn)
                    else:
                        n0 = 4 * qd - (8 * m - 4)
                        stg = w2all[:, 0:2, :]
                        for mp in range(2):
                            T.op("dve", lambda e, mp=mp: e.scalar_tensor_tensor(
                                out=stg.rearrange("p x (k2 a n) -> p (x k2) a n", a=2, n=128)[:, :, mp, :],
                                in0=S[:].rearrange("p (k a n) -> p k a n", k=4, a=2)[:, :, mp, :], scalar=0.125,
                                in1=btab[:, n0:n0 + 4, :], op0=ALU.mult, op1=ALU.add),
                                reads=sn + ["btab"], writes=[f"w2n{mp}"])
                        T.op("act", lambda e: e.activation(out=P, in_=stg.rearrange("p x n -> p (x n)"), func=AF.Exp), reads=["w2n0", "w2n1"], writes=pn + ["w20", "w21"])

                def pv(ui, m, qd):
                    sb_ = ui % 2
                    P = wball[:, sb_, :]
                    ob = 4
                    nblk = 8 * m + 8
                    def fpv(e):
                        r = None
                        for kb in range(4):
                            j = 4 * qd + kb
                            for mp in range(2):
                                r = e.matmul(out=ps[ob + mp][:, 0:130], lhsT=P[:, kb * 256 + mp * 128: kb * 256 + (mp + 1) * 128], rhs=VH[:, j, :],
                                             start=(j == 0), stop=(j == nblk - 1))
                        return r
                    T.op("pe", fpv, reads=[f"wb{2 * sb_}", f"wb{2 * sb_ + 1}", "VH1"] + VHN, writes=[f"ps{ob}", f"ps{ob + 1}"])

                def fin1(m):
                    ob = 4
                    o1, o2 = ps[ob], ps[ob + 1]
                    o1n, o2n = f"ps{ob}", f"ps{ob + 1}"
                    osb1, osb2 = w1[1][:, 512:642], w1[1][:, 768:898]
                    T.op("act", lambda e: e.activation(out=osb1, in_=ps[ob][:, 0:130], func=AF.Copy), reads=[o1n], writes=["osb1"])
                    T.op("dve", lambda e: e.tensor_copy(out=osb2, in_=ps[ob + 1][:, 0:130]), reads=[o2n], writes=["osb2"])
                    o1, o2, o1n, o2n = osb1, osb2, "osb1", "osb2"
                    r1, r1n = newstat()
                    r2, r2n = newstat()
                    T.op("dve", lambda e: e.reciprocal(out=r1, in_=o1[:, 128:129]), reads=[o1n], writes=[r1n])
                    T.op("dve", lambda e: e.reciprocal(out=r2, in_=o2[:, 128:129]), reads=[o2n], writes=[r2n])
                    T.op("dve", lambda e: e.tensor_tensor(out=r2, in0=r2, in1=lamw[:, 2:3], op=ALU.mult), reads=[r2n, "lamw"], writes=[r2n])
                    at = w1[1]
                    T.op("dve", lambda e: e.tensor_scalar(out=at[:, 0:128], in0=o2[:, 0:128], scalar1=r2, scalar2=None, op0=ALU.mult),
                         reads=[o2n, r2n], writes=["w1_1"])
                    T.op("dve", lambda e: e.scalar_tensor_tensor(out=at[:, 128:256], in0=o1[:, 0:128], scalar=r1, in1=at[:, 0:128], op0=ALU.mult, op1=ALU.add),
                         reads=[o1n, r1n, "w1_1"], writes=["w1_1b"])
                    rr, rrn = rstd_from(at[:, 128:256], "w1_1b", 128)
                    hbm = hb[m % 2]
                    T.op("dve", lambda e: e.scalar_tensor_tensor(out=hbm[:, 0:128], in0=at[:, 128:256], scalar=rr, in1=gsub[:], op0=ALU.mult, op1=ALU.mult),
                         reads=["w1_1b", rrn, "gsub"], writes=[f"hb{m % 2}"])

                def fin2(m, h=h):
                    transpose_to(CT[:, 2 + h:3 + h, m * 128:(m + 1) * 128], "CTb", hb[m % 2], f"hb{m % 2}", 1, evac="dve")

                pending = []
                score(0, *units[0])
                for ui, (m, qd) in enumerate(units):
                    if ui + 1 < len(units):
                        score(ui + 1, *units[ui + 1])
                    softmax(ui, m, qd)
                    pv(ui, m, qd)
                    for item in list(pending):
                        item[0] -= 1
                        if item[0] <= 0:
                            fin2(item[1])
                            pending.remove(item)
                    if qd == 2 * m + 1:
                        fin1(m)
                        pending.append([3, m])
                for item in pending:
                    fin2(item[1])
            T.barrier()
            if upto == 2:
                return
            ld(gv[:, 0, :], gvec[l][:, 1, :], "gv0")
            ld(gv[:, 1, :], gvec[l][:, 2, :], "gv1")
            for kc in range(8):
                T.dma("sp", lambda e, kc=kc: e.dma_start(out=WOUT[:, kc, :], in_=Wg_out[l * 1024 + kc * 128:l * 1024 + (kc + 1) * 128, :]),
                      reads=["Wg_out"], writes=[f"arenaW{kc}"], slot=f"win{kc}")
            def mmO(m):
                qs = slice(m * 128, (m + 1) * 128)
                pbase = 5 if m % 2 == 0 else 0
                for nh in range(2):
                    def fo(e, nh=nh, qs=qs):
                        r = None
                        for kc in range(8):
                            r = e.matmul(out=ps[pbase + nh][:], lhsT=CT[:, kc, qs], rhs=WOUT[:, kc, nh * 512:(nh + 1) * 512], start=(kc == 0), stop=(kc == 7))
                        return r
                    T.op("pe", fo, reads=["CT", "CTb"] + ARW, writes=[f"ps{pbase + nh}"])
            mmO(0)
            for m in range(NT):
                b = m % 2
                qs = slice(m * 128, (m + 1) * 128)
                pbase = 5 if m % 2 == 0 else 0
                T.dma("sp", lambda e, m=m, b=b: e.dma_start(out=xt[b][:], in_=xsrc[m]), writes=[f"xt{b}"], slot=f"xt{b}")
                if m + 1 < NT:
                    mmO(m + 1)
                for nh in range(2):
                    T.op("dve", lambda e, nh=nh, b=b, pbase=pbase: e.tensor_copy(out=w1[b][:, nh * 512:(nh + 1) * 512], in_=ps[pbase + nh][:]),
                         reads=[f"ps{pbase + nh}"], writes=[f"w1_{b}"])
                r, rn = rstd_from(w1[b][:], f"w1_{b}", D)
                T.op("dve", lambda e, b=b, r=r: e.scalar_tensor_tensor(out=w1[b][:], in0=w1[b][:], scalar=r, in1=gv[:, 0, :], op0=ALU.mult, op1=ALU.mult),
                     reads=[f"w1_{b}", rn, "gv0"], writes=[f"w1_{b}"])
                T.op("dve", lambda e, b=b: e.tensor_tensor(out=xt[b][:], in0=xt[b][:], in1=w1[b][:], op=ALU.add), reads=[f"xt{b}", f"w1_{b}"], writes=[f"xt{b}"])
                T.dma("sp", lambda e, m=m, b=b: e.dma_start(out=xdst[m], in_=xt[b][:]), reads=[f"xt{b}"], writes=[f"xmid{m}"], slot=f"xo{b}")
                r, rn = rstd_from(xt[b][:], f"xt{b}", D)
                T.op("dve", lambda e, b=b, r=r: e.scalar_tensor_tensor(out=hb[b][:], in0=xt[b][:], scalar=r, in1=gv[:, 1, :], op0=ALU.mult, op1=ALU.mult),
                     reads=[f"xt{b}", rn, "gv1"], writes=[f"hb{b}"])
                transpose_to(H2T[:, :, qs], "H2T", hb[b], f"hb{b}", 8)
            T.op("dve", lambda e: e.tensor_copy(out=tailT[:], in_=H2T[:].rearrange("p k (t n) -> p k t n", n=128)[:, :, :, 126:128]),
                 reads=["H2T"], writes=["tailT"])
            T.dma("sp", lambda e: e.dma_start(out=Tc, in_=tailT[:].rearrange("p k t n -> p (k t n)")), reads=["tailT"], writes=["Tc"], slot="tc")

        def phase_C(l, xsrc, xdst):
            ld(gv[:, 0, :], gvec[l][:, 3, :], "gv0")
            T.op("pool", lambda e: e.memset(candT[:, 0, :], 0.0), writes=["cand00", "cand01"])
            cv_ = candT[:].rearrange("p r (k t n) -> p r k t n", k=8, t=NT)
            tgv = Tg.rearrange("(r p) (k t n) -> r p k t n", p=128, k=8, t=NT)
            if NT > 1:
                for k in range(8):
                    T.dma("sp", lambda e, k=k: e.dma_start(out=cv_[:, 0, k, 1:NT, :], in_=tgv[7, :, k, 0:NT - 1, :]), reads=["Tg"], writes=[f"cand0{k % 2}"], slot=f"cand0{k % 2}")
            for r_ in range(1, 8):
                T.dma("sp", lambda e, r_=r_: e.dma_start(out=candT[:, r_, :], in_=Tg[(r_ - 1) * 128:r_ * 128, :]), reads=["Tg"], writes=[f"cand{r_}"], slot=f"cand{r_}")
            halo_select(halTf[:], "halTf", candT, CANDN, 8 * NT * 2)
            T.op("dve", lambda e: e.tensor_copy(out=halT[:].rearrange("p k t n -> p (k t n)"), in_=halTf[:]), reads=["halTf"], writes=["halT"])
            for j in range(22):
                T.dma("sp", lambda e, j=j: e.dma_start(out=WDN[:, j, :], in_=Wg_dn[l * DFF + j * 128:l * DFF + (j + 1) * 128, :]),
                      reads=["Wg_dn"], writes=[f"wdn{j % 4}"], slot=f"wdn{j % 4}")
            NPASS = max(1, NT // 8)
            TP = NT // NPASS
            for pa in range(NPASS):
                t0 = pa * TP
                for j in range(22):
                    for part in range(2):
                        fc = part * 22 + j
                        col = part * DFF + j * 128
                        wbuf = wup_t[fc % 3]
                        wn = f"wup{fc % 3}"
                        gch = l * 44 + fc
                        T.dma("sp", lambda e, gch=gch, wbuf=wbuf: e.dma_start(out=wbuf[:].rearrange("p k n -> p (k n)"), in_=Wg[gch * 128:(gch + 1) * 128, :]),
                              reads=["Wg"], writes=[wn], slot=wn)
                        ub = upsb[part]
                        nbank = (TP + 3) // 4
                        for hb_ in range(nbank):
                            nt_ = min(4, TP - hb_ * 4)
                            cs = slice((t0 + hb_ * 4) * 128, (t0 + hb_ * 4 + nt_) * 128)
                            pbk = (fc * 2 + hb_) % 4
                            def fu(e, cs=cs, pbk=pbk, wbuf=wbuf, nt_=nt_):
                                r = None
                                for kc in range(8):
                                    r = e.matmul(out=ps[pbk][:, 0:nt_ * 128], lhsT=wbuf[:, kc, :], rhs=H2T[:, kc, cs], start=(kc == 0), stop=(kc == 7))
                                return r
                            T.op("pe", fu, reads=["H2T", wn], writes=[f"ps{pbk}"])
                            T.op("act", lambda e, pbk=pbk, ub=ub, hb_=hb_, nt_=nt_: e.activation(
                                out=ub[:, hb_ * 4:hb_ * 4 + nt_, 2:130], in_=ps[pbk][:, 0:nt_ * 128].rearrange("p (t n) -> p t n", n=128), func=AF.Copy),
                                reads=[f"ps{pbk}"], writes=[f"upsb{part}"])
                        def ft(e, wbuf=wbuf, t0=t0):
                            r = None
                            for kc in range(8):
                                r = e.matmul(out=ps[4][:, 0:TP * 2].rearrange("p (t n) -> p t n", n=2), lhsT=wbuf[:, kc, :], rhs=halT[:, kc, t0:t0 + TP, :], start=(kc == 0), stop=(kc == 7))
                            return r
                        T.op("pe", ft, reads=["halT", wn], writes=["ps4"])
                        T.op("act", lambda e, ub=ub: e.activation(out=ub[:, 0:TP, 0:2], in_=ps[4][:, 0:TP * 2].rearrange("p (t n) -> p t n", n=2), func=AF.Copy),
                             reads=["ps4"], writes=[f"upsb{part}"])
                        ac = acc[part]
                        T.op("dve", lambda e, ub=ub, ac=ac, fc=fc: e.tensor_scalar(out=ac[:, 0:TP, :], in0=ub[:, 0:TP, 2:130], scalar1=fcw[:, fc, 2:3], scalar2=fcw[:, fc, 3:4], op0=ALU.mult, op1=ALU.add),
                             reads=[f"upsb{part}", "fcw"], writes=[f"acc{part}"])
                        for k in range(2):
                            T.op("dve", lambda e, ub=ub, ac=ac, fc=fc, k=k: e.scalar_tensor_tensor(out=ac[:, 0:TP, :], in0=ub[:, 0:TP, k:128 + k], scalar=fcw[:, fc, k:k + 1], in1=ac[:, 0:TP, :], op0=ALU.mult, op1=ALU.add),
                                 reads=[f"upsb{part}", "fcw", f"acc{part}"], writes=[f"acc{part}"])
                        if part == 0:
                            T.op("act", lambda e, ac=ac: e.activation(out=gact[:, 0:TP, :], in_=ac[:, 0:TP, :], func=AF.Gelu_apprx_tanh), reads=["acc0"], writes=["gact"])
                        else:
                            T.op("pool", lambda e, ac=ac, j=j: e.tensor_tensor(out=GT[:, j, 0:TP * 128].rearrange("p (t n) -> p t n", n=128), in0=gact[:, 0:TP, :], in1=ac[:, 0:TP, :], op=ALU.mult),
                                 reads=["gact", "acc1"], writes=["GT"])
                for tt in range(TP):
                    m = t0 + tt
                    b = m % 2
                    T.dma("sp", lambda e, m=m, b=b: e.dma_start(out=xt[b][:], in_=xsrc[m]), reads=[f"xmid{m}"], writes=[f"xt{b}"], slot=f"xt{b}")
                    for nh in range(2):
                        def fd(e, nh=nh, tt=tt):
                            r = None
                            for j in range(22):
                                r = e.matmul(out=ps[5 + nh][:], lhsT=GT[:, j, tt * 128:(tt + 1) * 128], rhs=WDN[:, j, nh * 512:(nh + 1) * 512], start=(j == 0), stop=(j == 21))
                            return r
                        T.op("pe", fd, reads=["GT", "wdn0", "wdn1", "wdn2", "wdn3"], writes=[f"ps{5 + nh}"])
                        T.op("dve", lambda e, nh=nh, b=b: e.tensor_copy(out=w1[b][:, nh * 512:(nh + 1) * 512], in_=ps[5 + nh][:]), reads=[f"ps{5 + nh}"], writes=[f"w1_{b}"])
                    r, rn = rstd_from(w1[b][:], f"w1_{b}", D)
                    T.op("dve", lambda e, b=b, r=r: e.scalar_tensor_tensor(out=w1[b][:], in0=w1[b][:], scalar=r, in1=gv[:, 0, :], op0=ALU.mult, op1=ALU.mult),
                         reads=[f"w1_{b}", rn, "gv0"], writes=[f"w1_{b}"])
                    T.op("dve", lambda e, b=b: e.tensor_tensor(out=xt[b][:], in0=xt[b][:], in1=w1[b][:], op=ALU.add), reads=[f"xt{b}", f"w1_{b}"], writes=[f"xt{b}"])
                    T.dma("sp", lambda e, m=m, b=b: e.dma_start(out=xdst[m], in_=xt[b][:]), reads=[f"xt{b}"], writes=[f"xo{m}"], slot=f"xo{b}")

        wup_t = [sb(f"wup{i}", [128, 8, 128], BF16) for i in range(3)]

        def dump(which):
            T.barrier()
            if which in ("A0", "A1"):
                T.dma("sp", lambda e: e.dma_start(out=dbg["q"], in_=QT[:]), reads=["QT"], writes=["dq"], slot="dbg0")
                T.dma("sp", lambda e: e.dma_start(out=dbg["ct"], in_=CT[:]), reads=["CT"], writes=["dc"], slot="dbg1")
                for i in range(2):
                    T.dma("sp", lambda e, i=i: e.dma_start(out=dbg["gl"].rearrange("p i (t n) -> p i t n", n=128)[:, i], in_=GL[:, i, :, 32:160]), reads=["GL"], writes=[f"dg{i}"], slot=f"dbg2{i}")
                T.dma("sp", lambda e: e.dma_start(out=dbg["kg"], in_=Kg), reads=["Kg"], writes=["dk"], slot="dbg3")
                T.dma("sp", lambda e: e.dma_start(out=dbg["vg"], in_=Vg), reads=["Vg"], writes=["dv"], slot="dbg4")
            if which in ("B0", "B1"):
                T.dma("sp", lambda e: e.dma_start(out=dbg["ct"], in_=CT[:]), reads=["CT"], writes=["dc"], slot="dbg1")
                T.dma("sp", lambda e: e.dma_start(out=dbg["h2"], in_=H2T[:]), reads=["H2T"], writes=["dh"], slot="dbg2")
                for m in range(NT):
                    T.dma("sp", lambda e, m=m: e.dma_start(out=y_out[m], in_=xbuf[0][m]), reads=[f"xmid{m}"], writes=[f"y{m}"], slot=f"dy{m % 4}")
            T.barrier()

        def convert(src, dst_c, dst_g, rows, step, name):
            pieces = []
            for i, r0 in enumerate(range(0, rows, step)):
                T.dma("pool", lambda e, r0=r0: e.dma_start(out=dst_c[r0:r0 + step, :].rearrange("p (a b) -> p a b", b=512),
                                                         in_=src[r0:r0 + step, :].rearrange("p (a b) -> p a b", b=512)),
                      writes=[f"{name}c{i}"], slot=f"cv{name}{i % 4}")
                pieces.append(f"{name}c{i}")
            gather(dst_c, dst_g, pieces, name)
        convert(win_part, Wc_in, Wg_in, 256, 128, "Wg_in")

        done = False
        for l in range(DEPTH):
            xsrc = x_in if l == 0 else xbuf[1]
            layer_consts(l)
            phase_A(l, xsrc)
            if l == 0:
                for i in range(11):
                    T.dma("pool", lambda e, i=i: e.dma_start(out=Wc[i * 128:(i + 1) * 128, :].rearrange("p (a b) -> p a b", b=512),
                                                           in_=wup_part[i].rearrange("p (a b) -> p a b", b=512)),
                          writes=[f"Wc{i}"], slot=f"wc{i % 4}")
                gather(Wc, Wg, [f"Wc{i}" for i in range(11)], "Wg")
                convert(wout_part, Wc_out, Wg_out, 256, 128, "Wg_out")
                convert(wdn_part, Wc_dn, Wg_dn, 704, 64, "Wg_dn")
            T.barrier()
            gather(Hc, Hg, ["Hc0", "Hc1"], "Hg")
            T.barrier()
            if stop_after == f"A{l}":
                dump(stop_after); done = True; break
            if stop_after in (f"P{l}", f"Q{l}"):
                phase_B(l, xsrc, xbuf[0], upto=1 if stop_after[0] == "P" else 2)
                T.barrier()
                T.dma("sp", lambda e: e.dma_start(out=dbg["ct"], in_=CT[:]), reads=["CT"], writes=["dc"], slot="dbg1")
                T.barrier()
                done = True
                break
            phase_B(l, xsrc, xbuf[0])
            T.barrier()
            gather(Tc, Tg, ["Tc"], "Tg")
            T.barrier()
            if stop_after == f"B{l}":
                dump(stop_after); done = True; break
            phase_C(l, xbuf[0], y_out if l == DEPTH - 1 else xbuf[1])
            T.barrier()
            if stop_after == f"C{l}":
                for m in range(NT):
                    T.dma("sp", lambda e, m=m: e.dma_start(out=y_out[m], in_=xbuf[1][m]), reads=[f"xo{m}"], writes=[f"y{m}"], slot=f"dy{m % 4}")
                T.barrier()
                done = True
                break
        T.barrier()
        T.run(block)
    return nc


def host_inputs(inputs, NT=16):
    f = lambda a: np.ascontiguousarray(np.asarray(a, dtype=np.float32))
    x = f(inputs["x"])[0]
    S = x.shape[0]
    nblk = S // 128
    assert nblk == NT * 8
    xb = x.reshape(NT, 8, 128, D)
    bc = lambda a, shape: np.ascontiguousarray(np.broadcast_to(a, shape))
    gvec = np.stack([f(inputs[k]) for k in ("pre_mix_g", "post_mix_g", "pre_ffn_g", "post_ffn_g")], 1)
    gvec = bc(gvec[:, None], (DEPTH, 128, 4, D))
    gmv = np.stack([f(inputs["gm_ln_g"]), f(inputs["gm_ln_b"])], 1)
    gmv = bc(gmv[:, None], (DEPTH, 128, 2, 256))
    gm_wT = np.ascontiguousarray(f(inputs["gm_w_s"]).transpose(0, 3, 1, 2))
    gm_bs = np.ascontiguousarray(np.repeat(f(inputs["gm_b_s"]).transpose(0, 2, 1), 64, axis=2))
    tri = (np.arange(128)[:, None] <= np.arange(128)[None, :]).astype(np.float32)
    lam_in = np.stack([f(inputs[k]) for k in ("da_lq1", "da_lk1", "da_lq2", "da_lk2")], 1)
    lam_in = bc(lam_in[:, None], (DEPTH, 128, 4, 64))
    subln = bc(f(inputs["da_subln_g"])[:, None], (DEPTH, 128, 128))
    rb = f(inputs["rel_bias"])
    cfar = bc(rb[31][None], (128, 4))
    cvw = np.ascontiguousarray(f(inputs["cv_dw_w"]).reshape(DEPTH, 31, 2, 128).transpose(0, 3, 2, 1))
    cvp = np.stack([f(inputs[k]).reshape(DEPTH, 2, 128) for k in ("cv_dw_b", "cv_ln_g", "cv_ln_b")], -1)
    cvp = np.ascontiguousarray(cvp.transpose(0, 2, 1, 3))
    gavg = np.zeros((128, 128), np.float32)
    gavg[:64, :64] = 1.0 / 64
    gavg[64:, 64:] = 1.0 / 64
    fw = f(inputs["ffn_conv_w"]).reshape(DEPTH, 3, NFC, 128)
    fb = f(inputs["ffn_conv_b"]).reshape(DEPTH, 1, NFC, 128)
    fcw = np.ascontiguousarray(np.concatenate([fw, fb], 1).transpose(0, 3, 2, 1))
    ident = np.eye(128, dtype=np.float32)
    wup = f(inputs["ffn_w_up"])
    win_f = f(inputs["w_in"]).reshape(DEPTH * D, INW)
    wout_f = f(inputs["w_out"]).reshape(DEPTH * D, D)
    wdn_f = f(inputs["ffn_w_down"]).reshape(DEPTH * DFF, D)
    common = dict(
                  gvec=gvec, gmv=gmv, gm_wT=gm_wT, gm_bs=gm_bs, tri=tri, lam_in=lam_in, subln=subln, cfar=cfar,
                  cvw=cvw, cvp=cvp, gavg=gavg, fcw=fcw, ident=ident)
    k = np.arange(128)[:, None, None]
    n = np.arange(12)[None, :, None]
    q = np.arange(128)[None, None, :]
    maps = []
    for c in range(NCORES):
        rel = (c + 4 - n) * 128 + q - k
        idx = t5_bucket(rel)
        tab = rb[idx]
        tab = np.where((rel >= 0)[..., None], tab, np.float32(NEG)).astype(np.float32)
        btab = np.ascontiguousarray(tab.transpose(3, 0, 1, 2))
        sel = np.zeros((128, 8), np.float32)
        sel[:, c] = 1.0
        parts = []
        for i in range(11):
            l_, fc = divmod(c * 11 + i, NFC)
            part, j = divmod(fc, 22)
            col = part * DFF + j * 128
            parts.append(wup[l_][:, col:col + 128].reshape(8, 128, 128).transpose(1, 0, 2).reshape(128, 1024))
        d = dict(common)
        d.update(x=np.ascontiguousarray(xb[:, c]), btab=btab, sel=sel, wup_part=np.ascontiguousarray(np.stack(parts)),
                 win_part=np.ascontiguousarray(win_f[c * 256:(c + 1) * 256]), wout_part=np.ascontiguousarray(wout_f[c * 256:(c + 1) * 256]),
                 wdn_part=np.ascontiguousarray(wdn_f[c * 704:(c + 1) * 704]))
        maps.append(d)
    return maps


_CACHE = {}


def kernel(**inputs):
    NT = 16
    if "nc" not in _CACHE:
        _CACHE["nc"] = build(NT)
    nc = _CACHE["nc"]
    maps = host_inputs(inputs, NT)
    res = run_bass_kernel_spmd(nc, maps, core_ids=list(range(NCORES)))
    out = np.zeros((NT, 8, 128, D), np.float32)
    for c in range(NCORES):
        out[:, c] = np.asarray(res.results[c]["y"])
    return out.reshape(1, NT * 8 * 128, D)
```
